# Optimizing a Trainium2 kernel written in Bass

```python
import jax, jax.numpy as jnp
from jax import lax
import numpy as np

D_MODEL = 1024
BATCH = 4
SEQ = 4096
DEPTH = 2

W_BRANCH = D_MODEL // 2
N_BRANCH = 4
CONV_WIDTH = 4
LRU_BLOCKS = 8
LRU_C = 8.0
MLSTM_HEADS = 4
MLSTM_QKV_BLOCK = 4
HGRN_HEADS = 4
GLA_HEADS = 4
GLA_DK = W_BRANCH // 2
GLA_DV = W_BRANCH
GLA_GATE_RANK = 16
GLA_GATE_TAU = 16.0
CHUNK = 64
NORM_EPS = 1e-6

SPLIT_SIZES = (
    W_BRANCH, W_BRANCH,
    W_BRANCH, W_BRANCH, W_BRANCH,
    W_BRANCH, W_BRANCH, W_BRANCH, W_BRANCH,
    GLA_DK, GLA_DK, GLA_DV, GLA_GATE_RANK, GLA_DV,
    N_BRANCH * D_MODEL,
)
D_IN = sum(SPLIT_SIZES)

kernel_name = "hybrid_lru_mlstm_hgrn2_gla_gated_merge"


def rmsnorm(x, g):
    xf = x.astype(jnp.float32)
    return xf * lax.rsqrt(jnp.mean(xf * xf, axis=-1, keepdims=True) + NORM_EPS) * g


def head_rmsnorm(x, w):
    B, S, H, d = x.shape
    y = x * lax.rsqrt(jnp.mean(x * x, axis=-1, keepdims=True) + NORM_EPS) * w
    return y.reshape(B, S, H * d)


def head_layernorm(x, w, n_heads):
    B, S, W = x.shape
    xh = x.reshape(B, S, n_heads, W // n_heads)
    mu = jnp.mean(xh, axis=-1, keepdims=True)
    xc = xh - mu
    var = jnp.mean(xc * xc, axis=-1, keepdims=True)
    return (xc * lax.rsqrt(var + NORM_EPS)).reshape(B, S, W) * w


def causal_conv(x, w, b):
    K = w.shape[0]
    S = x.shape[1]
    xp = jnp.pad(x, ((0, 0), (K - 1, 0), (0, 0)))
    y = b
    for j in range(K):
        y = y + xp[:, j:j + S] * w[j]
    return y


def headwise(x, w):
    B, S, W = x.shape
    nb, blk, _ = w.shape
    return jnp.einsum('bsni,nij->bsnj', x.reshape(B, S, nb, blk), w).reshape(B, S, W)


def split_heads(x, n_heads):
    B, S, W = x.shape
    return x.reshape(B, S, n_heads, W // n_heads)


def _to_chunks(t):
    B, S, H = t.shape[:3]
    t = t.reshape((B, S // CHUNK, CHUNK, H) + t.shape[3:])
    return jnp.moveaxis(t, (1, 3), (0, 2))


def _from_chunks(t):
    nc, B, H, L, d = t.shape
    return jnp.moveaxis(t, (0, 2), (1, 3)).reshape(B, nc * L, H, d)


def rg_lru(x, w_a, b_a, w_x, b_x, lam):
    B, S, W = x.shape
    xb = x.reshape(B, S, LRU_BLOCKS, W // LRU_BLOCKS)
    r = jax.nn.sigmoid(jnp.einsum('bsni,nij->bsnj', xb, w_a).reshape(B, S, W) + b_a)
    i = jax.nn.sigmoid(jnp.einsum('bsni,nij->bsnj', xb, w_x).reshape(B, S, W) + b_x)
    log_a = -LRU_C * r * jax.nn.softplus(-lam)
    a = jnp.exp(log_a)
    u = jnp.sqrt(-jnp.expm1(2.0 * log_a)) * (i * x)

    def combine(e1, e2):
        a1, b1 = e1
        a2, b2 = e2
        return a1 * a2, a2 * b1 + b2

    _, h = lax.associative_scan(combine, (a, u), axis=1)
    return h


def mlstm_chunkwise(q, k, v, log_i, log_f):
    B, S, H, d = q.shape
    q = q * d ** -0.5
    mask = jnp.tril(jnp.ones((CHUNK, CHUNK), dtype=bool))

    def step(carry, xs):
        C, n, m = carry
        qc, kc, vc, ic, fc = xs
        b = jnp.cumsum(fc, axis=-1)
        Dm = jnp.where(mask, b[..., :, None] - b[..., None, :] + ic[..., None, :], -jnp.inf)
        inter = b + m[..., None]
        m_row = jnp.maximum(inter, jnp.max(Dm, axis=-1))
        w_inter = jnp.exp(inter - m_row)
        s = jnp.einsum('bhtd,bhsd->bhts', qc, kc) * jnp.exp(Dm - m_row[..., None])
        num = jnp.einsum('bhts,bhse->bhte', s, vc) + w_inter[..., None] * jnp.einsum('bhtd,bhde->bhte', qc, C)
        den = jnp.sum(s, axis=-1) + w_inter * jnp.einsum('bhtd,bhd->bht', qc, n)
        h = num / jnp.maximum(jnp.abs(den), jnp.exp(-m_row))[..., None]
        b_last = b[..., -1]
        w_log = b_last[..., None] - b + ic
        m_new = jnp.maximum(b_last + m, jnp.max(w_log, axis=-1))
        decay = jnp.exp(b_last + m - m_new)
        w_s = jnp.exp(w_log - m_new[..., None])
        C = decay[..., None, None] * C + jnp.einsum('bhs,bhsd,bhse->bhde', w_s, kc, vc)
        n = decay[..., None] * n + jnp.einsum('bhs,bhsd->bhd', w_s, kc)
        return (C, n, m_new), h

    init = (jnp.zeros((B, H, d, d), jnp.float32), jnp.zeros((B, H, d), jnp.float32),
            jnp.zeros((B, H), jnp.float32))
    xs = tuple(_to_chunks(t) for t in (q, k, v, log_i, log_f))
    _, h = lax.scan(step, init, xs)
    return _from_chunks(h)


def gla_chunkwise(q, k, v, log_g):
    B, S, H, dk = q.shape
    dv = v.shape[-1]
    q = q * dk ** -0.5
    mask = jnp.tril(jnp.ones((CHUNK, CHUNK), dtype=bool))

    def step(state, xs):
        qc, kc, vc, gc = xs
        G = jnp.cumsum(gc, axis=2)
        diff = jnp.where(mask[:, :, None], G[:, :, :, None, :] - G[:, :, None, :, :], -jnp.inf)
        A = jnp.einsum('bhtd,bhsd,bhtsd->bhts', qc, kc, jnp.exp(diff))
        o = jnp.einsum('bhts,bhse->bhte', A, vc) + jnp.einsum('bhtd,bhde->bhte', qc * jnp.exp(G), state)
        G_last = G[:, :, -1:, :]
        state = (jnp.exp(G_last[:, :, 0, :, None]) * state
                 + jnp.einsum('bhsd,bhse->bhde', kc * jnp.exp(G_last - G), vc))
        return state, o

    init = jnp.zeros((B, H, dk, dv), jnp.float32)
    xs = tuple(_to_chunks(t) for t in (q, k, v, log_g))
    _, o = lax.scan(step, init, xs)
    return _from_chunks(o)


def hybrid_mixer(h, w_in, lru_conv_w, lru_conv_b, lru_wa, lru_ba, lru_wx, lru_bx, lru_lambda,
                 m_conv_w, m_conv_b, m_wq, m_wk, m_wv, m_wi, m_bi, m_wf, m_bf, m_norm_w, m_skip,
                 h_lb, h_norm_w, g_w_lr2, g_b_lr2, g_norm_w, w_branch, w_out):
    B, S, _ = h.shape
    u = jnp.einsum('bsd,de->bse', h, w_in).astype(jnp.float32)
    idx = np.cumsum(np.array(SPLIT_SIZES))[:-1].tolist()
    (lru_x, lru_z, m_x, m_o, m_z, h_q, h_f, h_i, h_z,
     g_q, g_k, g_v, g_lr, g_z, merge) = jnp.split(u, idx, axis=-1)

    xa = causal_conv(lru_x, lru_conv_w, lru_conv_b)
    y_a = rg_lru(xa, lru_wa, lru_ba, lru_wx, lru_bx, lru_lambda) * jax.nn.silu(lru_z)

    xm = jax.nn.silu(causal_conv(m_x, m_conv_w, m_conv_b))
    q = headwise(xm, m_wq)
    k = headwise(xm, m_wk)
    v = headwise(m_x, m_wv)
    qkv = jnp.concatenate([q, k, v], axis=-1)
    log_i = qkv @ m_wi + m_bi
    log_f = jax.nn.log_sigmoid(qkv @ m_wf + m_bf)
    hm = mlstm_chunkwise(split_heads(q, MLSTM_HEADS), split_heads(k, MLSTM_HEADS),
                         split_heads(v, MLSTM_HEADS), log_i, log_f).reshape(B, S, W_BRANCH)
    hm = jax.nn.sigmoid(m_o) * hm
    y_b = (head_layernorm(hm, m_norm_w, MLSTM_HEADS) + m_skip * xm) * jax.nn.silu(m_z)

    hq = jax.nn.silu(h_q)
    log_fh = jnp.logaddexp(jnp.log(h_lb), jnp.log1p(-h_lb) + jax.nn.log_sigmoid(h_f))
    hk = (1.0 - h_lb) * jax.nn.sigmoid(-h_f)
    oh = gla_chunkwise(split_heads(hq, HGRN_HEADS), split_heads(hk, HGRN_HEADS),
                       split_heads(h_i, HGRN_HEADS), split_heads(log_fh, HGRN_HEADS))
    y_c = head_rmsnorm(oh, h_norm_w) * jax.nn.silu(h_z)

    log_gk = jax.nn.log_sigmoid(g_lr @ g_w_lr2 + g_b_lr2) / GLA_GATE_TAU
    og = gla_chunkwise(split_heads(g_q, GLA_HEADS), split_heads(g_k, GLA_HEADS),
                       split_heads(g_v, GLA_HEADS), split_heads(log_gk, GLA_HEADS))
    y_d = head_rmsnorm(og, g_norm_w) * jax.nn.silu(g_z)

    branches = jnp.stack([y_a, y_b, y_c, y_d], axis=2)
    proj = jnp.einsum('bsnw,nwd->bsnd', branches, w_branch)
    gates = jax.nn.sigmoid(merge.reshape(B, S, N_BRANCH, D_MODEL))
    merged = jnp.sum(gates * proj, axis=2)
    return merged @ w_out


def setup_inputs(seed: int = 0) -> dict:
    key = jax.random.key(seed)
    ks = jax.random.split(key, 32)

    def nrm(k, shape, scale):
        return jax.random.normal(k, shape, jnp.float32) * scale

    W = W_BRANCH
    nb_m = W // MLSTM_QKV_BLOCK
    u = jax.random.uniform(ks[7], (DEPTH, W), jnp.float32, minval=0.9, maxval=0.999)
    base = u ** (1.0 / LRU_C)
    lam = jnp.log(base / (1.0 - base))
    return {
        "x": nrm(ks[0], (BATCH, SEQ, D_MODEL), 1.0),
        "norm_g": 1.0 + nrm(ks[1], (DEPTH, D_MODEL), 0.02),
        "w_in": nrm(ks[2], (DEPTH, D_MODEL, D_IN), D_MODEL ** -0.5),
        "lru_conv_w": nrm(ks[3], (DEPTH, CONV_WIDTH, W), CONV_WIDTH ** -0.5),
        "lru_conv_b": nrm(ks[4], (DEPTH, W), 0.01),
        "lru_wa": nrm(ks[5], (DEPTH, LRU_BLOCKS, W // LRU_BLOCKS, W // LRU_BLOCKS), (W // LRU_BLOCKS) ** -0.5),
        "lru_ba": nrm(ks[6], (DEPTH, W), 0.01),
        "lru_wx": nrm(ks[8], (DEPTH, LRU_BLOCKS, W // LRU_BLOCKS, W // LRU_BLOCKS), (W // LRU_BLOCKS) ** -0.5),
        "lru_bx": nrm(ks[9], (DEPTH, W), 0.01),
        "lru_lambda": lam,
        "m_conv_w": nrm(ks[10], (DEPTH, CONV_WIDTH, W), CONV_WIDTH ** -0.5),
        "m_conv_b": nrm(ks[11], (DEPTH, W), 0.01),
        "m_wq": nrm(ks[12], (DEPTH, nb_m, MLSTM_QKV_BLOCK, MLSTM_QKV_BLOCK), MLSTM_QKV_BLOCK ** -0.5),
        "m_wk": nrm(ks[13], (DEPTH, nb_m, MLSTM_QKV_BLOCK, MLSTM_QKV_BLOCK), MLSTM_QKV_BLOCK ** -0.5),
        "m_wv": nrm(ks[14], (DEPTH, nb_m, MLSTM_QKV_BLOCK, MLSTM_QKV_BLOCK), MLSTM_QKV_BLOCK ** -0.5),
        "m_wi": nrm(ks[15], (DEPTH, 3 * W, MLSTM_HEADS), 0.01),
        "m_bi": nrm(ks[16], (DEPTH, MLSTM_HEADS), 0.1),
        "m_wf": nrm(ks[17], (DEPTH, 3 * W, MLSTM_HEADS), 0.01),
        "m_bf": jnp.broadcast_to(jnp.linspace(3.0, 6.0, MLSTM_HEADS, dtype=jnp.float32), (DEPTH, MLSTM_HEADS)) + nrm(ks[18], (DEPTH, MLSTM_HEADS), 0.01),
        "m_norm_w": 1.0 + nrm(ks[19], (DEPTH, W), 0.02),
        "m_skip": 1.0 + nrm(ks[20], (DEPTH, W), 0.02),
        "h_lb_logits": nrm(ks[21], (DEPTH, W), 0.5),
        "h_norm_w": 1.0 + nrm(ks[22], (DEPTH, W // HGRN_HEADS), 0.02),
        "g_w_lr2": nrm(ks[23], (DEPTH, GLA_GATE_RANK, GLA_DK), GLA_GATE_RANK ** -0.5),
        "g_b_lr2": nrm(ks[24], (DEPTH, GLA_DK), 0.01),
        "g_norm_w": 1.0 + nrm(ks[25], (DEPTH, GLA_DV // GLA_HEADS), 0.02),
        "w_branch": nrm(ks[26], (DEPTH, N_BRANCH, W, D_MODEL), W ** -0.5),
        "w_out": nrm(ks[27], (DEPTH, D_MODEL, D_MODEL), D_MODEL ** -0.5),
        "final_g": 1.0 + nrm(ks[28], (D_MODEL,), 0.02),
    }


def reference(x, norm_g, w_in, lru_conv_w, lru_conv_b, lru_wa, lru_ba, lru_wx, lru_bx, lru_lambda,
              m_conv_w, m_conv_b, m_wq, m_wk, m_wv, m_wi, m_bi, m_wf, m_bf, m_norm_w, m_skip,
              h_lb_logits, h_norm_w, g_w_lr2, g_b_lr2, g_norm_w, w_branch, w_out, final_g):
    lb_all = jnp.cumsum(jax.nn.softmax(h_lb_logits.astype(jnp.float32), axis=0), axis=0)
    lb_all = lb_all - lb_all[0]
    for l in range(DEPTH):
        h = rmsnorm(x, norm_g[l])
        x = x + hybrid_mixer(h, w_in[l], lru_conv_w[l], lru_conv_b[l], lru_wa[l], lru_ba[l],
                             lru_wx[l], lru_bx[l], lru_lambda[l], m_conv_w[l], m_conv_b[l],
                             m_wq[l], m_wk[l], m_wv[l], m_wi[l], m_bi[l], m_wf[l], m_bf[l],
                             m_norm_w[l], m_skip[l], lb_all[l], h_norm_w[l], g_w_lr2[l],
                             g_b_lr2[l], g_norm_w[l], w_branch[l], w_out[l])
    return rmsnorm(x, final_g)
```

```python
import numpy as np
import concourse.bass as bass
import concourse.mybir as mybir
from concourse.bass_utils import run_bass_kernel_spmd

F32 = mybir.dt.float32
BF16 = mybir.dt.bfloat16
AF = mybir.ActivationFunctionType
ALU = mybir.AluOpType

D = 1024
W = 512
D_IN = 10256
DEPTH = 2
T = 512
L = 64
NCH = T // L
EPS = 1e-6
C_LRUX, C_LRUZ = 0, 512
C_MX, C_MO, C_MZ = 1024, 1536, 2048
C_HQ, C_HF, C_HI, C_HZ = 2560, 3072, 3584, 4096
C_GQ, C_GK, C_GV, C_GLR, C_GZ = 4608, 4864, 5120, 5632, 5648
C_MERGE = 6160

PV_PER_LAYER = 96
def pv(l, name, c=0):
    base = l * PV_PER_LAYER
    table = {
        "lru_cw": 0,
        "lru_cb": 16,
        "lru_ba": 20,
        "lru_bx": 24,
        "lru_lam": 28,
        "m_cw": 32,
        "m_cb": 48,
        "m_nw": 52,
        "m_skip": 56,
        "h_lbl": 60,
        "h_nw": 64,
        "g_b2": 65,
        "g_nw": 67,
        "norm_g": 68,
        "final_g": 76,
        "flag_b": 84,
        "keep": 85,
        "h_lbl1": 86,
    }
    return base + table[name] + c
SAME_ENGINE_WINDOW = 3
PL = 1
PV_COLS = PL * PV_PER_LAYER


class Buf:
    __slots__ = ("w", "r", "name", "excl")
    def __init__(self, name="", excl=False):
        self.w = None
        self.r = []
        self.name = name
        self.excl = excl


class Trk:
    def __init__(self, nc, engs, sems):
        self.nc = nc
        self.E = engs
        self.S = sems
        self.tick = {k: 0 for k in engs}
        self.waited = {k: {} for k in engs}
        self.dmaval = {}
        self.nwait = 0
        self.ninst = 0

    def _deps(self, en, reads, writes):
        need = {}
        def add(dep):
            if dep is None:
                return
            kind, key, val = dep
            k = (kind, key)
            if need.get(k, 0) < val:
                need[k] = val
        for b in reads:
            add(b.w)
            if b.excl:
                for r in b.r:
                    add(r)
        for b in writes:
            add(b.w)
            for r in b.r:
                add(r)
        out = []
        for (kind, key), val in need.items():
            if kind == "e" and key == en:
                if en == "pe":
                    continue
                if en != "pool" and val <= self.tick[en] - SAME_ENGINE_WINDOW:
                    continue
            if self.waited[en].get((kind, key), 0) >= val:
                continue
            self.waited[en][(kind, key)] = val
            out.append((self.S[key], val))
        return out

    def op(self, en, fn, reads=(), writes=()):
        eng = self.E[en]
        waits = self._deps(en, reads, writes)
        for (s, v) in waits[1:]:
            eng.wait_ge(s, v)
            self.nwait += 1
        ins = fn(eng)
        if waits:
            ins._wait_ge(waits[0][0], waits[0][1])
        self.tick[en] += 1
        ins.then_inc(self.S[en], 1)
        me = ("e", en, self.tick[en])
        for b in reads:
            if b.excl:
                b.w = me
                b.r = []
            else:
                b.r.append(me)
        for b in writes:
            b.w = me
            b.r = []
        self.ninst += 1
        return ins

    def mm(self, fns, reads=(), writes=()):
        en = "pe"
        eng = self.E[en]
        waits = self._deps(en, reads, writes)
        for (s, v) in waits[1:]:
            eng.wait_ge(s, v)
            self.nwait += 1
        ins = None
        for i, fn in enumerate(fns):
            ins = fn(eng)
            if i == 0 and waits:
                ins._wait_ge(waits[0][0], waits[0][1])
        self.tick[en] += 1
        ins.then_inc(self.S[en], 1)
        me = ("e", en, self.tick[en])
        for b in reads:
            if b.excl:
                b.w = me
                b.r = []
            else:
                b.r.append(me)
        for b in writes:
            b.w = me
            b.r = []
        self.ninst += len(fns)

    def dma(self, en, semkey, out, in_, reads=(), writes=()):
        eng = self.E[en]
        waits = self._deps(en, reads, writes)
        for (s, v) in waits:
            eng.wait_ge(s, v)
            self.nwait += 1
        ins = eng.dma_start(out=out, in_=in_)
        self.dmaval[semkey] = self.dmaval.get(semkey, 0) + 16
        ins.then_inc(self.S[semkey], 16)
        me = ("d", semkey, self.dmaval[semkey])
        for b in reads:
            if b.excl:
                b.w = me
                b.r = []
            else:
                b.r.append(me)
        for b in writes:
            b.w = me
            b.r = []
        self.ninst += 1

    def coll(self, en, semkey, fn, reads=(), writes=()):
        eng = self.E[en]
        waits = self._deps(en, reads, writes)
        for (s, v) in waits:
            eng.wait_ge(s, v)
            self.nwait += 1
        ins = fn(eng)
        self.dmaval[semkey] = self.dmaval.get(semkey, 0) + 1
        ins.then_inc(self.S[semkey], 1)
        me = ("d", semkey, self.dmaval[semkey])
        for b in reads:
            if b.excl:
                b.w = me
                b.r = []
            else:
                b.r.append(me)
        for b in writes:
            b.w = me
            b.r = []
        self.ninst += 1

    def final_wait(self, en, semkey):
        self.E[en].wait_ge(self.S[semkey], self.dmaval[semkey])


def build_program(S, n_layers=PL, n_pairs=4):
    assert S % T == 0
    NT = S // T
    nc = bass.Bass("TRN2", target_bir_lowering=False)

    xT_d = nc.dram_tensor("xT", [D, S], F32, kind="ExternalInput").ap()
    w_in_d = nc.dram_tensor("w_in", [PL, D, D_IN], F32, kind="ExternalInput").ap()
    w_br_d = nc.dram_tensor("w_branch", [PL, 4, W, D], F32, kind="ExternalInput").ap()
    w_out_d = nc.dram_tensor("w_out", [PL, D, D], F32, kind="ExternalInput").ap()
    pvec_d = nc.dram_tensor("pvec", [128, PV_COLS], F32, kind="ExternalInput").ap()
    bd_d = nc.dram_tensor("bd", [128, PL * 20, 128], F32, kind="ExternalInput").ap()
    wif_d = nc.dram_tensor("wif", [128, PL * 2 * 12, 4], F32, kind="ExternalInput").ap()
    gb_d = nc.dram_tensor("gbias", [4, PL * 2], F32, kind="ExternalInput").ap()
    lr2_d = nc.dram_tensor("lr2", [16, PL, 256], F32, kind="ExternalInput").ap()
    cst_d = nc.dram_tensor("cst", [128, 1664], F32, kind="ExternalInput").ap()
    outT_d = nc.dram_tensor("outT", [D, S], F32, kind="ExternalOutput").ap()
    send_d = nc.dram_tensor("send", [D, T], F32, kind="Internal").ap()
    recv_d = nc.dram_tensor("recv", [2 * D, T], F32, kind="Internal").ap()
    B_send = Buf("send"); B_recv = Buf("recv")

    from contextlib import ExitStack
    es = ExitStack()
    sb = lambda name, shape, dt: es.enter_context(nc.sbuf_tensor(name, shape, dt))

    xT = sb("xT_s", [128, 8, T], F32); B_x = Buf("xT")
    hT = sb("hT_s", [128, 8, T + L], BF16); B_h = Buf("hT")
    yT = sb("yT_s", [128, 16, T], BF16); B_y = [Buf("y%d" % i) for i in range(16)]
    acc = sb("acc_s", [128, 8, T], F32); B_acc = [Buf("acc%d" % i) for i in range(8)]
    mg = sb("mg_s", [128, 8, T + L], BF16); B_mgc = [Buf("mg%d" % i) for i in range(8)]
    NSLOT = 3
    ring = [sb("ring%d" % i, [128, 8, 1024], BF16) for i in range(NSLOT)]
    B_ring = [Buf("ring%d" % i) for i in range(NSLOT)]
    pvec = sb("pvec_s", [128, PV_COLS], F32); B_pv = Buf("pvec")
    dcst = sb("dcst_s", [128, 40], F32)
    gbn = sb("gbn_s", [4, 1], F32)
    bd = sb("bd_s", [128, PL * 20, 128], BF16)
    wif = sb("wif_s", [128, PL * 2 * 12, 4], BF16)
    gb = sb("gb_s", [4, PL * 2], F32)
    lr2 = sb("lr2_s", [16, PL, 256], BF16)
    cst = sb("cst_s", [128, 896], F32)
    identb = sb("identb_s", [128, 128], BF16)
    B_const = Buf("const")
    ident_f = cst[:, 0:128]
    ones_f = cst[:, 128:256]
    maskT = cst[0:64, 256:320]
    rmask = cst[:, 320:832]
    rowmask = cst[:, 832:834]
    sel = sb("sel_s", [4, 4, 128], F32)
    maskrep = sb("maskrep_s", [64, 4, L], F32)
    onesT = sb("onesT_s", [128, T], F32)
    ones_T = onesT[:, :]
    ones4 = onesT[0:4, :]

    NF = 16
    EG4 = sb("EG4_s", [128, 4, T], F32)
    FB4 = sb("FB4_s", [128, 4, T], F32)
    _fmap = {7: 0, 8: 1, 9: 2, 14: 3}
    Ft = [EG4[:, i - 3, :] if 3 <= i <= 6 else (FB4[:, _fmap[i], :] if i in _fmap else sb("F%d" % i, [128, T], F32)) for i in range(NF)]
    brh = [FB4[:, 0:2, :].bitcast(BF16).rearrange("p a (b c) -> p (a b) c", b=2),
           FB4[:, 2:4, :].bitcast(BF16).rearrange("p a (b c) -> p (a b) c", b=2)]
    BF = [Buf("F%d" % i) for i in range(NF)]
    B_brh = [[Buf("brh0"), BF[7], BF[8]], [Buf("brh1"), BF[9], BF[14]]]
    NB = 8
    BtP = [sb("B%d" % i, [128, T + L], BF16) for i in range(NB)]
    Bt = [t_[:, 0:T] for t_ in BtP]
    BB = [Buf("B%d" % i) for i in range(NB)]
    xpad = sb("xpad_s", [128, T + 3], F32); B_xpad = Buf("xpad")
    xpad2 = sb("xpad2_s", [128, T + 3], F32); B_xpad2 = Buf("xpad2")
    GZ = sb("GZ_s", [128, 4, T], F32); B_gz = [Buf("gz%d" % i) for i in range(4)]
    xm_f = acc[:, 0:4, :]; B_xm = B_acc[0:4]
    xm_b = mg[:, 0:4, 0:T]; B_xmb = B_mgc[0:4]
    mx_b = mg[:, 4:8, 0:T]; B_mxb = B_mgc[4:8]
    mx_bP = mg[:, 4:8, :]
    qkv_b = yT[:, 4:16, :]; B_qkv = B_y[4:16]
    vflat = sb("vflat_s", [128, 4096], BF16); B_vflat = Buf("vflat")
    vtok2 = vflat[:, :].rearrange("p (h j e) -> p h j e", h=2, j=NCH, e=256)
    vtok4 = vflat[:, :].rearrange("p (h j e) -> p h j e", h=4, j=NCH, e=128)
    B_vtok = [B_vflat] * 4
    g4 = [Ft[12 + i][0:4, :] for i in range(4)]; B_g4 = [BF[12 + i] for i in range(4)]
    glr = sb("glr_s", [16, T], BF16); B_glr = Buf("glr")
    ATm = sb("ATm_s", [128, 4, 2, L], BF16); B_ATm = [[Buf("ATm%d_%d" % (i, p)) for p in range(2)] for i in range(4)]
    ktok = sb("ktok_s", [128, 4, 2, 128], BF16); B_ktok = [[Buf("ktok%d_%d" % (i, p)) for p in range(2)] for i in range(4)]
    Sbf = sb("Sbf_s", [128, 4, 256], BF16); B_Sbf = [Buf("Sbf%d" % i) for i in range(4)]
    Stmp = sb("Stmp_s", [128, 4, 128], F32); B_Stmp = [Buf("Stmp%d" % i) for i in range(4)]
    st_conv = sb("st_conv", [128, PL * 2 * 4, 3], F32); B_stconv = Buf("stconv")
    st_lru = sb("st_lru", [128, PL * 4], F32); B_stlru = Buf("stlru")
    st_C = sb("st_C", [128, PL * 4, 256], F32); B_stC = [[Buf("stC%d_%d" % (l, h)) for h in range(4)] for l in range(PL)]
    st_H = sb("st_H", [128, PL * 4, 128], F32); B_stH = [[Buf("stH%d_%d" % (l, h)) for h in range(4)] for l in range(PL)]
    st_G = sb("st_G", [128, PL * 4, 128], F32); B_stG = [[Buf("stG%d_%d" % (l, h)) for h in range(4)] for l in range(PL)]
    osb = acc

    ps = [es.enter_context(nc.psum_tensor("ps%d" % i, [128, 512], F32)) for i in range(8)]
    B_ps = [Buf("ps%d" % i, excl=True) for i in range(8)]
    B_psA = [[Buf("psA%d_%d" % (i, p)) for p in range(2)] for i in range(4)]
    B_psT = [[Buf("psT%d_%d" % (i, p)) for p in range(2)] for i in range(4)]
    pool_banks = [0, 1, 2, 3, 4, 5]
    ps_state = {"next": 0, "held": set(), "live": set()}
    def ps_alloc(keep=False):
        for _ in range(2 * len(pool_banks)):
            b = pool_banks[ps_state["next"] % len(pool_banks)]
            ps_state["next"] += 1
            if b not in ps_state["held"] and b not in ps_state["live"]:
                if keep:
                    ps_state["live"].add(b)
                return b
        raise RuntimeError("out of PSUM banks")
    def ps_free(*bs):
        for b in bs:
            ps_state["live"].discard(b)
    def ps_hold(n):
        out = []
        for _ in range(n):
            b = ps_alloc()
            ps_state["held"].add(b)
            out.append(b)
        return out
    def ps_release(bs):
        for b in bs:
            ps_state["held"].discard(b)

    sem_names = ["pe", "act", "dve", "pool", "sp", "d_setup", "d_setup2", "d_x", "d_out", "d_send", "d_recv", "cc", "d_brh0", "d_brh1"] + ["d_ring%d" % i for i in range(NSLOT)]
    sems = {n: es.enter_context(nc.semaphore("s_" + n)) for n in sem_names}
    block = es.enter_context(nc.Block())
    prog = []

    streams = {"pe": [], "act": [], "dve": [], "pool": [], "sp": []}

    class Rec:
        def __init__(self, en):
            self.en = en
        def __getattr__(self, name):
            en = self.en
            def call(*a, **k):
                h = RecIns()
                streams[en].append((name, a, k, h))
                return h
            return call

    class RecIns:
        def __init__(self):
            self.post = []
        def _wait_ge(self, s, v):
            self.post.append(("_wait_ge", (s, v)))
            return self
        def then_inc(self, s, v):
            self.post.append(("then_inc", (s, v)))
            return self

    engs = {k: Rec(k) for k in streams}
    tk = Trk(nc, engs, sems)

    def act(out, in_, func, reads, writes, bias=None, scale=None):
        kw = {}
        if bias is not None:
            kw["bias"] = bias
        if scale is not None:
            kw["scale"] = scale
        tk.op("act", lambda e: e.activation(out=out, in_=in_, func=func, **kw), reads, writes)

    def tt(en, out, in0, in1, op, reads, writes):
        tk.op(en, lambda e: e.tensor_tensor(out=out, in0=in0, in1=in1, op=op), reads, writes)

    def ts(en, out, in0, s1, s2, op0, op1, reads, writes):
        if s2 is None:
            tk.op(en, lambda e: e.tensor_scalar(out=out, in0=in0, scalar1=s1, scalar2=None, op0=op0), reads, writes)
        else:
            tk.op(en, lambda e: e.tensor_scalar(out=out, in0=in0, scalar1=s1, scalar2=s2, op0=op0, op1=op1), reads, writes)

    def stt(out, in0, scalar, in1, op0, op1, reads, writes):
        tk.op("dve", lambda e: e.scalar_tensor_tensor(out=out, in0=in0, scalar=scalar, in1=in1, op0=op0, op1=op1), reads, writes)

    def cpy(en, out, in_, reads, writes):
        if en == "act":
            tk.op("act", lambda e: e.copy(out=out, in_=in_), reads, writes)
        else:
            tk.op(en, lambda e: e.tensor_copy(out=out, in_=in_), reads, writes)

    def mmg(lst, reads, writes):
        fns = []
        for (o, l_, r_, st, sp) in lst:
            fns.append((lambda o=o, l_=l_, r_=r_, st=st, sp=sp: (lambda e: e.matmul(o, l_, r_, start=st, stop=sp)))())
        tk.mm(fns, reads, writes)

    setup_bufs = [B_pv, B_const]
    tk.dma("sp", "d_setup", pvec[:], pvec_d[:, :], [], [B_pv])
    tk.dma("sp", "d_setup", cst[:], cst_d[:, 0:896], [], [B_const])
    tk.dma("sp", "d_setup", gb[:], gb_d[:, :], [], [B_const])
    tk.dma("sp", "d_setup", sel[:], cst_d[0:4, 896:1408].rearrange("p (j m) -> p j m", j=4), [], [B_const])
    tk.dma("sp", "d_setup", maskrep[:], cst_d[0:64, 1408:1664].rearrange("p (h t) -> p h t", h=4), [], [B_const])
    tk.dma("pool", "d_setup2", bd[:], bd_d[:, :, :], [], [B_const])
    tk.dma("pool", "d_setup2", wif[:], wif_d[:, :, :], [], [B_const])
    tk.dma("pool", "d_setup2", lr2[:], lr2_d[:, :, :], [], [B_const])
    tk.dma("pool", "d_setup2", identb[:], cst_d[:, 0:128], [], [B_const])
    for en_ in ("pe", "act", "dve", "pool"):
        for sk_ in ("d_setup", "d_setup2"):
            engs[en_].wait_ge(sems[sk_], tk.dmaval[sk_])
            tk.waited[en_][("d", sk_)] = tk.dmaval[sk_]
    for b_ in (B_pv, B_const):
        b_.w = None

    B_dc = Buf("dcst")
    DC_S1, DC_S2, DC_HBA, DC_HBX, DC_C0, DC_C1, DC_HSK, DC_NB2, DC_LB = 0, 4, 8, 12, 16, 20, 24, 28, 30
    l = 0
    lam = pvec[:, pv(l, "lru_lam"):pv(l, "lru_lam") + 4]
    act(dcst[:, 0:4], lam, AF.Exp, [B_pv], [B_dc], scale=-1.0)
    act(dcst[:, 0:4], dcst[:, 0:4], AF.Ln, [B_dc], [B_dc], bias=1.0)
    ts("dve", dcst[:, 4:8], dcst[:, 0:4], -8.0, None, ALU.mult, None, [B_dc], [B_dc])
    ts("dve", dcst[:, 0:4], dcst[:, 0:4], -4.0, None, ALU.mult, None, [B_dc], [B_dc])
    ts("dve", dcst[:, 8:12], pvec[:, pv(l, "lru_ba"):pv(l, "lru_ba") + 4], 0.5, None, ALU.mult, None, [B_pv], [B_dc])
    ts("dve", dcst[:, 12:16], pvec[:, pv(l, "lru_bx"):pv(l, "lru_bx") + 4], 0.5, None, ALU.mult, None, [B_pv], [B_dc])
    ts("dve", dcst[:, 24:28], pvec[:, pv(l, "m_skip"):pv(l, "m_skip") + 4], 0.5, None, ALU.mult, None, [B_pv], [B_dc])
    ts("dve", dcst[:, 28:30], pvec[:, pv(l, "g_b2"):pv(l, "g_b2") + 2], -1.0, None, ALU.mult, None, [B_pv], [B_dc])
    ts("dve", gbn[:, 0:1], gb[:, 1:2], -1.0, None, ALU.mult, None, [B_const], [B_dc])
    l0 = pvec[:, pv(l, "h_lbl"):pv(l, "h_lbl") + 4]
    l1 = pvec[:, pv(l, "h_lbl1"):pv(l, "h_lbl1") + 4]
    tt("dve", dcst[:, 30:34], l1, l0, ALU.subtract, [B_pv], [B_dc])
    act(dcst[:, 30:34], dcst[:, 30:34], AF.Tanh, [B_dc], [B_dc], scale=0.5)
    ts("dve", dcst[:, 30:34], dcst[:, 30:34], 0.5, 0.5, ALU.mult, ALU.add, [B_dc], [B_dc])
    ts("dve", dcst[:, 30:34], dcst[:, 30:34], pvec[:, pv(l, "flag_b"):pv(l, "flag_b") + 1], None, ALU.mult, None, [B_dc, B_pv], [B_dc])
    ts("dve", dcst[:, 20:24], dcst[:, 30:34], -0.5, 0.5, ALU.mult, ALU.add, [B_dc], [B_dc])
    tt("dve", dcst[:, 16:20], dcst[:, 30:34], dcst[:, 20:24], ALU.add, [B_dc], [B_dc])
    tk.op("dve", lambda e: e.memset(st_conv[:], 0.0), [], [B_stconv])
    tk.op("dve", lambda e: e.memset(st_lru[:], 0.0), [], [B_stlru])
    allC = [b for l in B_stC for b in l]; allH = [b for l in B_stH for b in l]; allG = [b for l in B_stG for b in l]
    tk.op("dve", lambda e: e.memset(st_C[:], 0.0), [], allC)
    tk.op("dve", lambda e: e.memset(st_H[:], 0.0), [], allH)
    tk.op("dve", lambda e: e.memset(st_G[:], 0.0), [], allG)
    tk.op("dve", lambda e: e.memset(onesT[:], 1.0), [], [B_const])
    tk.op("dve", lambda e: e.memset(vflat[:], 0.0), [], [B_vflat])
    tk.op("dve", lambda e: e.memset(ATm[:], 0.0), [], [b_ for l_ in B_ATm for b_ in l_])
    tk.op("dve", lambda e: e.memset(ktok[:], 0.0), [], [b_ for l_ in B_ktok for b_ in l_])
    tk.op("dve", lambda e: e.memset(hT[:], 0.0), [], [B_h])
    tk.op("dve", lambda e: e.memset(mg[:], 0.0), [], B_mgc)
    for i_ in range(NB):
        tk.op("dve", lambda e, i_=i_: e.memset(BtP[i_][:], 0.0), [], [BB[i_]])

    def wgroups(l):
        g = []
        g.append(("in", C_LRUX, 1024))
        g.append(("in", C_MX, 1024))
        g.append(("in", C_MZ, 512))
        g.append(("in", C_HQ, 1024))
        g.append(("in", C_HI, 1024))
        g.append(("in", C_GQ, 1024))
        g.append(("in", C_GLR, 528))
        for n in range(4):
            g.append(("in", C_MERGE + n * 1024, 1024))
        g.append(("out", 0, 1024))
        return g
    wsched = []
    for t in range(NT + 1):
        for l in range(n_layers):
            for gi, g in enumerate(wgroups(l)):
                wsched.append((l, g))
    wstate = {"issued": 0, "consumed": 0, "done": 0}
    def pump():
        while wstate["issued"] < len(wsched) and wstate["issued"] - wstate["done"] < NSLOT:
            i = wstate["issued"]
            l, (kind, a, ncols) = wsched[i]
            slot = i % NSLOT
            if kind == "in":
                src = w_in_d[l].rearrange("(kc p) c -> p kc c", p=128)[:, :, a:a + ncols]
                tk.dma("pool", "d_ring%d" % slot, ring[slot][:, :, 0:ncols], src, [], [B_ring[slot]])
            elif kind == "br":
                src = w_br_d[l, a].rearrange("(kc p) c -> p kc c", p=128)
                tk.dma("pool", "d_ring%d" % slot, ring[slot][:, 0:4, :], src, [], [B_ring[slot]])
            else:
                src = w_out_d[l].rearrange("(kc p) c -> p kc c", p=128)
                tk.dma("pool", "d_ring%d" % slot, ring[slot][:, :, :], src, [], [B_ring[slot]])
            wstate["issued"] += 1
    def issue_brh(l, n, half):
        src = w_br_d[l, n].rearrange("(kc p) c -> p kc c", p=128)[:, :, half * 512:(half + 1) * 512]
        tk.dma("pool", "d_brh%d" % half, brh[half], src, [], B_brh[half])

    def next_weight():
        i = wstate["consumed"]
        assert i < wstate["issued"], "weight group not issued (ring too small for live groups)"
        wstate["consumed"] += 1
        return ring[i % NSLOT], B_ring[i % NSLOT]
    def weights_done(n, defer=False):
        wstate["done"] += n
        if not defer:
            pump()

    def proj_fm(slot, B_slot, off, ncols, bank, extra_reads=()):
        lst = [(ps[bank][0:ncols, :], slot[:, kc, off:off + ncols], hT[:, kc, 0:T], kc == 0, kc == 7) for kc in range(8)]
        mmg(lst, [B_slot, B_h] + list(extra_reads), [B_ps[bank]])

    def stats_bcast(src_list, src_bufs, bank):
        n = len(src_list)
        lst = [(ps[bank][:, :], ones_f, src_list[i], i == 0, i == n - 1) for i in range(n)]
        mmg(lst, [B_const] + list(src_bufs), [B_ps[bank]])

    def rstd_from(bank, nfeat, out_f, out_buf):
        act(out_f, ps[bank][:, :], AF.Ln, [B_ps[bank]], [out_buf], bias=EPS, scale=1.0 / nfeat)
        act(out_f, out_f, AF.Exp, [out_buf], [out_buf], scale=-0.5)

    def rmsnorm_to_h(gcol0):
        sq = Ft[0]
        b = ps_alloc()
        for kc in range(8):
            act(Ft[kc % 2][:, :], xT[:, kc, :], AF.Square, [B_x], [BF[kc % 2]])
            tk.mm([lambda e, kc=kc: e.matmul(ps[b][:, :], ones_f, Ft[kc % 2][:, :], start=(kc == 0), stop=(kc == 7))],
                  [B_const, BF[kc % 2]], [B_ps[b]])
        rstd_from(b, D, Ft[2][:, :], BF[2])
        for kc in range(8):
            stt(hT[:, kc, 0:T], xT[:, kc, :], pvec[:, gcol0 + kc:gcol0 + kc + 1], Ft[2][:, :], ALU.mult, ALU.mult,
                [B_x, B_pv, BF[2]], [B_h])

    def conv_fm(bank, l, br, c, cw0, cb, out_f, out_buf, xp=None, Bxp=None):
        if xp is None:
            xp, Bxp = xpad, B_xpad
        si = (l * 2 + br) * 4 + c
        cpy("dve", xp[:, 0:3], st_conv[:, si, :], [B_stconv], [Bxp])
        cpy("act", xp[:, 3:3 + T], ps[bank][:, :], [B_ps[bank]], [Bxp])
        cpy("act", st_conv[:, si, :], xp[:, T:T + 3], [Bxp], [B_stconv])
        ts("dve", out_f, xp[:, 0:T], pvec[:, cw0:cw0 + 1], pvec[:, cb:cb + 1], ALU.mult, ALU.add,
           [Bxp, B_pv], [out_buf])
        for j in range(1, 4):
            stt(out_f, xp[:, j:j + T], pvec[:, cw0 + j:cw0 + j + 1], out_f, ALU.mult, ALU.add,
                [Bxp, B_pv, out_buf], [out_buf])

    def run_interleaved(gens):
        live = list(gens)
        while live:
            for g in list(live):
                try:
                    next(g)
                except StopIteration:
                    live.remove(g)

    B_ATm1 = Buf("ATm"); B_ktok1 = Buf("ktok"); B_Sb1 = Buf("Sbf")

    def chunk_attn(heads, vtok, qT, kT, kP, B_q, B_k, vcols, state_all, state_f, B_state, local, decb, dec, B_dec, obanks, dbanks=None, ubank=None):
        nh = len(heads)
        vs = {h: hi for hi, h in enumerate(heads)}
        Bst = [B_state[h] for h in heads]
        Bq = [B_q[h] for h in heads]
        Bk = list({id(B_k[h]): B_k[h] for h in heads}.values())
        Bd = list({id(B_dec[h]): B_dec[h] for h in heads}.values())
        Sb = Sbf[:, 0:nh, 0:vcols]
        ureg_all = ps[ubank][:, 0:nh * vcols].rearrange("p (h e) -> p h e", h=nh)
        if not local:
            fns = []
            for hi, h in enumerate(heads):
                fns.append(lambda e, hi=hi, h=h: e.matmul(ps[ubank][:, hi * vcols:(hi + 1) * vcols], ident_f, state_f[h],
                                                       start=(hi == 0), stop=(hi == nh - 1), skip_group_check=True))
            tk.mm(fns, [B_const] + Bst, [B_ps[ubank]])
        cpy("act", Sb, state_all, Bst, [B_Sb1])

        def stage1(j):
            cs = slice(j * L, (j + 1) * L)
            fns = []
            for hi, h in enumerate(heads):
                fns.append(lambda e, hi=hi, h=h: e.matmul(ps[6][:, hi * 64:(hi + 1) * 64], kP[h][:, j * L:j * L + 128], qT[h][:, cs], start=True, stop=True))
            tk.mm(fns, Bk + Bq, [B_ps[6]])
            fns = []
            for hi, h in enumerate(heads):
                fns.append(lambda e, hi=hi, h=h: e.matmul(ps[7][:, hi * 128:(hi + 1) * 128], kP[h][:, j * L:j * L + 128], identb[:, :], start=True, stop=True))
            tk.mm(fns, Bk + [B_const], [B_ps[7]])
            tt("dve", ATm[0:64, 0:nh, 0, :], ps[6][0:64, 0:nh * 64].rearrange("p (h t) -> p h t", h=nh), maskrep[:, 0:nh, :], ALU.mult,
               [B_ps[6], B_const], [B_ATm1])
            cpy("act", ktok[0:64, 0:nh, 0, :], ps[7][0:64, 0:nh * 128].rearrange("p (h e) -> p h e", h=nh), [B_ps[7]], [B_ktok1])

        def stage2_pe(j):
            cs = slice(j * L, (j + 1) * L)
            for hi, h in enumerate(heads):
                lst = [(ps[obanks[hi]][:, cs], vtok[:, hi, j, 0:128], ATm[:, hi, 0, :], True, False),
                       (ps[obanks[hi]][:, cs], Sbf[:, hi, 0:128], qT[h][:, cs], False, True)]
                mmg(lst, [B_vflat, B_ATm1, B_Sb1, B_q[h]], [B_ps[obanks[hi]]])
                if vcols == 256:
                    lst = [(ps[dbanks[hi]][:, cs], vtok[:, hi, j, 128:256], ATm[:, hi, 0, :], True, False),
                           (ps[dbanks[hi]][:, cs], Sbf[:, hi, 128:256], qT[h][:, cs], False, True)]
                    mmg(lst, [B_vflat, B_ATm1, B_Sb1, B_q[h]], [B_ps[dbanks[hi]]])
            fns = []
            for hi, h in enumerate(heads):
                if local:
                    fns.append(lambda e, hi=hi: e.matmul(ps[ubank][:, hi * vcols:(hi + 1) * vcols], ktok[:, hi, 0, :], vtok[:, hi, j, 0:vcols],
                                                        start=True, stop=True, skip_group_check=True))
                else:
                    fns.append(lambda e, hi=hi: e.matmul(ps[ubank][:, hi * vcols:(hi + 1) * vcols], ktok[:, hi, 0, :], vtok[:, hi, j, 0:vcols],
                                                        start=False, stop=(j == NCH - 1), skip_group_check=True))
            tk.mm(fns, [B_ktok1, B_vflat], [B_ps[ubank]])

        def stage2_state(j):
            if local:
                tt("dve", state_all, state_all, ureg_all, ALU.add, Bst + [B_ps[ubank]], Bst)
                tt("dve", state_all, state_all, decb(j), ALU.mult, Bst + Bd, Bst)
                if j < NCH - 1:
                    cpy("act", Sb, state_all, Bst, [B_Sb1])
            else:
                if j < NCH - 1:
                    cpy("act", Sb, ureg_all, [B_ps[ubank]], [B_Sb1])
                else:
                    for hi, h in enumerate(heads):
                        dcol = dec[h][:, T - 1:T]
                        ts("dve", state_f[h], ps[ubank][:, hi * vcols:(hi + 1) * vcols], dcol, None, ALU.mult, None,
                           [B_ps[ubank], B_dec[h]], [B_state[h]])

        stage1(0)
        for j in range(NCH):
            stage2_pe(j)
            if j + 1 < NCH:
                stage1(j + 1)
            stage2_state(j)

    pump()
    keepc = pvec[:, pv(0, "keep"):pv(0, "keep") + 1]
    flagb = pvec[:, pv(0, "flag_b"):pv(0, "flag_b") + 1]
    for step in range(NT + 1):
        t = min(step, NT - 1)
        tsl = slice(t * T, (t + 1) * T)
        tk.dma("sp", "d_x", xT[:, :, :], xT_d.rearrange("(kc p) s -> p kc s", p=128)[:, :, tsl], [], [B_x])
        if step >= 1:
            tk.dma("sp", "d_recv", acc[:, :, :], recv_d[0:D, :].rearrange("(kc p) s -> p kc s", p=128), [B_recv], B_acc)
            for kc in range(8):
                stt(xT[:, kc, :], acc[:, kc, :], flagb, xT[:, kc, :], ALU.mult, ALU.add, [B_acc[kc], B_pv, B_x], [B_x])
        if step == 1:
            ts("dve", st_conv[:, :, :], st_conv[:, :, :], keepc, None, ALU.mult, None, [B_stconv, B_pv], [B_stconv])
            ts("dve", st_lru[:, :], st_lru[:, :], keepc, None, ALU.mult, None, [B_stlru, B_pv], [B_stlru])
            for h in range(4):
                ts("dve", st_C[:, h, :], st_C[:, h, :], keepc, None, ALU.mult, None, [B_stC[0][h], B_pv], [B_stC[0][h]])
                ts("dve", st_H[:, h, :], st_H[:, h, :], keepc, None, ALU.mult, None, [B_stH[0][h], B_pv], [B_stH[0][h]])
                ts("dve", st_G[:, h, :], st_G[:, h, :], keepc, None, ALU.mult, None, [B_stG[0][h], B_pv], [B_stG[0][h]])

        for l in range(n_layers):
            rmsnorm_to_h(pv(l, "norm_g"))

            slotA, B_slotA = next_weight()
            slot, B_slot = next_weight()
            slot2, B_slot2 = next_weight()
            def gen_lru(slot=slotA, B_slot=B_slotA):
                for c in range(4):
                    b1 = ps_alloc(True)
                    proj_fm(slot, B_slot, c * 128, 128, b1)
                    yield
                    xa, Bxa = Ft[0], BF[0]
                    conv_fm(b1, l, 0, c, pv(l, "lru_cw", c * 4), pv(l, "lru_cb", c), xa[:, :], Bxa)
                    ps_free(b1)
                    yield
                    cpy("act", Bt[0][:, :], xa[:, :], [Bxa], [BB[0]])
                    yield
                    b2 = ps_alloc(True); b3 = ps_alloc(True)
                    mmg([(ps[b2][:, :], bd[:, l * 20 + c, :], Bt[0][:, :], True, True)], [B_const, BB[0]], [B_ps[b2]])
                    yield
                    mmg([(ps[b3][:, :], bd[:, l * 20 + 4 + c, :], Bt[0][:, :], True, True)], [B_const, BB[0]], [B_ps[b3]])
                    yield
                    r_, Br = Ft[1], BF[1]
                    i_, Bi = Ft[2], BF[2]
                    act(r_[:, :], ps[b2][:, :], AF.Tanh, [B_ps[b2], B_dc], [Br], bias=dcst[:, DC_HBA + c:DC_HBA + c + 1], scale=0.5)
                    ps_free(b2)
                    yield
                    act(i_[:, :], ps[b3][:, :], AF.Tanh, [B_ps[b3], B_dc], [Bi], bias=dcst[:, DC_HBX + c:DC_HBX + c + 1], scale=0.5)
                    ps_free(b3)
                    yield
                    a_, Ba = Ft[3], BF[3]
                    m_, Bm = Ft[4], BF[4]
                    act(a_[:, :], r_[:, :], AF.Exp, [Br, B_dc], [Ba], scale=dcst[:, DC_S1 + c:DC_S1 + c + 1], bias=dcst[:, DC_S1 + c:DC_S1 + c + 1])
                    yield
                    act(m_[:, :], r_[:, :], AF.Exp, [Br, B_dc], [Bm], scale=dcst[:, DC_S2 + c:DC_S2 + c + 1], bias=dcst[:, DC_S2 + c:DC_S2 + c + 1])
                    yield
                    act(m_[:, :], m_[:, :], AF.Ln, [Bm], [Bm], bias=1.0, scale=-1.0)
                    yield
                    act(m_[:, :], m_[:, :], AF.Exp, [Bm], [Bm], scale=0.5)
                    yield
                    stt(i_[:, :], i_[:, :], 1.0, xa[:, :], ALU.add, ALU.mult, [Bi, Bxa], [Bi])
                    yield
                    stt(i_[:, :], i_[:, :], 0.5, m_[:, :], ALU.mult, ALU.mult, [Bi, Bm], [Bi])
                    yield
                    hs, Bhs = Ft[5], BF[5]
                    sc = l * 4 + c
                    tk.op("dve", lambda e, sc=sc: e.tensor_tensor_scan(out=hs[:, :], data0=a_[:, :], data1=i_[:, :],
                                                                       initial=st_lru[:, sc:sc + 1], op0=ALU.mult, op1=ALU.add),
                          [Ba, Bi, B_stlru], [Bhs])
                    cpy("act", st_lru[:, sc:sc + 1], hs[:, T - 1:T], [Bhs], [B_stlru])
                    yield
                    b4 = ps_alloc(True)
                    proj_fm(slot, B_slot, 512 + c * 128, 128, b4)
                    yield
                    sz, Bsz = Ft[6], BF[6]
                    act(sz[:, :], ps[b4][:, :], AF.Tanh, [B_ps[b4]], [Bsz], scale=0.5)
                    yield
                    stt(sz[:, :], sz[:, :], 1.0, ps[b4][:, :], ALU.add, ALU.mult, [Bsz, B_ps[b4]], [Bsz])
                    ps_free(b4)
                    yield
                    stt(yT[:, 0 + c, :], sz[:, :], 0.5, hs[:, :], ALU.mult, ALU.mult, [Bhs, Bsz], [B_y[0 + c]])
                    yield

            def gen_mprep():
                for h in range(4):
                    b1 = ps_alloc(True)
                    proj_fm(slot, B_slot, h * 128, 128, b1)
                    yield
                    cpy("dve", mx_b[:, h, :], ps[b1][:, :], [B_ps[b1]], [B_mxb[h]])
                    yield
                    xc, Bxc = Ft[7], BF[7]
                    conv_fm(b1, l, 1, h, pv(l, "m_cw", h * 4), pv(l, "m_cb", h), xc[:, :], Bxc, xp=xpad2, Bxp=B_xpad2)
                    ps_free(b1)
                    yield
                    act(xm_f[:, h, :], xc[:, :], AF.Tanh, [Bxc], [B_xm[h]], scale=0.5)
                    yield
                    stt(xm_f[:, h, :], xm_f[:, h, :], 1.0, xc[:, :], ALU.add, ALU.mult, [B_xm[h], Bxc], [B_xm[h]])
                    yield
                    act(xm_b[:, h, :], xm_f[:, h, :], AF.Identity, [B_xm[h]], [B_xmb[h]], scale=0.5)
                    yield
                    for qi, (srcb, Bsrc) in enumerate([(xm_b, B_xmb), (xm_b, B_xmb), (mx_b, B_mxb)]):
                        b2 = ps_alloc(True)
                        mmg([(ps[b2][:, :], bd[:, l * 20 + 8 + qi * 4 + h, :], srcb[:, h, :], True, True)], [B_const, Bsrc[h]], [B_ps[b2]])
                        cpy("act" if qi != 1 else "dve", qkv_b[:, qi * 4 + h, :], ps[b2][:, :], [B_ps[b2]], [B_qkv[qi * 4 + h]])
                        ps_free(b2)
                        yield

            run_interleaved([gen_lru(), gen_mprep()])
            weights_done(1)

            bi_ = ps_alloc(); bf_ = ps_alloc()
            for gi, bank in ((0, bi_), (1, bf_)):
                lst = [(ps[bank][0:4, :], wif[:, (l * 2 + gi) * 12 + ci, :], qkv_b[:, ci, :], ci == 0, ci == 11) for ci in range(12)]
                mmg(lst, [B_const] + B_qkv, [B_ps[bank]])
            li, lf, G, eG, wk = g4[0], g4[1], g4[2], g4[3], g4[0]
            act(li[:, :], ps[bi_][0:4, :], AF.Identity, [B_ps[bi_], B_const], [B_g4[0]], bias=gb[:, l * 2:l * 2 + 1])
            act(lf[:, :], ps[bf_][0:4, :], AF.Exp, [B_ps[bf_], B_dc], [B_g4[1]], bias=gbn[:, 0:1], scale=-1.0)
            act(lf[:, :], lf[:, :], AF.Ln, [B_g4[1]], [B_g4[1]], bias=1.0)
            tk.op("dve", lambda e: e.tensor_tensor_scan(out=G[:, :], data0=ones4, data1=lf[:, :], initial=0.0,
                                                        op0=ALU.mult, op1=ALU.add), [B_g4[1], B_const], [B_g4[2]])
            act(eG[:, :], G[:, :], AF.Exp, [B_g4[2]], [B_g4[3]], scale=-1.0)
            tt("dve", wk[:, :], li[:, :], G[:, :], ALU.add, [B_g4[0], B_g4[2]], [B_g4[0]])
            act(wk[:, :], wk[:, :], AF.Exp, [B_g4[0]], [B_g4[0]])
            GO = acc[:, 4:8, :]; B_go = B_acc[4:8]
            for h in range(4):
                b9 = ps_alloc()
                proj_fm(slot, B_slot, 512 + h * 128, 128, b9)
                act(GO[:, h, :], ps[b9][:, :], AF.Tanh, [B_ps[b9]], [B_go[h]], scale=0.5)
                b12 = ps_alloc()
                proj_fm(slot2, B_slot2, h * 128, 128, b12)
                act(GZ[:, h, :], ps[b12][:, :], AF.Tanh, [B_ps[b12]], [B_gz[h]], scale=0.5)
                stt(GZ[:, h, :], GZ[:, h, :], 1.0, ps[b12][:, :], ALU.add, ALU.mult, [B_gz[h], B_ps[b12]], [B_gz[h]])
            weights_done(2)
            for pair in range(2):
                hp = [pair * 2, pair * 2 + 1]
                qT_ = {}; kT_ = {}; kP_ = {}; Bq_ = {}; Bk_ = {}; dec_ = {}; Bdec_ = {}
                tk.op("pool", lambda e: e.memset(vtok2[0:64, :, :, 128:256], 1.0), [], [B_vflat])
                for hi, h in enumerate(hp):
                    for half in range(2):
                        b3 = ps_alloc()
                        fns = []
                        for jj in range(4):
                            j = half * 4 + jj
                            fns.append(lambda e, j=j, jj=jj, b3=b3, h=h: e.matmul(ps[b3][:, jj * 128:(jj + 1) * 128], mx_bP[:, h, j * L:j * L + 128],
                                                                                 bd[:, l * 20 + 16 + h, :], start=True, stop=True))
                        tk.mm(fns, [B_mxb[h], B_const], [B_ps[b3]])
                        cpy("act", vtok2[0:64, hi, half * 4:half * 4 + 4, 0:128], ps[b3][0:64, :].rearrange("p (j e) -> p j e", j=4),
                            [B_ps[b3]], [B_vflat])
                    b5 = ps_alloc(); b6 = ps_alloc()
                    mmg([(ps[b5][:, :], sel[:, h, :], eG[:, :], True, True)], [B_const, B_g4[3]], [B_ps[b5]])
                    mmg([(ps[b6][:, :], sel[:, h, :], wk[:, :], True, True)], [B_const, B_g4[0]], [B_ps[b6]])
                    eGb, BeGb = Ft[6 + hi], BF[6 + hi]
                    wkb, Bwkb = Ft[8 + hi], BF[8 + hi]
                    cpy("act", eGb[:, :], ps[b5][:, :], [B_ps[b5]], [BeGb])
                    cpy("act", wkb[:, :], ps[b6][:, :], [B_ps[b6]], [Bwkb])
                    b7 = ps_alloc(); b8 = ps_alloc()
                    mmg([(ps[b7][:, :], bd[:, l * 20 + 8 + h, :], xm_b[:, h, :], True, True)], [B_const, B_xmb[h]], [B_ps[b7]])
                    mmg([(ps[b8][:, :], bd[:, l * 20 + 12 + h, :], xm_b[:, h, :], True, True)], [B_const, B_xmb[h]], [B_ps[b8]])
                    stt(Bt[1 + hi][:, :], ps[b7][:, :], 128.0 ** -0.5, eGb[:, :], ALU.mult, ALU.mult, [B_ps[b7], BeGb], [BB[1 + hi]])
                    tt("dve", Bt[3 + hi][:, :], ps[b8][:, :], wkb[:, :], ALU.mult, [B_ps[b8], Bwkb], [BB[3 + hi]])
                    qT_[h] = Bt[1 + hi]; kT_[h] = Bt[3 + hi]; kP_[h] = BtP[3 + hi]; Bq_[h] = BB[1 + hi]; Bk_[h] = BB[3 + hi]
                    dec_[h] = eGb; Bdec_[h] = BeGb
                held = ps_hold(5)
                ob = held[0:2]; db = held[2:4]; ub = held[4]
                sfl = {h: st_C[:, l * 4 + h, :] for h in hp}
                chunk_attn(hp, vtok2, qT_, kT_, kP_, Bq_, Bk_, 256, st_C[:, l * 4 + pair * 2:l * 4 + pair * 2 + 2, :], sfl, {h: B_stC[l][h] for h in hp}, False, None, dec_, Bdec_, ob, db, ub)
                for hi, h in enumerate(hp):
                    dn, Bdn = Ft[hi], BF[hi]
                    ts("dve", dn[:, :], ps[db[hi]][:, :], -1.0, 1.0, ALU.mult, ALU.max, [B_ps[db[hi]]], [Bdn])
                    tt("dve", dn[:, :], dn[:, :], ps[db[hi]][:, :], ALU.max, [Bdn, B_ps[db[hi]]], [Bdn])
                for hi, h in enumerate(hp):
                    act(Ft[hi][:, :], Ft[hi][:, :], AF.Ln, [BF[hi]], [BF[hi]])
                for hi, h in enumerate(hp):
                    act(Ft[hi][:, :], Ft[hi][:, :], AF.Exp, [BF[hi]], [BF[hi]], scale=-1.0)
                for hi, h in enumerate(hp):
                    stt(Ft[10 + hi][:, :], ps[ob[hi]][:, :], 0.5, Ft[hi][:, :], ALU.mult, ALU.mult, [B_ps[ob[hi]], BF[hi]], [BF[10 + hi]])
                ps_release(held)

                def gen_post(hi, h):
                    hm, Bhm = Ft[10 + hi], BF[10 + hi]
                    base = 2 + 4 * hi
                    stt(hm[:, :], GO[:, h, :], 1.0, hm[:, :], ALU.add, ALU.mult, [Bhm, B_go[h]], [Bhm])
                    yield
                    b10 = ps_alloc(True)
                    stats_bcast([hm[:, :]], [Bhm], b10)
                    yield
                    xcn, Bxcn = Ft[base], BF[base]
                    stt(xcn[:, :], ps[b10][:, :], -1.0 / 128.0, hm[:, :], ALU.mult, ALU.add, [B_ps[b10], Bhm], [Bxcn])
                    ps_free(b10)
                    yield
                    sq, Bsq = Ft[base + 1], BF[base + 1]
                    act(sq[:, :], xcn[:, :], AF.Square, [Bxcn], [Bsq])
                    yield
                    b11 = ps_alloc(True)
                    stats_bcast([sq[:, :]], [Bsq], b11)
                    yield
                    rs, Brs = Ft[base + 2], BF[base + 2]
                    act(rs[:, :], ps[b11][:, :], AF.Ln, [B_ps[b11]], [Brs], bias=EPS, scale=1.0 / 128)
                    ps_free(b11)
                    yield
                    act(rs[:, :], rs[:, :], AF.Exp, [Brs], [Brs], scale=-0.5)
                    yield
                    tt("dve", xcn[:, :], xcn[:, :], rs[:, :], ALU.mult, [Bxcn, Brs], [Bxcn])
                    yield
                    sk, Bsk = Ft[base + 3], BF[base + 3]
                    act(sk[:, :], xm_f[:, h, :], AF.Identity, [B_xm[h], B_dc], [Bsk], scale=dcst[:, DC_HSK + h:DC_HSK + h + 1])
                    yield
                    stt(xcn[:, :], xcn[:, :], pvec[:, pv(l, "m_nw", h):pv(l, "m_nw", h) + 1], sk[:, :], ALU.mult, ALU.add,
                        [Bxcn, B_pv, Bsk], [Bxcn])
                    yield
                    stt(yT[:, 4 + h, :], GZ[:, h, :], 0.5, xcn[:, :], ALU.mult, ALU.mult, [Bxcn, B_gz[h]], [B_y[4 + h]])
                    yield
                run_interleaved([gen_post(hi, h) for hi, h in enumerate(hp)])

            slot, B_slot = next_weight()
            slot2, B_slot2 = next_weight()
            qT_ = {}; kT_ = {}; kP_ = {}; Bq_ = {}; Bk_ = {}; dec_ = {}; Bdec_ = {}
            def gen_hprep(heads_, tset):
                F_f, F_lg, F_G, F_en, F_qs, F_kk = tset
                for h in heads_:
                    bfk = ps_alloc(True)
                    proj_fm(slot, B_slot, 512 + h * 128, 128, bfk)
                    yield
                    f_, Bf_ = Ft[F_f], BF[F_f]
                    act(f_[:, :], ps[bfk][:, :], AF.Tanh, [B_ps[bfk]], [Bf_], scale=0.5)
                    ps_free(bfk)
                    yield
                    ts("dve", f_[:, :], f_[:, :], dcst[:, DC_C1 + h:DC_C1 + h + 1], dcst[:, DC_C0 + h:DC_C0 + h + 1],
                       ALU.mult, ALU.add, [Bf_, B_dc], [Bf_])
                    yield
                    lg, Blg = Ft[F_lg], BF[F_lg]
                    act(lg[:, :], f_[:, :], AF.Ln, [Bf_], [Blg])
                    yield
                    Gc, BGc = Ft[F_G], BF[F_G]
                    tk.op("dve", lambda e: e.tensor_tensor_scan(out=Gc[:, :], data0=rmask, data1=lg[:, :], initial=0.0, op0=ALU.mult, op1=ALU.add),
                          [Blg, B_const], [BGc])
                    yield
                    eGc, BeGc = Ft[3 + h], BF[3 + h]
                    act(eGc[:, :], Gc[:, :], AF.Exp, [BGc], [BeGc])
                    yield
                    enG, BenG = Ft[F_en], BF[F_en]
                    act(enG[:, :], Gc[:, :], AF.Exp, [BGc], [BenG], scale=-1.0)
                    yield
                    bq = ps_alloc(True)
                    proj_fm(slot, B_slot, h * 128, 128, bq)
                    yield
                    qs, Bqs = Ft[F_qs], BF[F_qs]
                    act(qs[:, :], ps[bq][:, :], AF.Tanh, [B_ps[bq]], [Bqs], scale=0.5)
                    yield
                    stt(qs[:, :], qs[:, :], 1.0, ps[bq][:, :], ALU.add, ALU.mult, [Bqs, B_ps[bq]], [Bqs])
                    ps_free(bq)
                    yield
                    stt(Bt[h][:, :], qs[:, :], 0.5 * 128.0 ** -0.5, eGc[:, :], ALU.mult, ALU.mult, [Bqs, BeGc], [BB[h]])
                    yield
                    kk, Bkk = Ft[F_kk], BF[F_kk]
                    ts("dve", kk[:, :], f_[:, :], -1.0, 1.0, ALU.mult, ALU.add, [Bf_], [Bkk])
                    yield
                    tt("dve", Bt[4 + h][:, :], kk[:, :], enG[:, :], ALU.mult, [Bkk, BenG], [BB[4 + h]])
                    yield
                    qT_[h] = Bt[h]; kT_[h] = Bt[4 + h]; kP_[h] = BtP[4 + h]; Bq_[h] = BB[h]; Bk_[h] = BB[4 + h]
                    dec_[h] = eGc; Bdec_[h] = BeGc

            def gen_hvz():
                for h in range(4):
                    b12 = ps_alloc(True)
                    proj_fm(slot2, B_slot2, 512 + h * 128, 128, b12)
                    yield
                    act(GZ[:, h, :], ps[b12][:, :], AF.Tanh, [B_ps[b12]], [B_gz[h]], scale=0.5)
                    yield
                    stt(GZ[:, h, :], GZ[:, h, :], 1.0, ps[b12][:, :], ALU.add, ALU.mult, [B_gz[h], B_ps[b12]], [B_gz[h]])
                    ps_free(b12)
                    yield
                for j in range(NCH):
                    b3 = ps_alloc(True)
                    lst = [(ps[b3][:, :], hT[:, kc, j * L:j * L + 128], slot2[:, kc, 0:512], kc == 0, kc == 7) for kc in range(8)]
                    mmg(lst, [B_h, B_slot2], [B_ps[b3]])
                    yield
                    cpy("act", vtok4[0:64, :, j, :], ps[b3][0:64, :].rearrange("p (h e) -> p h e", h=4), [B_ps[b3]], [B_vflat])
                    ps_free(b3)
                    yield

            run_interleaved([gen_hprep([0, 2], (0, 1, 2, 7, 8, 9)), gen_hprep([1, 3], (10, 11, 12, 13, 14, 15)), gen_hvz()])
            weights_done(2)
            held = ps_hold(5)
            ob = held[0:4]; ub = held[4]
            sfl = {h: st_H[:, l * 4 + h, :] for h in range(4)}
            chunk_attn([0, 1, 2, 3], vtok4, qT_, kT_, kP_, Bq_, Bk_, 128, st_H[:, l * 4:l * 4 + 4, :], sfl, {h: B_stH[l][h] for h in range(4)}, True,
                       lambda j: EG4[:, :, j * L + L - 1:j * L + L].to_broadcast([128, 4, 128]), dec_, Bdec_, ob, None, ub)
            for h in range(4):
                cpy("act", Ft[10 + h][:, :], ps[ob[h]][:, :], [B_ps[ob[h]]], [BF[10 + h]])
            ps_release(held)
            pb_ = []
            for h in range(4):
                act(Ft[h][:, :], Ft[10 + h][:, :], AF.Square, [BF[10 + h]], [BF[h]])
                b10 = ps_alloc(); pb_.append(b10)
                stats_bcast([Ft[h][:, :]], [BF[h]], b10)
            for h in range(4):
                act(Ft[h][:, :], ps[pb_[h]][:, :], AF.Ln, [B_ps[pb_[h]]], [BF[h]], bias=EPS, scale=1.0 / 128)
            for h in range(4):
                act(Ft[h][:, :], Ft[h][:, :], AF.Exp, [BF[h]], [BF[h]], scale=-0.5)
            for h in range(4):
                o_, Bo_ = Ft[10 + h], BF[10 + h]
                stt(o_[:, :], o_[:, :], pvec[:, pv(l, "h_nw"):pv(l, "h_nw") + 1], Ft[h][:, :], ALU.mult, ALU.mult, [Bo_, B_pv, BF[h]], [Bo_])
                stt(yT[:, 8 + h, :], GZ[:, h, :], 0.5, o_[:, :], ALU.mult, ALU.mult, [Bo_, B_gz[h]], [B_y[8 + h]])

            slot, B_slot = next_weight()
            slot2, B_slot2 = next_weight()
            bl = ps_alloc()
            proj_fm(slot2, B_slot2, 0, 16, bl)
            cpy("act", glr[:, :], ps[bl][0:16, :], [B_ps[bl]], [B_glr])
            qT_ = {}; kT_ = {}; kP_ = {}; Bq_ = {}; Bk_ = {}; dec_ = {}; Bdec_ = {}
            def gen_gprep(cc, tset):
                F_lg, F_G, F_en, F_qe = tset
                bg = ps_alloc(True)
                mmg([(ps[bg][:, :], lr2[:, l, cc * 128:(cc + 1) * 128], glr[:, :], True, True)], [B_const, B_glr], [B_ps[bg]])
                yield
                lg, Blg = Ft[F_lg], BF[F_lg]
                act(lg[:, :], ps[bg][:, :], AF.Exp, [B_ps[bg], B_dc], [Blg], bias=dcst[:, DC_NB2 + cc:DC_NB2 + cc + 1], scale=-1.0)
                ps_free(bg)
                yield
                act(lg[:, :], lg[:, :], AF.Ln, [Blg], [Blg], bias=1.0)
                yield
                Gc, BGc = Ft[F_G], BF[F_G]
                tk.op("dve", lambda e: e.tensor_tensor_scan(out=Gc[:, :], data0=ones_T, data1=lg[:, :], initial=0.0, op0=ALU.mult, op1=ALU.add),
                      [Blg, B_const], [BGc])
                yield
                eGc, BeGc = Ft[2 + cc], BF[2 + cc]
                act(eGc[:, :], Gc[:, :], AF.Exp, [BGc], [BeGc], scale=-1.0 / 16.0)
                yield
                enG, BenG = Ft[F_en], BF[F_en]
                act(enG[:, :], Gc[:, :], AF.Exp, [BGc], [BenG], scale=1.0 / 16.0)
                yield
                bq = ps_alloc(True)
                proj_fm(slot, B_slot, cc * 128, 128, bq)
                yield
                qe, Bqe = Ft[F_qe], BF[F_qe]
                stt(qe[:, :], ps[bq][:, :], 64.0 ** -0.5, eGc[:, :], ALU.mult, ALU.mult, [B_ps[bq], BeGc], [Bqe])
                ps_free(bq)
                yield
                for hh in range(2):
                    h = cc * 2 + hh
                    ts("dve", Bt[h][:, :], qe[:, :], rowmask[:, hh:hh + 1], None, ALU.mult, None, [Bqe, B_const], [BB[h]])
                    yield
                    qT_[h] = Bt[h]; Bq_[h] = BB[h]
                    kT_[h] = Bt[4 + cc]; kP_[h] = BtP[4 + cc]; Bk_[h] = BB[4 + cc]
                    dec_[h] = eGc; Bdec_[h] = BeGc
                bk = ps_alloc(True)
                proj_fm(slot, B_slot, 256 + cc * 128, 128, bk)
                yield
                tt("dve", Bt[4 + cc][:, :], ps[bk][:, :], enG[:, :], ALU.mult, [B_ps[bk], BenG], [BB[4 + cc]])
                ps_free(bk)
                yield

            def gen_gvz():
                for h in range(4):
                    b12 = ps_alloc(True)
                    proj_fm(slot2, B_slot2, 16 + h * 128, 128, b12)
                    yield
                    act(GZ[:, h, :], ps[b12][:, :], AF.Tanh, [B_ps[b12]], [B_gz[h]], scale=0.5)
                    yield
                    stt(GZ[:, h, :], GZ[:, h, :], 1.0, ps[b12][:, :], ALU.add, ALU.mult, [B_gz[h], B_ps[b12]], [B_gz[h]])
                    ps_free(b12)
                    yield
                for j in range(NCH):
                    b3 = ps_alloc(True)
                    lst = [(ps[b3][:, :], hT[:, kc, j * L:j * L + 128], slot[:, kc, 512:1024], kc == 0, kc == 7) for kc in range(8)]
                    mmg(lst, [B_h, B_slot], [B_ps[b3]])
                    yield
                    cpy("act", vtok4[0:64, :, j, :], ps[b3][0:64, :].rearrange("p (h e) -> p h e", h=4), [B_ps[b3]], [B_vflat])
                    ps_free(b3)
                    yield

            run_interleaved([gen_gprep(0, (0, 1, 4, 5)), gen_gprep(1, (6, 7, 8, 9)), gen_gvz()])
            issue_brh(l, 0, 0)
            issue_brh(l, 0, 1)
            weights_done(2)
            held = ps_hold(5)
            ob = held[0:4]; ub = held[4]
            sfl = {h: st_G[:, l * 4 + h, :] for h in range(4)}
            chunk_attn([0, 1, 2, 3], vtok4, qT_, kT_, kP_, Bq_, Bk_, 128, st_G[:, l * 4:l * 4 + 4, :], sfl, {h: B_stG[l][h] for h in range(4)}, False, None, dec_, Bdec_, ob, None, ub)
            for h in range(4):
                cpy("act", Ft[10 + h][:, :], ps[ob[h]][:, :], [B_ps[ob[h]]], [BF[10 + h]])
            ps_release(held)
            pb_ = []
            for h in range(4):
                act(Ft[h][:, :], Ft[10 + h][:, :], AF.Square, [BF[10 + h]], [BF[h]])
                b10 = ps_alloc(); pb_.append(b10)
                stats_bcast([Ft[h][:, :]], [BF[h]], b10)
            for h in range(4):
                act(Ft[h][:, :], ps[pb_[h]][:, :], AF.Ln, [B_ps[pb_[h]]], [BF[h]], bias=EPS, scale=1.0 / 128)
            for h in range(4):
                act(Ft[h][:, :], Ft[h][:, :], AF.Exp, [BF[h]], [BF[h]], scale=-0.5)
            for h in range(4):
                o_, Bo_ = Ft[10 + h], BF[10 + h]
                stt(o_[:, :], o_[:, :], pvec[:, pv(l, "g_nw"):pv(l, "g_nw") + 1], Ft[h][:, :], ALU.mult, ALU.mult, [Bo_, B_pv, BF[h]], [Bo_])
                stt(yT[:, 12 + h, :], GZ[:, h, :], 0.5, o_[:, :], ALU.mult, ALU.mult, [Bo_, B_gz[h]], [B_y[12 + h]])

            for n in range(4):
                slot, B_slot = next_weight()
                for dc in range(8):
                    bgate = ps_alloc(); bpr = ps_alloc()
                    proj_fm(slot, B_slot, dc * 128, 128, bgate)
                    hb = dc // 4
                    lst = [(ps[bpr][:, :], brh[hb][:, wc, (dc % 4) * 128:(dc % 4 + 1) * 128], yT[:, n * 4 + wc, :], wc == 0, wc == 3) for wc in range(4)]
                    mmg(lst, B_brh[hb] + B_y[n * 4:n * 4 + 4], [B_ps[bpr]])
                    if n < 3 and dc % 4 == 3:
                        issue_brh(l, n + 1, hb)
                    sg, Bsg = Ft[dc % 2], BF[dc % 2]
                    act(sg[:, :], ps[bgate][:, :], AF.Tanh, [B_ps[bgate]], [Bsg], scale=0.5)
                    if n == 0:
                        stt(acc[:, dc, :], sg[:, :], 1.0, ps[bpr][:, :], ALU.add, ALU.mult, [B_ps[bpr], Bsg], [B_acc[dc]])
                    else:
                        pr, Bpr = Ft[2 + dc % 2], BF[2 + dc % 2]
                        stt(pr[:, :], sg[:, :], 1.0, ps[bpr][:, :], ALU.add, ALU.mult, [B_ps[bpr], Bsg], [Bpr])
                        if n < 3:
                            tt("pool", acc[:, dc, :], acc[:, dc, :], pr[:, :], ALU.add, [B_acc[dc], Bpr], [B_acc[dc]])
                        else:
                            tt("pool", mg[:, dc, 0:T], acc[:, dc, :], pr[:, :], ALU.add, [B_acc[dc], Bpr], [B_mgc[dc]])
                weights_done(1, defer=(n == 3 and step < NT))
            slot, B_slot = next_weight()
            for ec in range(8):
                bo = ps_alloc()
                lst = [(ps[bo][:, :], slot[:, dc, ec * 128:(ec + 1) * 128], mg[:, dc, 0:T], dc == 0, dc == 7) for dc in range(8)]
                mmg(lst, [B_slot] + B_mgc, [B_ps[bo]])
                stt(xT[:, ec, :], ps[bo][:, :], 0.5, xT[:, ec, :], ALU.mult, ALU.add, [B_x, B_ps[bo]], [B_x])
            weights_done(1, defer=(step < NT))

        if step < NT:
            tk.dma("sp", "d_send", send_d.rearrange("(kc p) s -> p kc s", p=128), xT[:, :, :], [B_x], [B_send])
            tk.coll("pool", "cc", lambda e: e.collective_compute("AllGather", ALU.bypass, replica_groups=[[2 * i, 2 * i + 1] for i in range(n_pairs)],
                                                                ins=[send_d], outs=[recv_d]), [B_send], [B_recv])
            pump()
        if step >= 1:
            b = ps_alloc()
            for kc in range(8):
                act(Ft[kc % 2][:, :], xT[:, kc, :], AF.Square, [B_x], [BF[kc % 2]])
                tk.mm([lambda e, kc=kc, b=b: e.matmul(ps[b][:, :], ones_f, Ft[kc % 2][:, :], start=(kc == 0), stop=(kc == 7))],
                      [B_const, BF[kc % 2]], [B_ps[b]])
            rstd_from(b, D, Ft[2][:, :], BF[2])
            for kc in range(8):
                stt(osb[:, kc, :], xT[:, kc, :], pvec[:, pv(0, "final_g") + kc:pv(0, "final_g") + kc + 1], Ft[2][:, :], ALU.mult, ALU.mult,
                    [B_x, B_pv, BF[2]], [B_acc[kc]])
            osl = slice((step - 1) * T, step * T)
            tk.dma("sp", "d_out", outT_d.rearrange("(kc p) s -> p kc s", p=128)[:, :, osl], osb[:, :, :], B_acc, [])

    tk.final_wait("sp", "d_out")

    def replay(en):
        def f(e):
            for (name, a, k, h) in streams[en]:
                ins = getattr(e, name)(*a, **k)
                for (pn, pa) in h.post:
                    getattr(ins, pn)(*pa)
        return f
    block.tensor(replay("pe"))
    block.scalar(replay("act"))
    block.vector(replay("dve"))
    block.gpsimd(replay("pool"))
    block.sync(replay("sp"))
    es.close()
    return nc, tk


def _host_pack(inputs, l, is_b):
    f = lambda k: np.asarray(inputs[k], dtype=np.float32)
    pvec = np.zeros((128, PV_COLS), np.float32)
    def put(name, vec, nchunks, stride=1, off=0):
        for c in range(nchunks):
            pvec[:, pv(0, name) + c * stride + off] = vec[c * 128:(c + 1) * 128]
    for j in range(4):
        put("lru_cw", f("lru_conv_w")[l, j], 4, stride=4, off=j)
        put("m_cw", f("m_conv_w")[l, j], 4, stride=4, off=j)
    put("lru_cb", f("lru_conv_b")[l], 4)
    put("lru_ba", f("lru_ba")[l], 4)
    put("lru_bx", f("lru_bx")[l], 4)
    put("lru_lam", f("lru_lambda")[l], 4)
    put("m_cb", f("m_conv_b")[l], 4)
    put("m_nw", f("m_norm_w")[l], 4)
    put("m_skip", f("m_skip")[l], 4)
    put("h_lbl", f("h_lb_logits")[0], 4)
    put("h_lbl1", f("h_lb_logits")[1], 4)
    put("h_nw", f("h_norm_w")[l], 1)
    put("g_b2", f("g_b_lr2")[l], 2)
    put("g_nw", f("g_norm_w")[l], 1)
    put("norm_g", f("norm_g")[l], 8)
    put("final_g", f("final_g"), 8)
    pvec[:, pv(0, "flag_b")] = 1.0 if is_b else 0.0
    pvec[:, pv(0, "keep")] = 0.0 if is_b else 1.0
    bd = np.zeros((128, 20, 128), np.float32)
    for gi, key in enumerate(["lru_wa", "lru_wx"]):
        w = f(key)[l]
        for c in range(4):
            for b in range(2):
                bd[b * 64:(b + 1) * 64, gi * 4 + c, b * 64:(b + 1) * 64] = w[2 * c + b]
    for gi, key in enumerate(["m_wq", "m_wk", "m_wv"]):
        w = f(key)[l]
        for h in range(4):
            for b in range(32):
                bd[b * 4:(b + 1) * 4, 8 + gi * 4 + h, b * 4:(b + 1) * 4] = w[32 * h + b]
    wif = np.zeros((128, 2 * 12, 4), np.float32)
    gb = np.zeros((4, 2), np.float32)
    for gi, key in enumerate(["m_wi", "m_wf"]):
        w = f(key)[l]
        for ci in range(12):
            wif[:, gi * 12 + ci, :] = w[ci * 128:(ci + 1) * 128, :]
    gb[:, 0] = f("m_bi")[l]
    gb[:, 1] = f("m_bf")[l]
    lr2 = np.ascontiguousarray(f("g_w_lr2")[l][:, None, :])
    cst = np.zeros((128, 1664), np.float32)
    cst[:, 0:128] = np.eye(128, dtype=np.float32)
    cst[:, 128:256] = 1.0
    cst[0:64, 256:320] = np.triu(np.ones((64, 64), np.float32))
    rm = np.ones((T,), np.float32); rm[::L] = 0.0
    cst[:, 320:832] = rm[None, :]
    cst[0:64, 832] = 1.0
    cst[64:128, 833] = 1.0
    for j in range(4):
        cst[j, 896 + j * 128:896 + (j + 1) * 128] = 1.0
        cst[0:64, 1408 + j * 64:1408 + (j + 1) * 64] = np.triu(np.ones((64, 64), np.float32))
    return dict(w_in=np.ascontiguousarray(f("w_in")[l:l + 1]), w_branch=np.ascontiguousarray(f("w_branch")[l:l + 1]),
                w_out=np.ascontiguousarray(f("w_out")[l:l + 1]), pvec=pvec, bd=bd, wif=wif, gbias=gb, lr2=lr2, cst=cst)


_PROG_CACHE = {}


def kernel(**inputs):
    x = np.asarray(inputs["x"], dtype=np.float32)
    B, S, _ = x.shape
    packs = [_host_pack(inputs, 0, False), _host_pack(inputs, 1, True)]
    if (S, B) not in _PROG_CACHE:
        _PROG_CACHE[(S, B)] = build_program(S, n_pairs=B)[0]
    nc = _PROG_CACHE[(S, B)]
    zeros = np.zeros((D, S), np.float32)
    in_maps = []
    for b in range(B):
        ma = dict(packs[0]); ma["xT"] = np.ascontiguousarray(x[b].T)
        mb = dict(packs[1]); mb["xT"] = zeros
        in_maps += [ma, mb]
    res = run_bass_kernel_spmd(nc, in_maps, core_ids=list(range(2 * B)))
    out = np.stack([np.ascontiguousarray(res.results[2 * b + 1]["outT"].T) for b in range(B)], axis=0)
    return out.astype(np.float32)
```

```python
import numpy as np
import concourse.bass as bass
import concourse.mybir as mybir
from concourse.bass_utils import run_bass_kernel_spmd

F32 = mybir.dt.float32
BF16 = mybir.dt.bfloat16
AF = mybir.ActivationFunctionType
ALU = mybir.AluOpType

D = 1024
W = 512
D_IN = 10256
DEPTH = 2
T = 512
L = 64
NCH = T // L
EPS = 1e-6
C_LRUX, C_LRUZ = 0, 512
C_MX, C_MO, C_MZ = 1024, 1536, 2048
C_HQ, C_HF, C_HI, C_HZ = 2560, 3072, 3584, 4096
C_GQ, C_GK, C_GV, C_GLR, C_GZ = 4608, 4864, 5120, 5632, 5648
C_MERGE = 6160

PV_PER_LAYER = 96
def pv(l, name, c=0):
    base = l * PV_PER_LAYER
    table = {
        "lru_cw": 0,
        "lru_cb": 16,
        "lru_ba": 20,
        "lru_bx": 24,
        "lru_lam": 28,
        "m_cw": 32,
        "m_cb": 48,
        "m_nw": 52,
        "m_skip": 56,
        "h_lbl": 60,
        "h_nw": 64,
        "g_b2": 65,
        "g_nw": 67,
        "norm_g": 68,
        "final_g": 76,
        "flag_b": 84,
        "keep": 85,
        "h_lbl1": 86,
    }
    return base + table[name] + c
SAME_ENGINE_WINDOW = 3
PL = 1
PV_COLS = PL * PV_PER_LAYER


class Buf:
    __slots__ = ("w", "r", "name", "excl")
    def __init__(self, name="", excl=False):
        self.w = None
        self.r = []
        self.name = name
        self.excl = excl


class Trk:
    def __init__(self, nc, engs, sems):
        self.nc = nc
        self.E = engs
        self.S = sems
        self.tick = {k: 0 for k in engs}
        self.waited = {k: {} for k in engs}
        self.dmaval = {}
        self.nwait = 0
        self.ninst = 0
        self.efree = {k: 0.0 for k in engs}
        self.fin = {}
        self.last_fin = None
        self.COST = {"act": 0.6, "dve": 0.65, "pool": 1.6, "pe": 0.27, "sp": 0.1}

    def _time(self, en, cost, ndma=None):
        ready = 0.0
        for (kind, key), val in self._need.items():
            t = self.fin.get((kind, key, val), 0.0)
            if kind == "e" and key != en:
                t += 0.15
            if t > ready:
                ready = t
        start = max(self.efree[en], ready)
        fin = start + cost
        self.efree[en] = fin
        return fin

    def _deps(self, en, reads, writes):
        need = {}
        def add(dep):
            if dep is None:
                return
            kind, key, val = dep
            k = (kind, key)
            if need.get(k, 0) < val:
                need[k] = val
        for b in reads:
            add(b.w)
            if b.excl:
                for r in b.r:
                    add(r)
        for b in writes:
            add(b.w)
            for r in b.r:
                add(r)
        out = []
        self._need = need
        for (kind, key), val in need.items():
            if kind == "e" and key == en:
                if en == "pe":
                    continue
                if en != "pool" and val <= self.tick[en] - SAME_ENGINE_WINDOW:
                    continue
            if self.waited[en].get((kind, key), 0) >= val:
                continue
            self.waited[en][(kind, key)] = val
            out.append((self.S[key], val))
        return out

    def op(self, en, fn, reads=(), writes=(), cost=None):
        eng = self.E[en]
        waits = self._deps(en, reads, writes)
        tfin = self._time(en, self.COST[en] if cost is None else cost)
        for (s, v) in waits[1:]:
            eng.wait_ge(s, v)
            self.nwait += 1
        ins = fn(eng)
        if waits:
            ins._wait_ge(waits[0][0], waits[0][1])
        self.tick[en] += 1
        ins.then_inc(self.S[en], 1)
        me = ("e", en, self.tick[en])
        self.fin[me] = tfin
        self.last_fin = tfin
        for b in reads:
            if b.excl:
                b.w = me
                b.r = []
            else:
                b.r.append(me)
        for b in writes:
            b.w = me
            b.r = []
        self.ninst += 1
        return ins

    def mm(self, fns, reads=(), writes=(), cost=None):
        en = "pe"
        eng = self.E[en]
        waits = self._deps(en, reads, writes)
        tfin = self._time(en, (0.1 * len(fns)) if cost is None else cost)
        for (s, v) in waits[1:]:
            eng.wait_ge(s, v)
            self.nwait += 1
        ins = None
        for i, fn in enumerate(fns):
            ins = fn(eng)
            if i == 0 and waits:
                ins._wait_ge(waits[0][0], waits[0][1])
        self.tick[en] += 1
        ins.then_inc(self.S[en], 1)
        me = ("e", en, self.tick[en])
        self.fin[me] = tfin
        self.last_fin = tfin
        for b in reads:
            if b.excl:
                b.w = me
                b.r = []
            else:
                b.r.append(me)
        for b in writes:
            b.w = me
            b.r = []
        self.ninst += len(fns)

    def dma(self, en, semkey, out, in_, reads=(), writes=()):
        eng = self.E[en]
        waits = self._deps(en, reads, writes)
        for (s, v) in waits:
            eng.wait_ge(s, v)
            self.nwait += 1
        ins = eng.dma_start(out=out, in_=in_)
        self.dmaval[semkey] = self.dmaval.get(semkey, 0) + 16
        ins.then_inc(self.S[semkey], 16)
        me = ("d", semkey, self.dmaval[semkey])
        self.fin[me] = self._time(en, 0.1) + 12.0
        for b in reads:
            if b.excl:
                b.w = me
                b.r = []
            else:
                b.r.append(me)
        for b in writes:
            b.w = me
            b.r = []
        self.ninst += 1

    def coll(self, en, semkey, fn, reads=(), writes=()):
        eng = self.E[en]
        waits = self._deps(en, reads, writes)
        for (s, v) in waits:
            eng.wait_ge(s, v)
            self.nwait += 1
        ins = fn(eng)
        self.dmaval[semkey] = self.dmaval.get(semkey, 0) + 1
        ins.then_inc(self.S[semkey], 1)
        me = ("d", semkey, self.dmaval[semkey])
        self.fin[me] = self._time(en, 0.1) + 35.0
        for b in reads:
            if b.excl:
                b.w = me
                b.r = []
            else:
                b.r.append(me)
        for b in writes:
            b.w = me
            b.r = []
        self.ninst += 1

    def final_wait(self, en, semkey):
        self.E[en].wait_ge(self.S[semkey], self.dmaval[semkey])


def build_program(S, n_layers=PL, n_pairs=4):
    assert S % T == 0
    NT = S // T
    nc = bass.Bass("TRN2", target_bir_lowering=False)

    xT_d = nc.dram_tensor("xT", [D, S], F32, kind="ExternalInput").ap()
    w_in_d = nc.dram_tensor("w_in", [PL, D, D_IN], F32, kind="ExternalInput").ap()
    w_br_d = nc.dram_tensor("w_branch", [PL, 4, W, D], F32, kind="ExternalInput").ap()
    w_out_d = nc.dram_tensor("w_out", [PL, D, D], F32, kind="ExternalInput").ap()
    pvec_d = nc.dram_tensor("pvec", [128, PV_COLS], F32, kind="ExternalInput").ap()
    bd_d = nc.dram_tensor("bd", [128, PL * 20, 128], F32, kind="ExternalInput").ap()
    wif_d = nc.dram_tensor("wif", [128, PL * 2 * 12, 4], F32, kind="ExternalInput").ap()
    gb_d = nc.dram_tensor("gbias", [4, PL * 2], F32, kind="ExternalInput").ap()
    lr2_d = nc.dram_tensor("lr2", [16, PL, 256], F32, kind="ExternalInput").ap()
    cst_d = nc.dram_tensor("cst", [128, 1664], F32, kind="ExternalInput").ap()
    outT_d = nc.dram_tensor("outT", [D, S], F32, kind="ExternalOutput").ap()
    send_d = nc.dram_tensor("send", [D, T], F32, kind="Internal").ap()
    recv_d = nc.dram_tensor("recv", [2 * D, T], F32, kind="Internal").ap()
    B_send = Buf("send"); B_recv = Buf("recv")

    from contextlib import ExitStack
    es = ExitStack()
    sb = lambda name, shape, dt: es.enter_context(nc.sbuf_tensor(name, shape, dt))

    xT = sb("xT_s", [128, 8, T], F32); B_x = Buf("xT")
    hT = sb("hT_s", [128, 8, T + L], BF16); B_h = Buf("hT")
    yT = sb("yT_s", [128, 16, T], BF16); B_y = [Buf("y%d" % i) for i in range(16)]
    acc = sb("acc_s", [128, 8, T], F32); B_acc = [Buf("acc%d" % i) for i in range(8)]
    mg = sb("mg_s", [128, 8, T + L], BF16); B_mgc = [Buf("mg%d" % i) for i in range(8)]
    NSLOT = 3
    ring = [sb("ring%d" % i, [128, 8, 1024], BF16) for i in range(NSLOT)]
    B_ring = [Buf("ring%d" % i) for i in range(NSLOT)]
    pvec = sb("pvec_s", [128, PV_COLS], F32); B_pv = Buf("pvec")
    dcst = sb("dcst_s", [128, 40], F32)
    gbn = sb("gbn_s", [4, 1], F32)
    bd = sb("bd_s", [128, PL * 20, 128], BF16)
    wif = sb("wif_s", [128, PL * 2 * 12, 4], BF16)
    gb = sb("gb_s", [4, PL * 2], F32)
    lr2 = sb("lr2_s", [16, PL, 256], BF16)
    cst = sb("cst_s", [128, 896], F32)
    identb = sb("identb_s", [128, 128], BF16)
    ones_b = sb("ones_b_s", [128, 128], BF16)
    B_const = Buf("const")
    ident_f = cst[:, 0:128]
    ones_f = cst[:, 128:256]
    maskT = cst[0:64, 256:320]
    rmask = cst[:, 320:832]
    rowmask = cst[:, 832:834]
    sel = sb("sel_s", [4, 4, 128], F32)
    maskrep = sb("maskrep_s", [64, 4, L], F32)
    onesT = sb("onesT_s", [128, T], F32)
    ones_T = onesT[:, :]
    ones4 = onesT[0:4, :]

    NF = 16
    EG4 = sb("EG4_s", [128, 4, T], F32)
    FB4 = sb("FB4_s", [128, 4, T], F32)
    _fmap = {7: 0, 8: 1, 9: 2, 14: 3}
    Ft = [EG4[:, i - 3, :] if 3 <= i <= 6 else (FB4[:, _fmap[i], :] if i in _fmap else sb("F%d" % i, [128, T], F32)) for i in range(NF)]
    brh = [FB4[:, 0:2, :].bitcast(BF16).rearrange("p a (b c) -> p (a b) c", b=2),
           FB4[:, 2:4, :].bitcast(BF16).rearrange("p a (b c) -> p (a b) c", b=2)]
    BF = [Buf("F%d" % i) for i in range(NF)]
    B_brh = [[Buf("brh0"), BF[7], BF[8]], [Buf("brh1"), BF[9], BF[14]]]
    NB = 8
    BtP = [sb("B%d" % i, [128, T + L], BF16) for i in range(NB)]
    Bt = [t_[:, 0:T] for t_ in BtP]
    BB = [Buf("B%d" % i) for i in range(NB)]
    xpad = sb("xpad_s", [128, T + 3], F32); B_xpad = Buf("xpad")
    xpad2 = sb("xpad2_s", [128, T + 3], F32); B_xpad2 = Buf("xpad2")
    GZ = sb("GZ_s", [128, 4, T], F32); B_gz = [Buf("gz%d" % i) for i in range(4)]
    xm_f = acc[:, 0:4, :]; B_xm = B_acc[0:4]
    xm_b = mg[:, 0:4, 0:T]; B_xmb = B_mgc[0:4]
    mx_b = mg[:, 4:8, 0:T]; B_mxb = B_mgc[4:8]
    mx_bP = mg[:, 4:8, :]
    qkv_b = yT[:, 4:16, :]; B_qkv = B_y[4:16]
    vflat = sb("vflat_s", [128, 4096], BF16); B_vflat = Buf("vflat")
    vtok2 = vflat[:, :].rearrange("p (h j e) -> p h j e", h=2, j=NCH, e=256)
    vtok4 = vflat[:, :].rearrange("p (h j e) -> p h j e", h=4, j=NCH, e=128)
    B_vtok = [B_vflat] * 4
    g4 = [Ft[12 + i][0:4, :] for i in range(4)]; B_g4 = [BF[12 + i] for i in range(4)]
    glr = sb("glr_s", [16, T], BF16); B_glr = Buf("glr")
    ATm = sb("ATm_s", [128, 4, 2, L], BF16); B_ATm = [[Buf("ATm%d_%d" % (i, p)) for p in range(2)] for i in range(4)]
    ktok = sb("ktok_s", [128, 4, 2, 128], BF16); B_ktok = [[Buf("ktok%d_%d" % (i, p)) for p in range(2)] for i in range(4)]
    Sbf = sb("Sbf_s", [128, 4, 256], BF16); B_Sbf = [Buf("Sbf%d" % i) for i in range(4)]
    Stmp = sb("Stmp_s", [128, 4, 128], F32); B_Stmp = [Buf("Stmp%d" % i) for i in range(4)]
    st_conv = sb("st_conv", [128, PL * 2 * 4, 3], F32); B_stconv = Buf("stconv")
    st_lru = sb("st_lru", [128, PL * 4], F32); B_stlru = Buf("stlru")
    st_C = sb("st_C", [128, PL * 4, 256], F32); B_stC = [[Buf("stC%d_%d" % (l, h)) for h in range(4)] for l in range(PL)]
    st_H = sb("st_H", [128, PL * 4, 128], F32); B_stH = [[Buf("stH%d_%d" % (l, h)) for h in range(4)] for l in range(PL)]
    st_G = sb("st_G", [128, PL * 4, 128], F32); B_stG = [[Buf("stG%d_%d" % (l, h)) for h in range(4)] for l in range(PL)]
    osb = acc

    ps = [es.enter_context(nc.psum_tensor("ps%d" % i, [128, 512], F32)) for i in range(8)]
    B_ps = [Buf("ps%d" % i, excl=True) for i in range(8)]
    B_psA = [[Buf("psA%d_%d" % (i, p)) for p in range(2)] for i in range(4)]
    B_psT = [[Buf("psT%d_%d" % (i, p)) for p in range(2)] for i in range(4)]
    pool_banks = [0, 1, 2, 3, 4, 5]
    ps_state = {"next": 0, "held": set(), "live": set()}
    def ps_alloc(keep=False):
        for _ in range(2 * len(pool_banks)):
            b = pool_banks[ps_state["next"] % len(pool_banks)]
            ps_state["next"] += 1
            if b not in ps_state["held"] and b not in ps_state["live"]:
                if keep:
                    ps_state["live"].add(b)
                return b
        raise RuntimeError("out of PSUM banks")
    def ps_free(*bs):
        for b in bs:
            ps_state["live"].discard(b)
    def ps_hold(n):
        out = []
        for _ in range(n):
            b = ps_alloc()
            ps_state["held"].add(b)
            out.append(b)
        return out
    def ps_release(bs):
        for b in bs:
            ps_state["held"].discard(b)

    sem_names = ["pe", "act", "dve", "pool", "sp", "d_setup", "d_setup2", "d_x", "d_out", "d_send", "d_recv", "cc", "d_brh0", "d_brh1"] + ["d_ring%d" % i for i in range(NSLOT)]
    sems = {n: es.enter_context(nc.semaphore("s_" + n)) for n in sem_names}
    block = es.enter_context(nc.Block())
    prog = []

    streams = {"pe": [], "act": [], "dve": [], "pool": [], "sp": []}

    class Rec:
        def __init__(self, en):
            self.en = en
        def __getattr__(self, name):
            en = self.en
            def call(*a, **k):
                h = RecIns()
                streams[en].append((name, a, k, h))
                return h
            return call

    class RecIns:
        def __init__(self):
            self.post = []
        def _wait_ge(self, s, v):
            self.post.append(("_wait_ge", (s, v)))
            return self
        def then_inc(self, s, v):
            self.post.append(("then_inc", (s, v)))
            return self

    engs = {k: Rec(k) for k in streams}
    tk = Trk(nc, engs, sems)

    def act(out, in_, func, reads, writes, bias=None, scale=None):
        kw = {}
        if bias is not None:
            kw["bias"] = bias
        if scale is not None:
            kw["scale"] = scale
        tk.op("act", lambda e: e.activation(out=out, in_=in_, func=func, **kw), reads, writes)

    def tt(en, out, in0, in1, op, reads, writes):
        tk.op(en, lambda e: e.tensor_tensor(out=out, in0=in0, in1=in1, op=op), reads, writes)

    def ts(en, out, in0, s1, s2, op0, op1, reads, writes):
        if s2 is None:
            tk.op(en, lambda e: e.tensor_scalar(out=out, in0=in0, scalar1=s1, scalar2=None, op0=op0), reads, writes)
        else:
            tk.op(en, lambda e: e.tensor_scalar(out=out, in0=in0, scalar1=s1, scalar2=s2, op0=op0, op1=op1), reads, writes)

    def stt(out, in0, scalar, in1, op0, op1, reads, writes):
        tk.op("dve", lambda e: e.scalar_tensor_tensor(out=out, in0=in0, scalar=scalar, in1=in1, op0=op0, op1=op1), reads, writes)

    def cpy(en, out, in_, reads, writes):
        if en == "act":
            tk.op("act", lambda e: e.copy(out=out, in_=in_), reads, writes)
        else:
            tk.op(en, lambda e: e.tensor_copy(out=out, in_=in_), reads, writes)

    def mmg(lst, reads, writes):
        fns = []
        cost = 0.0
        for (o, l_, r_, st, sp) in lst:
            fns.append((lambda o=o, l_=l_, r_=r_, st=st, sp=sp: (lambda e: e.matmul(o, l_, r_, start=st, stop=sp)))())
            n_mov = int(r_.shape[-1])
            cost += max(n_mov, 64) / 1900.0 * (4.0 if r_.dtype == F32 else 1.0) + 0.03
        tk.mm(fns, reads, writes, cost=cost)

    setup_bufs = [B_pv, B_const]
    tk.dma("sp", "d_setup", pvec[:], pvec_d[:, :], [], [B_pv])
    tk.dma("sp", "d_setup", cst[:], cst_d[:, 0:896], [], [B_const])
    tk.dma("sp", "d_setup", gb[:], gb_d[:, :], [], [B_const])
    tk.dma("sp", "d_setup", sel[:], cst_d[0:4, 896:1408].rearrange("p (j m) -> p j m", j=4), [], [B_const])
    tk.dma("sp", "d_setup", maskrep[:], cst_d[0:64, 1408:1664].rearrange("p (h t) -> p h t", h=4), [], [B_const])
    tk.dma("pool", "d_setup2", bd[:], bd_d[:, :, :], [], [B_const])
    tk.dma("pool", "d_setup2", wif[:], wif_d[:, :, :], [], [B_const])
    tk.dma("pool", "d_setup2", lr2[:], lr2_d[:, :, :], [], [B_const])
    tk.dma("pool", "d_setup2", identb[:], cst_d[:, 0:128], [], [B_const])
    tk.dma("pool", "d_setup2", ones_b[:], cst_d[:, 128:256], [], [B_const])
    for en_ in ("pe", "act", "dve", "pool"):
        for sk_ in ("d_setup", "d_setup2"):
            engs[en_].wait_ge(sems[sk_], tk.dmaval[sk_])
            tk.waited[en_][("d", sk_)] = tk.dmaval[sk_]
    for b_ in (B_pv, B_const):
        b_.w = None

    B_dc = Buf("dcst")
    DC_S1, DC_S2, DC_HBA, DC_HBX, DC_C0, DC_C1, DC_HSK, DC_NB2, DC_LB = 0, 4, 8, 12, 16, 20, 24, 28, 30
    l = 0
    lam = pvec[:, pv(l, "lru_lam"):pv(l, "lru_lam") + 4]
    act(dcst[:, 0:4], lam, AF.Exp, [B_pv], [B_dc], scale=-1.0)
    act(dcst[:, 0:4], dcst[:, 0:4], AF.Ln, [B_dc], [B_dc], bias=1.0)
    ts("dve", dcst[:, 4:8], dcst[:, 0:4], -8.0, None, ALU.mult, None, [B_dc], [B_dc])
    ts("dve", dcst[:, 0:4], dcst[:, 0:4], -4.0, None, ALU.mult, None, [B_dc], [B_dc])
    ts("dve", dcst[:, 8:12], pvec[:, pv(l, "lru_ba"):pv(l, "lru_ba") + 4], 0.5, None, ALU.mult, None, [B_pv], [B_dc])
    ts("dve", dcst[:, 12:16], pvec[:, pv(l, "lru_bx"):pv(l, "lru_bx") + 4], 0.5, None, ALU.mult, None, [B_pv], [B_dc])
    ts("dve", dcst[:, 24:28], pvec[:, pv(l, "m_skip"):pv(l, "m_skip") + 4], 0.5, None, ALU.mult, None, [B_pv], [B_dc])
    ts("dve", dcst[:, 28:30], pvec[:, pv(l, "g_b2"):pv(l, "g_b2") + 2], -1.0, None, ALU.mult, None, [B_pv], [B_dc])
    ts("dve", gbn[:, 0:1], gb[:, 1:2], -1.0, None, ALU.mult, None, [B_const], [B_dc])
    l0 = pvec[:, pv(l, "h_lbl"):pv(l, "h_lbl") + 4]
    l1 = pvec[:, pv(l, "h_lbl1"):pv(l, "h_lbl1") + 4]
    tt("dve", dcst[:, 30:34], l1, l0, ALU.subtract, [B_pv], [B_dc])
    act(dcst[:, 30:34], dcst[:, 30:34], AF.Tanh, [B_dc], [B_dc], scale=0.5)
    ts("dve", dcst[:, 30:34], dcst[:, 30:34], 0.5, 0.5, ALU.mult, ALU.add, [B_dc], [B_dc])
    ts("dve", dcst[:, 30:34], dcst[:, 30:34], pvec[:, pv(l, "flag_b"):pv(l, "flag_b") + 1], None, ALU.mult, None, [B_dc, B_pv], [B_dc])
    ts("dve", dcst[:, 20:24], dcst[:, 30:34], -0.5, 0.5, ALU.mult, ALU.add, [B_dc], [B_dc])
    tt("dve", dcst[:, 16:20], dcst[:, 30:34], dcst[:, 20:24], ALU.add, [B_dc], [B_dc])
    tk.op("dve", lambda e: e.memset(st_conv[:], 0.0), [], [B_stconv])
    tk.op("dve", lambda e: e.memset(st_lru[:], 0.0), [], [B_stlru])
    allC = [b for l in B_stC for b in l]; allH = [b for l in B_stH for b in l]; allG = [b for l in B_stG for b in l]
    tk.op("dve", lambda e: e.memset(st_C[:], 0.0), [], allC)
    tk.op("dve", lambda e: e.memset(st_H[:], 0.0), [], allH)
    tk.op("dve", lambda e: e.memset(st_G[:], 0.0), [], allG)
    tk.op("dve", lambda e: e.memset(onesT[:], 1.0), [], [B_const])
    tk.op("dve", lambda e: e.memset(vflat[:], 0.0), [], [B_vflat])
    tk.op("dve", lambda e: e.memset(ATm[:], 0.0), [], [b_ for l_ in B_ATm for b_ in l_])
    tk.op("dve", lambda e: e.memset(ktok[:], 0.0), [], [b_ for l_ in B_ktok for b_ in l_])
    tk.op("dve", lambda e: e.memset(hT[:], 0.0), [], [B_h])
    tk.op("dve", lambda e: e.memset(mg[:], 0.0), [], B_mgc)
    for i_ in range(NB):
        tk.op("dve", lambda e, i_=i_: e.memset(BtP[i_][:], 0.0), [], [BB[i_]])

    def wgroups(l):
        g = []
        g.append(("in", C_LRUX, 1024))
        g.append(("in", C_MX, 1024))
        g.append(("in", C_MZ, 512))
        g.append(("in", C_HQ, 1024))
        g.append(("in", C_HI, 1024))
        g.append(("in", C_GQ, 1024))
        g.append(("in", C_GLR, 528))
        for n in range(4):
            g.append(("in", C_MERGE + n * 1024, 1024))
        g.append(("out", 0, 1024))
        return g
    wsched = []
    for t in range(NT + 1):
        for l in range(n_layers):
            for gi, g in enumerate(wgroups(l)):
                wsched.append((l, g))
    wstate = {"issued": 0, "consumed": 0, "done": 0}
    def pump():
        while wstate["issued"] < len(wsched) and wstate["issued"] - wstate["done"] < NSLOT:
            i = wstate["issued"]
            l, (kind, a, ncols) = wsched[i]
            slot = i % NSLOT
            if kind == "in":
                src = w_in_d[l].rearrange("(kc p) c -> p kc c", p=128)[:, :, a:a + ncols]
                tk.dma("pool", "d_ring%d" % slot, ring[slot][:, :, 0:ncols], src, [], [B_ring[slot]])
            elif kind == "br":
                src = w_br_d[l, a].rearrange("(kc p) c -> p kc c", p=128)
                tk.dma("pool", "d_ring%d" % slot, ring[slot][:, 0:4, :], src, [], [B_ring[slot]])
            else:
                src = w_out_d[l].rearrange("(kc p) c -> p kc c", p=128)
                tk.dma("pool", "d_ring%d" % slot, ring[slot][:, :, :], src, [], [B_ring[slot]])
            wstate["issued"] += 1
    def issue_brh(l, n, half):
        src = w_br_d[l, n].rearrange("(kc p) c -> p kc c", p=128)[:, :, half * 512:(half + 1) * 512]
        tk.dma("pool", "d_brh%d" % half, brh[half], src, [], B_brh[half])

    def next_weight():
        i = wstate["consumed"]
        assert i < wstate["issued"], "weight group not issued (ring too small for live groups)"
        wstate["consumed"] += 1
        return ring[i % NSLOT], B_ring[i % NSLOT]
    def weights_done(n, defer=False):
        wstate["done"] += n
        if not defer:
            pump()

    def proj_fm(slot, B_slot, off, ncols, bank, extra_reads=()):
        lst = [(ps[bank][0:ncols, :], slot[:, kc, off:off + ncols], hT[:, kc, 0:T], kc == 0, kc == 7) for kc in range(8)]
        mmg(lst, [B_slot, B_h] + list(extra_reads), [B_ps[bank]])

    def bfv(i):
        return Ft[i].bitcast(BF16)[:, 0:T]

    def stats_bcast(src_list, src_bufs, bank, bf=False):
        n = len(src_list)
        lst = [(ps[bank][:, :], ones_b[:, :] if bf else ones_f, src_list[i], i == 0, i == n - 1) for i in range(n)]
        mmg(lst, [B_const] + list(src_bufs), [B_ps[bank]])

    def rstd_from(bank, nfeat, out_f, out_buf):
        act(out_f, ps[bank][:, :], AF.Ln, [B_ps[bank]], [out_buf], bias=EPS, scale=1.0 / nfeat)
        act(out_f, out_f, AF.Exp, [out_buf], [out_buf], scale=-0.5)

    def rmsnorm_to_h(gcol0):
        sq = Ft[0]
        b = ps_alloc()
        for kc in range(8):
            act(bfv(kc % 2), xT[:, kc, :], AF.Square, [B_x], [BF[kc % 2]])
            tk.mm([lambda e, kc=kc: e.matmul(ps[b][:, :], ones_b[:, :], bfv(kc % 2), start=(kc == 0), stop=(kc == 7))],
                  [B_const, BF[kc % 2]], [B_ps[b]], cost=0.3)
        rstd_from(b, D, Ft[2][:, :], BF[2])
        for kc in range(8):
            stt(hT[:, kc, 0:T], xT[:, kc, :], pvec[:, gcol0 + kc:gcol0 + kc + 1], Ft[2][:, :], ALU.mult, ALU.mult,
                [B_x, B_pv, BF[2]], [B_h])

    def conv_fm(bank, l, br, c, cw0, cb, out_f, out_buf, xp=None, Bxp=None):
        if xp is None:
            xp, Bxp = xpad, B_xpad
        si = (l * 2 + br) * 4 + c
        cpy("dve", xp[:, 0:3], st_conv[:, si, :], [B_stconv], [Bxp])
        cpy("act", xp[:, 3:3 + T], ps[bank][:, :], [B_ps[bank]], [Bxp])
        cpy("act", st_conv[:, si, :], xp[:, T:T + 3], [Bxp], [B_stconv])
        ts("dve", out_f, xp[:, 0:T], pvec[:, cw0:cw0 + 1], pvec[:, cb:cb + 1], ALU.mult, ALU.add,
           [Bxp, B_pv], [out_buf])
        for j in range(1, 4):
            stt(out_f, xp[:, j:j + T], pvec[:, cw0 + j:cw0 + j + 1], out_f, ALU.mult, ALU.add,
                [Bxp, B_pv, out_buf], [out_buf])

    def run_interleaved(gens):
        live = [[g, 0.0, i] for i, g in enumerate(gens)]
        while live:
            item = min(live, key=lambda it: (it[1], it[2]))
            tk.last_fin = None
            try:
                next(item[0])
            except StopIteration:
                live.remove(item)
                continue
            if tk.last_fin is not None:
                item[1] = tk.last_fin

    B_ATm1 = Buf("ATm"); B_ktok1 = Buf("ktok"); B_Sb1 = Buf("Sbf")

    def chunk_attn(heads, vtok, qT, kT, kP, B_q, B_k, vcols, state_all, state_f, B_state, local, decb, dec, B_dec, obanks, dbanks=None, ubank=None):
        nh = len(heads)
        vs = {h: hi for hi, h in enumerate(heads)}
        Bst = [B_state[h] for h in heads]
        Bq = [B_q[h] for h in heads]
        Bk = list({id(B_k[h]): B_k[h] for h in heads}.values())
        Bd = list({id(B_dec[h]): B_dec[h] for h in heads}.values())
        Sb = Sbf[:, 0:nh, 0:vcols]
        ureg_all = ps[ubank][:, 0:nh * vcols].rearrange("p (h e) -> p h e", h=nh)
        if not local:
            fns = []
            for hi, h in enumerate(heads):
                fns.append(lambda e, hi=hi, h=h: e.matmul(ps[ubank][:, hi * vcols:(hi + 1) * vcols], ident_f, state_f[h],
                                                       start=(hi == 0), stop=(hi == nh - 1), skip_group_check=True))
            tk.mm(fns, [B_const] + Bst, [B_ps[ubank]])
        cpy("act", Sb, state_all, Bst, [B_Sb1])

        def stage1(j):
            cs = slice(j * L, (j + 1) * L)
            fns = []
            for hi, h in enumerate(heads):
                fns.append(lambda e, hi=hi, h=h: e.matmul(ps[6][:, hi * 64:(hi + 1) * 64], kP[h][:, j * L:j * L + 128], qT[h][:, cs], start=True, stop=True))
            tk.mm(fns, Bk + Bq, [B_ps[6]])
            fns = []
            for hi, h in enumerate(heads):
                fns.append(lambda e, hi=hi, h=h: e.matmul(ps[7][:, hi * 128:(hi + 1) * 128], kP[h][:, j * L:j * L + 128], identb[:, :], start=True, stop=True))
            tk.mm(fns, Bk + [B_const], [B_ps[7]])
            tt("dve", ATm[0:64, 0:nh, 0, :], ps[6][0:64, 0:nh * 64].rearrange("p (h t) -> p h t", h=nh), maskrep[:, 0:nh, :], ALU.mult,
               [B_ps[6], B_const], [B_ATm1])
            cpy("act", ktok[0:64, 0:nh, 0, :], ps[7][0:64, 0:nh * 128].rearrange("p (h e) -> p h e", h=nh), [B_ps[7]], [B_ktok1])

        def stage2_pe(j):
            cs = slice(j * L, (j + 1) * L)
            for hi, h in enumerate(heads):
                lst = [(ps[obanks[hi]][:, cs], vtok[:, hi, j, 0:128], ATm[:, hi, 0, :], True, False),
                       (ps[obanks[hi]][:, cs], Sbf[:, hi, 0:128], qT[h][:, cs], False, True)]
                mmg(lst, [B_vflat, B_ATm1, B_Sb1, B_q[h]], [B_ps[obanks[hi]]])
                if vcols == 256:
                    lst = [(ps[dbanks[hi]][:, cs], vtok[:, hi, j, 128:256], ATm[:, hi, 0, :], True, False),
                           (ps[dbanks[hi]][:, cs], Sbf[:, hi, 128:256], qT[h][:, cs], False, True)]
                    mmg(lst, [B_vflat, B_ATm1, B_Sb1, B_q[h]], [B_ps[dbanks[hi]]])
            fns = []
            for hi, h in enumerate(heads):
                if local:
                    fns.append(lambda e, hi=hi: e.matmul(ps[ubank][:, hi * vcols:(hi + 1) * vcols], ktok[:, hi, 0, :], vtok[:, hi, j, 0:vcols],
                                                        start=True, stop=True, skip_group_check=True))
                else:
                    fns.append(lambda e, hi=hi: e.matmul(ps[ubank][:, hi * vcols:(hi + 1) * vcols], ktok[:, hi, 0, :], vtok[:, hi, j, 0:vcols],
                                                        start=False, stop=(j == NCH - 1), skip_group_check=True))
            tk.mm(fns, [B_ktok1, B_vflat], [B_ps[ubank]])

        def stage2_state(j):
            if local:
                tt("dve", state_all, state_all, ureg_all, ALU.add, Bst + [B_ps[ubank]], Bst)
                tt("dve", state_all, state_all, decb(j), ALU.mult, Bst + Bd, Bst)
                if j < NCH - 1:
                    cpy("act", Sb, state_all, Bst, [B_Sb1])
            else:
                if j < NCH - 1:
                    cpy("act", Sb, ureg_all, [B_ps[ubank]], [B_Sb1])
                else:
                    for hi, h in enumerate(heads):
                        dcol = dec[h][:, T - 1:T]
                        ts("dve", state_f[h], ps[ubank][:, hi * vcols:(hi + 1) * vcols], dcol, None, ALU.mult, None,
                           [B_ps[ubank], B_dec[h]], [B_state[h]])

        stage1(0)
        for j in range(NCH):
            stage2_pe(j)
            if j + 1 < NCH:
                stage1(j + 1)
            stage2_state(j)

    pump()
    keepc = pvec[:, pv(0, "keep"):pv(0, "keep") + 1]
    flagb = pvec[:, pv(0, "flag_b"):pv(0, "flag_b") + 1]
    for step in range(NT + 1):
        t = min(step, NT - 1)
        tsl = slice(t * T, (t + 1) * T)
        tk.dma("sp", "d_x", xT[:, :, :], xT_d.rearrange("(kc p) s -> p kc s", p=128)[:, :, tsl], [], [B_x])
        if step >= 1:
            tk.dma("sp", "d_recv", acc[:, :, :], recv_d[0:D, :].rearrange("(kc p) s -> p kc s", p=128), [B_recv], B_acc)
            for kc in range(8):
                stt(xT[:, kc, :], acc[:, kc, :], flagb, xT[:, kc, :], ALU.mult, ALU.add, [B_acc[kc], B_pv, B_x], [B_x])
        if step == 1:
            ts("dve", st_conv[:, :, :], st_conv[:, :, :], keepc, None, ALU.mult, None, [B_stconv, B_pv], [B_stconv])
            ts("dve", st_lru[:, :], st_lru[:, :], keepc, None, ALU.mult, None, [B_stlru, B_pv], [B_stlru])
            for h in range(4):
                ts("dve", st_C[:, h, :], st_C[:, h, :], keepc, None, ALU.mult, None, [B_stC[0][h], B_pv], [B_stC[0][h]])
                ts("dve", st_H[:, h, :], st_H[:, h, :], keepc, None, ALU.mult, None, [B_stH[0][h], B_pv], [B_stH[0][h]])
                ts("dve", st_G[:, h, :], st_G[:, h, :], keepc, None, ALU.mult, None, [B_stG[0][h], B_pv], [B_stG[0][h]])

        for l in range(n_layers):
            rmsnorm_to_h(pv(l, "norm_g"))

            slotA, B_slotA = next_weight()
            slot, B_slot = next_weight()
            slot2, B_slot2 = next_weight()
            def gen_lru(slot=slotA, B_slot=B_slotA):
                for c in range(4):
                    b1 = ps_alloc(True)
                    proj_fm(slot, B_slot, c * 128, 128, b1)
                    yield
                    xa, Bxa = Ft[0], BF[0]
                    conv_fm(b1, l, 0, c, pv(l, "lru_cw", c * 4), pv(l, "lru_cb", c), xa[:, :], Bxa)
                    ps_free(b1)
                    yield
                    cpy("act", Bt[0][:, :], xa[:, :], [Bxa], [BB[0]])
                    yield
                    b2 = ps_alloc(True); b3 = ps_alloc(True)
                    mmg([(ps[b2][:, :], bd[:, l * 20 + c, :], Bt[0][:, :], True, True)], [B_const, BB[0]], [B_ps[b2]])
                    yield
                    mmg([(ps[b3][:, :], bd[:, l * 20 + 4 + c, :], Bt[0][:, :], True, True)], [B_const, BB[0]], [B_ps[b3]])
                    yield
                    r_, Br = Ft[1], BF[1]
                    i_, Bi = Ft[2], BF[2]
                    act(r_[:, :], ps[b2][:, :], AF.Tanh, [B_ps[b2], B_dc], [Br], bias=dcst[:, DC_HBA + c:DC_HBA + c + 1], scale=0.5)
                    ps_free(b2)
                    yield
                    act(i_[:, :], ps[b3][:, :], AF.Tanh, [B_ps[b3], B_dc], [Bi], bias=dcst[:, DC_HBX + c:DC_HBX + c + 1], scale=0.5)
                    ps_free(b3)
                    yield
                    a_, Ba = Ft[3], BF[3]
                    m_, Bm = Ft[4], BF[4]
                    act(a_[:, :], r_[:, :], AF.Exp, [Br, B_dc], [Ba], scale=dcst[:, DC_S1 + c:DC_S1 + c + 1], bias=dcst[:, DC_S1 + c:DC_S1 + c + 1])
                    yield
                    act(m_[:, :], r_[:, :], AF.Exp, [Br, B_dc], [Bm], scale=dcst[:, DC_S2 + c:DC_S2 + c + 1], bias=dcst[:, DC_S2 + c:DC_S2 + c + 1])
                    yield
                    act(m_[:, :], m_[:, :], AF.Ln, [Bm], [Bm], bias=1.0, scale=-1.0)
                    yield
                    act(m_[:, :], m_[:, :], AF.Exp, [Bm], [Bm], scale=0.5)
                    yield
                    stt(i_[:, :], i_[:, :], 1.0, xa[:, :], ALU.add, ALU.mult, [Bi, Bxa], [Bi])
                    yield
                    stt(i_[:, :], i_[:, :], 0.5, m_[:, :], ALU.mult, ALU.mult, [Bi, Bm], [Bi])
                    yield
                    hs, Bhs = Ft[5], BF[5]
                    sc = l * 4 + c
                    tk.op("dve", lambda e, sc=sc: e.tensor_tensor_scan(out=hs[:, :], data0=a_[:, :], data1=i_[:, :],
                                                                       initial=st_lru[:, sc:sc + 1], op0=ALU.mult, op1=ALU.add),
                          [Ba, Bi, B_stlru], [Bhs])
                    cpy("act", st_lru[:, sc:sc + 1], hs[:, T - 1:T], [Bhs], [B_stlru])
                    yield
                    b4 = ps_alloc(True)
                    proj_fm(slot, B_slot, 512 + c * 128, 128, b4)
                    yield
                    sz, Bsz = Ft[6], BF[6]
                    act(sz[:, :], ps[b4][:, :], AF.Tanh, [B_ps[b4]], [Bsz], scale=0.5)
                    yield
                    stt(sz[:, :], sz[:, :], 1.0, ps[b4][:, :], ALU.add, ALU.mult, [Bsz, B_ps[b4]], [Bsz])
                    ps_free(b4)
                    yield
                    stt(yT[:, 0 + c, :], sz[:, :], 0.5, hs[:, :], ALU.mult, ALU.mult, [Bhs, Bsz], [B_y[0 + c]])
                    yield

            def gen_mprep():
                for h in range(4):
                    b1 = ps_alloc(True)
                    proj_fm(slot, B_slot, h * 128, 128, b1)
                    yield
                    cpy("dve", mx_b[:, h, :], ps[b1][:, :], [B_ps[b1]], [B_mxb[h]])
                    yield
                    xc, Bxc = Ft[7], BF[7]
                    conv_fm(b1, l, 1, h, pv(l, "m_cw", h * 4), pv(l, "m_cb", h), xc[:, :], Bxc, xp=xpad2, Bxp=B_xpad2)
                    ps_free(b1)
                    yield
                    act(xm_f[:, h, :], xc[:, :], AF.Tanh, [Bxc], [B_xm[h]], scale=0.5)
                    yield
                    stt(xm_f[:, h, :], xm_f[:, h, :], 1.0, xc[:, :], ALU.add, ALU.mult, [B_xm[h], Bxc], [B_xm[h]])
                    yield
                    act(xm_b[:, h, :], xm_f[:, h, :], AF.Identity, [B_xm[h]], [B_xmb[h]], scale=0.5)
                    yield
                    for qi, (srcb, Bsrc) in enumerate([(xm_b, B_xmb), (xm_b, B_xmb), (mx_b, B_mxb)]):
                        b2 = ps_alloc(True)
                        mmg([(ps[b2][:, :], bd[:, l * 20 + 8 + qi * 4 + h, :], srcb[:, h, :], True, True)], [B_const, Bsrc[h]], [B_ps[b2]])
                        cpy("act" if qi != 1 else "dve", qkv_b[:, qi * 4 + h, :], ps[b2][:, :], [B_ps[b2]], [B_qkv[qi * 4 + h]])
                        ps_free(b2)
                        yield

            run_interleaved([gen_lru(), gen_mprep()])
            weights_done(1)

            bi_ = ps_alloc(); bf_ = ps_alloc()
            for gi, bank in ((0, bi_), (1, bf_)):
                lst = [(ps[bank][0:4, :], wif[:, (l * 2 + gi) * 12 + ci, :], qkv_b[:, ci, :], ci == 0, ci == 11) for ci in range(12)]
                mmg(lst, [B_const] + B_qkv, [B_ps[bank]])
            li, lf, G, eG, wk = g4[0], g4[1], g4[2], g4[3], g4[0]
            act(li[:, :], ps[bi_][0:4, :], AF.Identity, [B_ps[bi_], B_const], [B_g4[0]], bias=gb[:, l * 2:l * 2 + 1])
            act(lf[:, :], ps[bf_][0:4, :], AF.Exp, [B_ps[bf_], B_dc], [B_g4[1]], bias=gbn[:, 0:1], scale=-1.0)
            act(lf[:, :], lf[:, :], AF.Ln, [B_g4[1]], [B_g4[1]], bias=1.0)
            tk.op("dve", lambda e: e.tensor_tensor_scan(out=G[:, :], data0=ones4, data1=lf[:, :], initial=0.0,
                                                        op0=ALU.mult, op1=ALU.add), [B_g4[1], B_const], [B_g4[2]])
            act(eG[:, :], G[:, :], AF.Exp, [B_g4[2]], [B_g4[3]], scale=-1.0)
            tt("dve", wk[:, :], li[:, :], G[:, :], ALU.add, [B_g4[0], B_g4[2]], [B_g4[0]])
            act(wk[:, :], wk[:, :], AF.Exp, [B_g4[0]], [B_g4[0]])
            GO = acc[:, 4:8, :]; B_go = B_acc[4:8]
            for h in range(4):
                b9 = ps_alloc()
                proj_fm(slot, B_slot, 512 + h * 128, 128, b9)
                act(GO[:, h, :], ps[b9][:, :], AF.Tanh, [B_ps[b9]], [B_go[h]], scale=0.5)
                b12 = ps_alloc()
                proj_fm(slot2, B_slot2, h * 128, 128, b12)
                act(GZ[:, h, :], ps[b12][:, :], AF.Tanh, [B_ps[b12]], [B_gz[h]], scale=0.5)
                stt(GZ[:, h, :], GZ[:, h, :], 1.0, ps[b12][:, :], ALU.add, ALU.mult, [B_gz[h], B_ps[b12]], [B_gz[h]])
            weights_done(2)
            for pair in range(2):
                hp = [pair * 2, pair * 2 + 1]
                qT_ = {}; kT_ = {}; kP_ = {}; Bq_ = {}; Bk_ = {}; dec_ = {}; Bdec_ = {}
                tk.op("pool", lambda e: e.memset(vtok2[0:64, :, :, 128:256], 1.0), [], [B_vflat])
                for hi, h in enumerate(hp):
                    for half in range(2):
                        b3 = ps_alloc()
                        fns = []
                        for jj in range(4):
                            j = half * 4 + jj
                            fns.append(lambda e, j=j, jj=jj, b3=b3, h=h: e.matmul(ps[b3][:, jj * 128:(jj + 1) * 128], mx_bP[:, h, j * L:j * L + 128],
                                                                                 bd[:, l * 20 + 16 + h, :], start=True, stop=True))
                        tk.mm(fns, [B_mxb[h], B_const], [B_ps[b3]])
                        cpy("act", vtok2[0:64, hi, half * 4:half * 4 + 4, 0:128], ps[b3][0:64, :].rearrange("p (j e) -> p j e", j=4),
                            [B_ps[b3]], [B_vflat])
                    b5 = ps_alloc(); b6 = ps_alloc()
                    mmg([(ps[b5][:, :], sel[:, h, :], eG[:, :], True, True)], [B_const, B_g4[3]], [B_ps[b5]])
                    mmg([(ps[b6][:, :], sel[:, h, :], wk[:, :], True, True)], [B_const, B_g4[0]], [B_ps[b6]])
                    eGb, BeGb = Ft[6 + hi], BF[6 + hi]
                    wkb, Bwkb = Ft[8 + hi], BF[8 + hi]
                    cpy("act", eGb[:, :], ps[b5][:, :], [B_ps[b5]], [BeGb])
                    cpy("act", wkb[:, :], ps[b6][:, :], [B_ps[b6]], [Bwkb])
                    b7 = ps_alloc(); b8 = ps_alloc()
                    mmg([(ps[b7][:, :], bd[:, l * 20 + 8 + h, :], xm_b[:, h, :], True, True)], [B_const, B_xmb[h]], [B_ps[b7]])
                    mmg([(ps[b8][:, :], bd[:, l * 20 + 12 + h, :], xm_b[:, h, :], True, True)], [B_const, B_xmb[h]], [B_ps[b8]])
                    stt(Bt[1 + hi][:, :], ps[b7][:, :], 128.0 ** -0.5, eGb[:, :], ALU.mult, ALU.mult, [B_ps[b7], BeGb], [BB[1 + hi]])
                    tt("dve", Bt[3 + hi][:, :], ps[b8][:, :], wkb[:, :], ALU.mult, [B_ps[b8], Bwkb], [BB[3 + hi]])
                    qT_[h] = Bt[1 + hi]; kT_[h] = Bt[3 + hi]; kP_[h] = BtP[3 + hi]; Bq_[h] = BB[1 + hi]; Bk_[h] = BB[3 + hi]
                    dec_[h] = eGb; Bdec_[h] = BeGb
                held = ps_hold(5)
                ob = held[0:2]; db = held[2:4]; ub = held[4]
                sfl = {h: st_C[:, l * 4 + h, :] for h in hp}
                chunk_attn(hp, vtok2, qT_, kT_, kP_, Bq_, Bk_, 256, st_C[:, l * 4 + pair * 2:l * 4 + pair * 2 + 2, :], sfl, {h: B_stC[l][h] for h in hp}, False, None, dec_, Bdec_, ob, db, ub)
                for hi, h in enumerate(hp):
                    dn, Bdn = Ft[hi], BF[hi]
                    ts("dve", dn[:, :], ps[db[hi]][:, :], -1.0, 1.0, ALU.mult, ALU.max, [B_ps[db[hi]]], [Bdn])
                    tt("dve", dn[:, :], dn[:, :], ps[db[hi]][:, :], ALU.max, [Bdn, B_ps[db[hi]]], [Bdn])
                for hi, h in enumerate(hp):
                    act(Ft[hi][:, :], Ft[hi][:, :], AF.Ln, [BF[hi]], [BF[hi]])
                for hi, h in enumerate(hp):
                    act(Ft[hi][:, :], Ft[hi][:, :], AF.Exp, [BF[hi]], [BF[hi]], scale=-1.0)
                for hi, h in enumerate(hp):
                    stt(Ft[10 + hi][:, :], ps[ob[hi]][:, :], 0.5, Ft[hi][:, :], ALU.mult, ALU.mult, [B_ps[ob[hi]], BF[hi]], [BF[10 + hi]])
                ps_release(held)

                def gen_post(hi, h):
                    hm, Bhm = Ft[10 + hi], BF[10 + hi]
                    base = 2 + 4 * hi
                    stt(hm[:, :], GO[:, h, :], 1.0, hm[:, :], ALU.add, ALU.mult, [Bhm, B_go[h]], [Bhm])
                    yield
                    b10 = ps_alloc(True)
                    stats_bcast([hm[:, :]], [Bhm], b10)
                    yield
                    xcn, Bxcn = Ft[base], BF[base]
                    stt(xcn[:, :], ps[b10][:, :], -1.0 / 128.0, hm[:, :], ALU.mult, ALU.add, [B_ps[b10], Bhm], [Bxcn])
                    ps_free(b10)
                    yield
                    sq, Bsq = Ft[base + 1], BF[base + 1]
                    act(sq.bitcast(BF16)[:, 0:T], xcn[:, :], AF.Square, [Bxcn], [Bsq])
                    yield
                    b11 = ps_alloc(True)
                    stats_bcast([sq.bitcast(BF16)[:, 0:T]], [Bsq], b11, bf=True)
                    yield
                    rs, Brs = Ft[base + 2], BF[base + 2]
                    act(rs[:, :], ps[b11][:, :], AF.Ln, [B_ps[b11]], [Brs], bias=EPS, scale=1.0 / 128)
                    ps_free(b11)
                    yield
                    act(rs[:, :], rs[:, :], AF.Exp, [Brs], [Brs], scale=-0.5)
                    yield
                    tt("dve", xcn[:, :], xcn[:, :], rs[:, :], ALU.mult, [Bxcn, Brs], [Bxcn])
                    yield
                    sk, Bsk = Ft[base + 3], BF[base + 3]
                    act(sk[:, :], xm_f[:, h, :], AF.Identity, [B_xm[h], B_dc], [Bsk], scale=dcst[:, DC_HSK + h:DC_HSK + h + 1])
                    yield
                    stt(xcn[:, :], xcn[:, :], pvec[:, pv(l, "m_nw", h):pv(l, "m_nw", h) + 1], sk[:, :], ALU.mult, ALU.add,
                        [Bxcn, B_pv, Bsk], [Bxcn])
                    yield
                    stt(yT[:, 4 + h, :], GZ[:, h, :], 0.5, xcn[:, :], ALU.mult, ALU.mult, [Bxcn, B_gz[h]], [B_y[4 + h]])
                    yield
                run_interleaved([gen_post(hi, h) for hi, h in enumerate(hp)])

            slot, B_slot = next_weight()
            slot2, B_slot2 = next_weight()
            qT_ = {}; kT_ = {}; kP_ = {}; Bq_ = {}; Bk_ = {}; dec_ = {}; Bdec_ = {}
            def gen_hprep(heads_, tset):
                F_f, F_lg, F_G, F_en, F_qs, F_kk = tset
                for h in heads_:
                    bfk = ps_alloc(True)
                    proj_fm(slot, B_slot, 512 + h * 128, 128, bfk)
                    yield
                    f_, Bf_ = Ft[F_f], BF[F_f]
                    act(f_[:, :], ps[bfk][:, :], AF.Tanh, [B_ps[bfk]], [Bf_], scale=0.5)
                    ps_free(bfk)
                    yield
                    ts("dve", f_[:, :], f_[:, :], dcst[:, DC_C1 + h:DC_C1 + h + 1], dcst[:, DC_C0 + h:DC_C0 + h + 1],
                       ALU.mult, ALU.add, [Bf_, B_dc], [Bf_])
                    yield
                    lg, Blg = Ft[F_lg], BF[F_lg]
                    act(lg[:, :], f_[:, :], AF.Ln, [Bf_], [Blg])
                    yield
                    Gc, BGc = Ft[F_G], BF[F_G]
                    tk.op("dve", lambda e: e.tensor_tensor_scan(out=Gc[:, :], data0=rmask, data1=lg[:, :], initial=0.0, op0=ALU.mult, op1=ALU.add),
                          [Blg, B_const], [BGc])
                    yield
                    eGc, BeGc = Ft[3 + h], BF[3 + h]
                    act(eGc[:, :], Gc[:, :], AF.Exp, [BGc], [BeGc])
                    yield
                    enG, BenG = Ft[F_en], BF[F_en]
                    act(enG[:, :], Gc[:, :], AF.Exp, [BGc], [BenG], scale=-1.0)
                    yield
                    bq = ps_alloc(True)
                    proj_fm(slot, B_slot, h * 128, 128, bq)
                    yield
                    qs, Bqs = Ft[F_qs], BF[F_qs]
                    act(qs[:, :], ps[bq][:, :], AF.Tanh, [B_ps[bq]], [Bqs], scale=0.5)
                    yield
                    stt(qs[:, :], qs[:, :], 1.0, ps[bq][:, :], ALU.add, ALU.mult, [Bqs, B_ps[bq]], [Bqs])
                    ps_free(bq)
                    yield
                    stt(Bt[h][:, :], qs[:, :], 0.5 * 128.0 ** -0.5, eGc[:, :], ALU.mult, ALU.mult, [Bqs, BeGc], [BB[h]])
                    yield
                    kk, Bkk = Ft[F_kk], BF[F_kk]
                    ts("dve", kk[:, :], f_[:, :], -1.0, 1.0, ALU.mult, ALU.add, [Bf_], [Bkk])
                    yield
                    tt("dve", Bt[4 + h][:, :], kk[:, :], enG[:, :], ALU.mult, [Bkk, BenG], [BB[4 + h]])
                    yield
                    qT_[h] = Bt[h]; kT_[h] = Bt[4 + h]; kP_[h] = BtP[4 + h]; Bq_[h] = BB[h]; Bk_[h] = BB[4 + h]
                    dec_[h] = eGc; Bdec_[h] = BeGc

            def gen_hvz():
                for h in range(4):
                    b12 = ps_alloc(True)
                    proj_fm(slot2, B_slot2, 512 + h * 128, 128, b12)
                    yield
                    act(GZ[:, h, :], ps[b12][:, :], AF.Tanh, [B_ps[b12]], [B_gz[h]], scale=0.5)
                    yield
                    stt(GZ[:, h, :], GZ[:, h, :], 1.0, ps[b12][:, :], ALU.add, ALU.mult, [B_gz[h], B_ps[b12]], [B_gz[h]])
                    ps_free(b12)
                    yield
                for j in range(NCH):
                    b3 = ps_alloc(True)
                    lst = [(ps[b3][:, :], hT[:, kc, j * L:j * L + 128], slot2[:, kc, 0:512], kc == 0, kc == 7) for kc in range(8)]
                    mmg(lst, [B_h, B_slot2], [B_ps[b3]])
                    yield
                    cpy("act", vtok4[0:64, :, j, :], ps[b3][0:64, :].rearrange("p (h e) -> p h e", h=4), [B_ps[b3]], [B_vflat])
                    ps_free(b3)
                    yield

            run_interleaved([gen_hprep([0, 2], (0, 1, 2, 7, 8, 9)), gen_hprep([1, 3], (10, 11, 12, 13, 14, 15)), gen_hvz()])
            weights_done(2)
            held = ps_hold(5)
            ob = held[0:4]; ub = held[4]
            sfl = {h: st_H[:, l * 4 + h, :] for h in range(4)}
            chunk_attn([0, 1, 2, 3], vtok4, qT_, kT_, kP_, Bq_, Bk_, 128, st_H[:, l * 4:l * 4 + 4, :], sfl, {h: B_stH[l][h] for h in range(4)}, True,
                       lambda j: EG4[:, :, j * L + L - 1:j * L + L].to_broadcast([128, 4, 128]), dec_, Bdec_, ob, None, ub)
            for h in range(4):
                cpy("act", Ft[10 + h][:, :], ps[ob[h]][:, :], [B_ps[ob[h]]], [BF[10 + h]])
            ps_release(held)
            pb_ = []
            for h in range(4):
                act(bfv(h), Ft[10 + h][:, :], AF.Square, [BF[10 + h]], [BF[h]])
                b10 = ps_alloc(); pb_.append(b10)
                stats_bcast([bfv(h)], [BF[h]], b10, bf=True)
            for h in range(4):
                act(Ft[h][:, :], ps[pb_[h]][:, :], AF.Ln, [B_ps[pb_[h]]], [BF[h]], bias=EPS, scale=1.0 / 128)
            for h in range(4):
                act(Ft[h][:, :], Ft[h][:, :], AF.Exp, [BF[h]], [BF[h]], scale=-0.5)
            for h in range(4):
                o_, Bo_ = Ft[10 + h], BF[10 + h]
                stt(o_[:, :], o_[:, :], pvec[:, pv(l, "h_nw"):pv(l, "h_nw") + 1], Ft[h][:, :], ALU.mult, ALU.mult, [Bo_, B_pv, BF[h]], [Bo_])
                stt(yT[:, 8 + h, :], GZ[:, h, :], 0.5, o_[:, :], ALU.mult, ALU.mult, [Bo_, B_gz[h]], [B_y[8 + h]])

            slot, B_slot = next_weight()
            slot2, B_slot2 = next_weight()
            bl = ps_alloc()
            proj_fm(slot2, B_slot2, 0, 16, bl)
            cpy("act", glr[:, :], ps[bl][0:16, :], [B_ps[bl]], [B_glr])
            qT_ = {}; kT_ = {}; kP_ = {}; Bq_ = {}; Bk_ = {}; dec_ = {}; Bdec_ = {}
            def gen_gprep(cc, tset):
                F_lg, F_G, F_en, F_qe = tset
                bg = ps_alloc(True)
                mmg([(ps[bg][:, :], lr2[:, l, cc * 128:(cc + 1) * 128], glr[:, :], True, True)], [B_const, B_glr], [B_ps[bg]])
                yield
                lg, Blg = Ft[F_lg], BF[F_lg]
                act(lg[:, :], ps[bg][:, :], AF.Exp, [B_ps[bg], B_dc], [Blg], bias=dcst[:, DC_NB2 + cc:DC_NB2 + cc + 1], scale=-1.0)
                ps_free(bg)
                yield
                act(lg[:, :], lg[:, :], AF.Ln, [Blg], [Blg], bias=1.0)
                yield
                Gc, BGc = Ft[F_G], BF[F_G]
                tk.op("dve", lambda e: e.tensor_tensor_scan(out=Gc[:, :], data0=ones_T, data1=lg[:, :], initial=0.0, op0=ALU.mult, op1=ALU.add),
                      [Blg, B_const], [BGc])
                yield
                eGc, BeGc = Ft[2 + cc], BF[2 + cc]
                act(eGc[:, :], Gc[:, :], AF.Exp, [BGc], [BeGc], scale=-1.0 / 16.0)
                yield
                enG, BenG = Ft[F_en], BF[F_en]
                act(enG[:, :], Gc[:, :], AF.Exp, [BGc], [BenG], scale=1.0 / 16.0)
                yield
                bq = ps_alloc(True)
                proj_fm(slot, B_slot, cc * 128, 128, bq)
                yield
                qe, Bqe = Ft[F_qe], BF[F_qe]
                stt(qe[:, :], ps[bq][:, :], 64.0 ** -0.5, eGc[:, :], ALU.mult, ALU.mult, [B_ps[bq], BeGc], [Bqe])
                ps_free(bq)
                yield
                for hh in range(2):
                    h = cc * 2 + hh
                    ts("dve", Bt[h][:, :], qe[:, :], rowmask[:, hh:hh + 1], None, ALU.mult, None, [Bqe, B_const], [BB[h]])
                    yield
                    qT_[h] = Bt[h]; Bq_[h] = BB[h]
                    kT_[h] = Bt[4 + cc]; kP_[h] = BtP[4 + cc]; Bk_[h] = BB[4 + cc]
                    dec_[h] = eGc; Bdec_[h] = BeGc
                bk = ps_alloc(True)
                proj_fm(slot, B_slot, 256 + cc * 128, 128, bk)
                yield
                tt("dve", Bt[4 + cc][:, :], ps[bk][:, :], enG[:, :], ALU.mult, [B_ps[bk], BenG], [BB[4 + cc]])
                ps_free(bk)
                yield

            def gen_gvz():
                for h in range(4):
                    b12 = ps_alloc(True)
                    proj_fm(slot2, B_slot2, 16 + h * 128, 128, b12)
                    yield
                    act(GZ[:, h, :], ps[b12][:, :], AF.Tanh, [B_ps[b12]], [B_gz[h]], scale=0.5)
                    yield
                    stt(GZ[:, h, :], GZ[:, h, :], 1.0, ps[b12][:, :], ALU.add, ALU.mult, [B_gz[h], B_ps[b12]], [B_gz[h]])
                    ps_free(b12)
                    yield
                for j in range(NCH):
                    b3 = ps_alloc(True)
                    lst = [(ps[b3][:, :], hT[:, kc, j * L:j * L + 128], slot[:, kc, 512:1024], kc == 0, kc == 7) for kc in range(8)]
                    mmg(lst, [B_h, B_slot], [B_ps[b3]])
                    yield
                    cpy("act", vtok4[0:64, :, j, :], ps[b3][0:64, :].rearrange("p (h e) -> p h e", h=4), [B_ps[b3]], [B_vflat])
                    ps_free(b3)
                    yield

            run_interleaved([gen_gprep(0, (0, 1, 4, 5)), gen_gprep(1, (6, 7, 8, 9)), gen_gvz()])
            issue_brh(l, 0, 0)
            issue_brh(l, 0, 1)
            weights_done(2)
            held = ps_hold(5)
            ob = held[0:4]; ub = held[4]
            sfl = {h: st_G[:, l * 4 + h, :] for h in range(4)}
            chunk_attn([0, 1, 2, 3], vtok4, qT_, kT_, kP_, Bq_, Bk_, 128, st_G[:, l * 4:l * 4 + 4, :], sfl, {h: B_stG[l][h] for h in range(4)}, False, None, dec_, Bdec_, ob, None, ub)
            for h in range(4):
                cpy("act", Ft[10 + h][:, :], ps[ob[h]][:, :], [B_ps[ob[h]]], [BF[10 + h]])
            ps_release(held)
            pb_ = []
            for h in range(4):
                act(bfv(h), Ft[10 + h][:, :], AF.Square, [BF[10 + h]], [BF[h]])
                b10 = ps_alloc(); pb_.append(b10)
                stats_bcast([bfv(h)], [BF[h]], b10, bf=True)
            for h in range(4):
                act(Ft[h][:, :], ps[pb_[h]][:, :], AF.Ln, [B_ps[pb_[h]]], [BF[h]], bias=EPS, scale=1.0 / 128)
            for h in range(4):
                act(Ft[h][:, :], Ft[h][:, :], AF.Exp, [BF[h]], [BF[h]], scale=-0.5)
            for h in range(4):
                o_, Bo_ = Ft[10 + h], BF[10 + h]
                stt(o_[:, :], o_[:, :], pvec[:, pv(l, "g_nw"):pv(l, "g_nw") + 1], Ft[h][:, :], ALU.mult, ALU.mult, [Bo_, B_pv, BF[h]], [Bo_])
                stt(yT[:, 12 + h, :], GZ[:, h, :], 0.5, o_[:, :], ALU.mult, ALU.mult, [Bo_, B_gz[h]], [B_y[12 + h]])

            for n in range(4):
                slot, B_slot = next_weight()
                for dc in range(8):
                    bgate = ps_alloc(); bpr = ps_alloc()
                    proj_fm(slot, B_slot, dc * 128, 128, bgate)
                    hb = dc // 4
                    lst = [(ps[bpr][:, :], brh[hb][:, wc, (dc % 4) * 128:(dc % 4 + 1) * 128], yT[:, n * 4 + wc, :], wc == 0, wc == 3) for wc in range(4)]
                    mmg(lst, B_brh[hb] + B_y[n * 4:n * 4 + 4], [B_ps[bpr]])
                    if n < 3 and dc % 4 == 3:
                        issue_brh(l, n + 1, hb)
                    sg, Bsg = Ft[dc % 2], BF[dc % 2]
                    act(sg[:, :], ps[bgate][:, :], AF.Tanh, [B_ps[bgate]], [Bsg], scale=0.5)
                    if n == 0:
                        stt(acc[:, dc, :], sg[:, :], 1.0, ps[bpr][:, :], ALU.add, ALU.mult, [B_ps[bpr], Bsg], [B_acc[dc]])
                    else:
                        pr, Bpr = Ft[2 + dc % 2], BF[2 + dc % 2]
                        stt(pr[:, :], sg[:, :], 1.0, ps[bpr][:, :], ALU.add, ALU.mult, [B_ps[bpr], Bsg], [Bpr])
                        if n < 3:
                            tt("pool", acc[:, dc, :], acc[:, dc, :], pr[:, :], ALU.add, [B_acc[dc], Bpr], [B_acc[dc]])
                        else:
                            tt("pool", mg[:, dc, 0:T], acc[:, dc, :], pr[:, :], ALU.add, [B_acc[dc], Bpr], [B_mgc[dc]])
                weights_done(1, defer=(n == 3 and step < NT))
            slot, B_slot = next_weight()
            for ec in range(8):
                bo = ps_alloc()
                lst = [(ps[bo][:, :], slot[:, dc, ec * 128:(ec + 1) * 128], mg[:, dc, 0:T], dc == 0, dc == 7) for dc in range(8)]
                mmg(lst, [B_slot] + B_mgc, [B_ps[bo]])
                stt(xT[:, ec, :], ps[bo][:, :], 0.5, xT[:, ec, :], ALU.mult, ALU.add, [B_x, B_ps[bo]], [B_x])
            weights_done(1, defer=(step < NT))

        if step < NT:
            tk.dma("sp", "d_send", send_d.rearrange("(kc p) s -> p kc s", p=128), xT[:, :, :], [B_x], [B_send])
            tk.coll("pool", "cc", lambda e: e.collective_compute("AllGather", ALU.bypass, replica_groups=[[2 * i, 2 * i + 1] for i in range(n_pairs)],
                                                                ins=[send_d], outs=[recv_d]), [B_send], [B_recv])
            pump()
        if step >= 1:
            b = ps_alloc()
            for kc in range(8):
                act(bfv(kc % 2), xT[:, kc, :], AF.Square, [B_x], [BF[kc % 2]])
                tk.mm([lambda e, kc=kc, b=b: e.matmul(ps[b][:, :], ones_b[:, :], bfv(kc % 2), start=(kc == 0), stop=(kc == 7))],
                      [B_const, BF[kc % 2]], [B_ps[b]], cost=0.3)
            rstd_from(b, D, Ft[2][:, :], BF[2])
            for kc in range(8):
                stt(osb[:, kc, :], xT[:, kc, :], pvec[:, pv(0, "final_g") + kc:pv(0, "final_g") + kc + 1], Ft[2][:, :], ALU.mult, ALU.mult,
                    [B_x, B_pv, BF[2]], [B_acc[kc]])
            osl = slice((step - 1) * T, step * T)
            tk.dma("sp", "d_out", outT_d.rearrange("(kc p) s -> p kc s", p=128)[:, :, osl], osb[:, :, :], B_acc, [])

    tk.final_wait("sp", "d_out")

    def replay(en):
        def f(e):
            for (name, a, k, h) in streams[en]:
                ins = getattr(e, name)(*a, **k)
                for (pn, pa) in h.post:
                    getattr(ins, pn)(*pa)
        return f
    block.tensor(replay("pe"))
    block.scalar(replay("act"))
    block.vector(replay("dve"))
    block.gpsimd(replay("pool"))
    block.sync(replay("sp"))
    es.close()
    return nc, tk


def _host_pack(inputs, l, is_b):
    f = lambda k: np.asarray(inputs[k], dtype=np.float32)
    pvec = np.zeros((128, PV_COLS), np.float32)
    def put(name, vec, nchunks, stride=1, off=0):
        for c in range(nchunks):
            pvec[:, pv(0, name) + c * stride + off] = vec[c * 128:(c + 1) * 128]
    for j in range(4):
        put("lru_cw", f("lru_conv_w")[l, j], 4, stride=4, off=j)
        put("m_cw", f("m_conv_w")[l, j], 4, stride=4, off=j)
    put("lru_cb", f("lru_conv_b")[l], 4)
    put("lru_ba", f("lru_ba")[l], 4)
    put("lru_bx", f("lru_bx")[l], 4)
    put("lru_lam", f("lru_lambda")[l], 4)
    put("m_cb", f("m_conv_b")[l], 4)
    put("m_nw", f("m_norm_w")[l], 4)
    put("m_skip", f("m_skip")[l], 4)
    put("h_lbl", f("h_lb_logits")[0], 4)
    put("h_lbl1", f("h_lb_logits")[1], 4)
    put("h_nw", f("h_norm_w")[l], 1)
    put("g_b2", f("g_b_lr2")[l], 2)
    put("g_nw", f("g_norm_w")[l], 1)
    put("norm_g", f("norm_g")[l], 8)
    put("final_g", f("final_g"), 8)
    pvec[:, pv(0, "flag_b")] = 1.0 if is_b else 0.0
    pvec[:, pv(0, "keep")] = 0.0 if is_b else 1.0
    bd = np.zeros((128, 20, 128), np.float32)
    for gi, key in enumerate(["lru_wa", "lru_wx"]):
        w = f(key)[l]
        for c in range(4):
            for b in range(2):
                bd[b * 64:(b + 1) * 64, gi * 4 + c, b * 64:(b + 1) * 64] = w[2 * c + b]
    for gi, key in enumerate(["m_wq", "m_wk", "m_wv"]):
        w = f(key)[l]
        for h in range(4):
            for b in range(32):
                bd[b * 4:(b + 1) * 4, 8 + gi * 4 + h, b * 4:(b + 1) * 4] = w[32 * h + b]
    wif = np.zeros((128, 2 * 12, 4), np.float32)
    gb = np.zeros((4, 2), np.float32)
    for gi, key in enumerate(["m_wi", "m_wf"]):
        w = f(key)[l]
        for ci in range(12):
            wif[:, gi * 12 + ci, :] = w[ci * 128:(ci + 1) * 128, :]
    gb[:, 0] = f("m_bi")[l]
    gb[:, 1] = f("m_bf")[l]
    lr2 = np.ascontiguousarray(f("g_w_lr2")[l][:, None, :])
    cst = np.zeros((128, 1664), np.float32)
    cst[:, 0:128] = np.eye(128, dtype=np.float32)
    cst[:, 128:256] = 1.0
    cst[0:64, 256:320] = np.triu(np.ones((64, 64), np.float32))
    rm = np.ones((T,), np.float32); rm[::L] = 0.0
    cst[:, 320:832] = rm[None, :]
    cst[0:64, 832] = 1.0
    cst[64:128, 833] = 1.0
    for j in range(4):
        cst[j, 896 + j * 128:896 + (j + 1) * 128] = 1.0
        cst[0:64, 1408 + j * 64:1408 + (j + 1) * 64] = np.triu(np.ones((64, 64), np.float32))
    return dict(w_in=np.ascontiguousarray(f("w_in")[l:l + 1]), w_branch=np.ascontiguousarray(f("w_branch")[l:l + 1]),
                w_out=np.ascontiguousarray(f("w_out")[l:l + 1]), pvec=pvec, bd=bd, wif=wif, gbias=gb, lr2=lr2, cst=cst)


_PROG_CACHE = {}


def kernel(**inputs):
    x = np.asarray(inputs["x"], dtype=np.float32)
    B, S, _ = x.shape
    packs = [_host_pack(inputs, 0, False), _host_pack(inputs, 1, True)]
    if (S, B) not in _PROG_CACHE:
        _PROG_CACHE[(S, B)] = build_program(S, n_pairs=B)[0]
    nc = _PROG_CACHE[(S, B)]
    zeros = np.zeros((D, S), np.float32)
    in_maps = []
    for b in range(B):
        ma = dict(packs[0]); ma["xT"] = np.ascontiguousarray(x[b].T)
        mb = dict(packs[1]); mb["xT"] = zeros
        in_maps += [ma, mb]
    res = run_bass_kernel_spmd(nc, in_maps, core_ids=list(range(2 * B)))
    out = np.stack([np.ascontiguousarray(res.results[2 * b + 1]["outT"].T) for b in range(B)], axis=0)
    return out.astype(np.float32)
```

```python
import numpy as np
import concourse.bass as bass
import concourse.mybir as mybir
from concourse.bass_utils import run_bass_kernel_spmd

F32 = mybir.dt.float32
BF16 = mybir.dt.bfloat16
AF = mybir.ActivationFunctionType
ALU = mybir.AluOpType

D = 1024
W = 512
D_IN = 10256
DEPTH = 2
T = 512
L = 64
NCH = T // L
EPS = 1e-6
C_LRUX, C_LRUZ = 0, 512
C_MX, C_MO, C_MZ = 1024, 1536, 2048
C_HQ, C_HF, C_HI, C_HZ = 2560, 3072, 3584, 4096
C_GQ, C_GK, C_GV, C_GLR, C_GZ = 4608, 4864, 5120, 5632, 5648
C_MERGE = 6160

PV_PER_LAYER = 96
def pv(l, name, c=0):
    base = l * PV_PER_LAYER
    table = {
        "lru_cw": 0,
        "lru_cb": 16,
        "lru_ba": 20,
        "lru_bx": 24,
        "lru_lam": 28,
        "m_cw": 32,
        "m_cb": 48,
        "m_nw": 52,
        "m_skip": 56,
        "h_lbl": 60,
        "h_nw": 64,
        "g_b2": 65,
        "g_nw": 67,
        "norm_g": 68,
        "final_g": 76,
        "flag_b": 84,
        "keep": 85,
        "h_lbl1": 86,
    }
    return base + table[name] + c
SAME_ENGINE_WINDOW = 3
PL = 1
PV_COLS = PL * PV_PER_LAYER


class Buf:
    __slots__ = ("w", "r", "name", "excl")
    def __init__(self, name="", excl=False):
        self.w = None
        self.r = []
        self.name = name
        self.excl = excl


class Trk:
    def __init__(self, nc, engs, sems):
        self.nc = nc
        self.E = engs
        self.S = sems
        self.tick = {k: 0 for k in engs}
        self.waited = {k: {} for k in engs}
        self.dmaval = {}
        self.nwait = 0
        self.ninst = 0
        self.efree = {k: 0.0 for k in engs}
        self.fin = {}
        self.last_fin = None
        self.COST = {"act": 0.6, "dve": 0.65, "pool": 1.6, "pe": 0.27, "sp": 0.1}

    def _time(self, en, cost, ndma=None):
        ready = 0.0
        for (kind, key), val in self._need.items():
            t = self.fin.get((kind, key, val), 0.0)
            if kind == "e" and key != en:
                t += 0.15
            if t > ready:
                ready = t
        start = max(self.efree[en], ready)
        fin = start + cost
        self.efree[en] = fin
        return fin

    def _deps(self, en, reads, writes):
        need = {}
        def add(dep):
            if dep is None:
                return
            kind, key, val = dep
            k = (kind, key)
            if need.get(k, 0) < val:
                need[k] = val
        for b in reads:
            add(b.w)
            if b.excl:
                for r in b.r:
                    add(r)
        for b in writes:
            add(b.w)
            for r in b.r:
                add(r)
        out = []
        self._need = need
        for (kind, key), val in need.items():
            if kind == "e" and key == en:
                if en == "pe":
                    continue
                if en != "pool" and val <= self.tick[en] - SAME_ENGINE_WINDOW:
                    continue
            if self.waited[en].get((kind, key), 0) >= val:
                continue
            self.waited[en][(kind, key)] = val
            out.append((self.S[key], val))
        return out

    def op(self, en, fn, reads=(), writes=(), cost=None):
        eng = self.E[en]
        waits = self._deps(en, reads, writes)
        tfin = self._time(en, self.COST[en] if cost is None else cost)
        for (s, v) in waits[1:]:
            eng.wait_ge(s, v)
            self.nwait += 1
        ins = fn(eng)
        if waits:
            ins._wait_ge(waits[0][0], waits[0][1])
        self.tick[en] += 1
        ins.then_inc(self.S[en], 1)
        me = ("e", en, self.tick[en])
        self.fin[me] = tfin
        self.last_fin = tfin
        for b in reads:
            if b.excl:
                b.w = me
                b.r = []
            else:
                b.r.append(me)
        for b in writes:
            b.w = me
            b.r = []
        self.ninst += 1
        return ins

    def mm(self, fns, reads=(), writes=(), cost=None):
        en = "pe"
        eng = self.E[en]
        waits = self._deps(en, reads, writes)
        tfin = self._time(en, (0.1 * len(fns)) if cost is None else cost)
        for (s, v) in waits[1:]:
            eng.wait_ge(s, v)
            self.nwait += 1
        ins = None
        for i, fn in enumerate(fns):
            ins = fn(eng)
            if i == 0 and waits:
                ins._wait_ge(waits[0][0], waits[0][1])
        self.tick[en] += 1
        ins.then_inc(self.S[en], 1)
        me = ("e", en, self.tick[en])
        self.fin[me] = tfin
        self.last_fin = tfin
        for b in reads:
            if b.excl:
                b.w = me
                b.r = []
            else:
                b.r.append(me)
        for b in writes:
            b.w = me
            b.r = []
        self.ninst += len(fns)

    def dma(self, en, semkey, out, in_, reads=(), writes=()):
        eng = self.E[en]
        waits = self._deps(en, reads, writes)
        for (s, v) in waits:
            eng.wait_ge(s, v)
            self.nwait += 1
        ins = eng.dma_start(out=out, in_=in_)
        self.dmaval[semkey] = self.dmaval.get(semkey, 0) + 16
        ins.then_inc(self.S[semkey], 16)
        me = ("d", semkey, self.dmaval[semkey])
        self.fin[me] = self._time(en, 0.1) + 12.0
        for b in reads:
            if b.excl:
                b.w = me
                b.r = []
            else:
                b.r.append(me)
        for b in writes:
            b.w = me
            b.r = []
        self.ninst += 1

    def coll(self, en, semkey, fn, reads=(), writes=()):
        eng = self.E[en]
        waits = self._deps(en, reads, writes)
        for (s, v) in waits:
            eng.wait_ge(s, v)
            self.nwait += 1
        ins = fn(eng)
        self.dmaval[semkey] = self.dmaval.get(semkey, 0) + 1
        ins.then_inc(self.S[semkey], 1)
        me = ("d", semkey, self.dmaval[semkey])
        self.fin[me] = self._time(en, 0.1) + 35.0
        for b in reads:
            if b.excl:
                b.w = me
                b.r = []
            else:
                b.r.append(me)
        for b in writes:
            b.w = me
            b.r = []
        self.ninst += 1

    def final_wait(self, en, semkey):
        self.E[en].wait_ge(self.S[semkey], self.dmaval[semkey])


def build_program(S, n_layers=PL, n_pairs=4):
    assert S % T == 0
    NT = S // T
    nc = bass.Bass("TRN2", target_bir_lowering=False)

    xT_d = nc.dram_tensor("xT", [D, S], F32, kind="ExternalInput").ap()
    w_in_d = nc.dram_tensor("w_in", [PL, D, D_IN], F32, kind="ExternalInput").ap()
    w_br_d = nc.dram_tensor("w_branch", [PL, 4, W, D], F32, kind="ExternalInput").ap()
    w_out_d = nc.dram_tensor("w_out", [PL, D, D], F32, kind="ExternalInput").ap()
    pvec_d = nc.dram_tensor("pvec", [128, PV_COLS], F32, kind="ExternalInput").ap()
    bd_d = nc.dram_tensor("bd", [128, PL * 20, 128], F32, kind="ExternalInput").ap()
    wif_d = nc.dram_tensor("wif", [128, PL * 2 * 12, 4], F32, kind="ExternalInput").ap()
    gb_d = nc.dram_tensor("gbias", [4, PL * 2], F32, kind="ExternalInput").ap()
    lr2_d = nc.dram_tensor("lr2", [16, PL, 256], F32, kind="ExternalInput").ap()
    cst_d = nc.dram_tensor("cst", [128, 1664], F32, kind="ExternalInput").ap()
    outT_d = nc.dram_tensor("outT", [D, S], F32, kind="ExternalOutput").ap()
    send_d = nc.dram_tensor("send", [D, T], F32, kind="Internal").ap()
    recv_d = nc.dram_tensor("recv", [2 * D, T], F32, kind="Internal").ap()
    B_send = Buf("send"); B_recv = Buf("recv")

    from contextlib import ExitStack
    es = ExitStack()
    sb = lambda name, shape, dt: es.enter_context(nc.sbuf_tensor(name, shape, dt))

    xT = sb("xT_s", [128, 8, T], F32); B_x = Buf("xT")
    hT = sb("hT_s", [128, 8, T + L], BF16); B_h = Buf("hT")
    yT = sb("yT_s", [128, 16, T], BF16); B_y = [Buf("y%d" % i) for i in range(16)]
    acc = sb("acc_s", [128, 8, T], F32); B_acc = [Buf("acc%d" % i) for i in range(8)]
    mg = sb("mg_s", [128, 8, T + L], BF16); B_mgc = [Buf("mg%d" % i) for i in range(8)]
    NSLOT = 3
    ring = [sb("ring%d" % i, [128, 8, 1024], BF16) for i in range(NSLOT)]
    B_ring = [Buf("ring%d" % i) for i in range(NSLOT)]
    pvec = sb("pvec_s", [128, PV_COLS], F32); B_pv = Buf("pvec")
    dcst = sb("dcst_s", [128, 40], F32)
    gbn = sb("gbn_s", [4, 1], F32)
    bd = sb("bd_s", [128, PL * 20, 128], BF16)
    wif = sb("wif_s", [128, PL * 2 * 12, 4], BF16)
    gb = sb("gb_s", [4, PL * 2], F32)
    lr2 = sb("lr2_s", [16, PL, 256], BF16)
    cst = sb("cst_s", [128, 896], F32)
    identb = sb("identb_s", [128, 128], BF16)
    ones_b = sb("ones_b_s", [128, 128], BF16)
    B_const = Buf("const")
    ident_f = cst[:, 0:128]
    ones_f = cst[:, 128:256]
    maskT = cst[0:64, 256:320]
    rmask = cst[:, 320:832]
    rowmask = cst[:, 832:834]
    sel = sb("sel_s", [4, 4, 128], F32)
    maskrep = sb("maskrep_s", [64, 4, L], F32)
    onesT = sb("onesT_s", [128, T], F32)
    ones_T = onesT[:, :]
    ones4 = onesT[0:4, :]

    NF = 16
    EG4 = sb("EG4_s", [128, 4, T], F32)
    FB4 = sb("FB4_s", [128, 4, T], F32)
    _fmap = {7: 0, 8: 1, 9: 2, 14: 3}
    Ft = [EG4[:, i - 3, :] if 3 <= i <= 6 else (FB4[:, _fmap[i], :] if i in _fmap else sb("F%d" % i, [128, T], F32)) for i in range(NF)]
    brh = [FB4[:, 0:2, :].bitcast(BF16).rearrange("p a (b c) -> p (a b) c", b=2),
           FB4[:, 2:4, :].bitcast(BF16).rearrange("p a (b c) -> p (a b) c", b=2)]
    BF = [Buf("F%d" % i) for i in range(NF)]
    B_brh = [[Buf("brh0"), BF[7], BF[8]], [Buf("brh1"), BF[9], BF[14]]]
    NB = 8
    BtP = [sb("B%d" % i, [128, T + L], BF16) for i in range(NB)]
    Bt = [t_[:, 0:T] for t_ in BtP]
    BB = [Buf("B%d" % i) for i in range(NB)]
    xpad = sb("xpad_s", [128, T + 3], F32); B_xpad = Buf("xpad")
    xpad2 = sb("xpad2_s", [128, T + 3], F32); B_xpad2 = Buf("xpad2")
    GZ = sb("GZ_s", [128, 4, T], F32); B_gz = [Buf("gz%d" % i) for i in range(4)]
    xm_f = acc[:, 0:4, :]; B_xm = B_acc[0:4]
    xm_b = mg[:, 0:4, 0:T]; B_xmb = B_mgc[0:4]
    mx_b = mg[:, 4:8, 0:T]; B_mxb = B_mgc[4:8]
    mx_bP = mg[:, 4:8, :]
    qkv_b = yT[:, 4:16, :]; B_qkv = B_y[4:16]
    vflat = sb("vflat_s", [128, 4096], BF16); B_vflat = Buf("vflat")
    vtok2 = vflat[:, :].rearrange("p (h j e) -> p h j e", h=2, j=NCH, e=256)
    vtok4 = vflat[:, :].rearrange("p (h j e) -> p h j e", h=4, j=NCH, e=128)
    B_vtok = [B_vflat] * 4
    g4 = [Ft[12 + i][0:4, :] for i in range(4)]; B_g4 = [BF[12 + i] for i in range(4)]
    glr = sb("glr_s", [16, T], BF16); B_glr = Buf("glr")
    ATm = sb("ATm_s", [128, 4, 2, L], BF16); B_ATm = [[Buf("ATm%d_%d" % (i, p)) for p in range(2)] for i in range(4)]
    ktok = sb("ktok_s", [128, 4, 2, 128], BF16); B_ktok = [[Buf("ktok%d_%d" % (i, p)) for p in range(2)] for i in range(4)]
    Sbf = sb("Sbf_s", [128, 4, 256], BF16); B_Sbf = [Buf("Sbf%d" % i) for i in range(4)]
    Stmp = sb("Stmp_s", [128, 4, 128], F32); B_Stmp = [Buf("Stmp%d" % i) for i in range(4)]
    st_conv = sb("st_conv", [128, PL * 2 * 4, 3], F32); B_stconv = Buf("stconv")
    st_lru = sb("st_lru", [128, PL * 4], F32); B_stlru = Buf("stlru")
    st_C = sb("st_C", [128, PL * 4, 256], F32); B_stC = [[Buf("stC%d_%d" % (l, h)) for h in range(4)] for l in range(PL)]
    st_H = sb("st_H", [128, PL * 4, 128], F32); B_stH = [[Buf("stH%d_%d" % (l, h)) for h in range(4)] for l in range(PL)]
    st_G = sb("st_G", [128, PL * 4, 128], F32); B_stG = [[Buf("stG%d_%d" % (l, h)) for h in range(4)] for l in range(PL)]
    osb = acc

    ps = [es.enter_context(nc.psum_tensor("ps%d" % i, [128, 512], F32)) for i in range(8)]
    B_ps = [Buf("ps%d" % i, excl=True) for i in range(8)]
    B_psA = [[Buf("psA%d_%d" % (i, p)) for p in range(2)] for i in range(4)]
    B_psT = [[Buf("psT%d_%d" % (i, p)) for p in range(2)] for i in range(4)]
    pool_banks = [0, 1, 2, 3, 4, 5]
    ps_state = {"next": 0, "held": set(), "live": set()}
    def ps_alloc(keep=False):
        for _ in range(2 * len(pool_banks)):
            b = pool_banks[ps_state["next"] % len(pool_banks)]
            ps_state["next"] += 1
            if b not in ps_state["held"] and b not in ps_state["live"]:
                if keep:
                    ps_state["live"].add(b)
                return b
        raise RuntimeError("out of PSUM banks")
    def ps_free(*bs):
        for b in bs:
            ps_state["live"].discard(b)
    def ps_hold(n):
        out = []
        for _ in range(n):
            b = ps_alloc()
            ps_state["held"].add(b)
            out.append(b)
        return out
    def ps_release(bs):
        for b in bs:
            ps_state["held"].discard(b)

    sem_names = ["pe", "act", "dve", "pool", "sp", "d_setup", "d_setup2", "d_x", "d_out", "d_send", "d_recv", "cc", "d_brh0", "d_brh1"] + ["d_ring%d" % i for i in range(NSLOT)]
    sems = {n: es.enter_context(nc.semaphore("s_" + n)) for n in sem_names}
    block = es.enter_context(nc.Block())
    prog = []

    streams = {"pe": [], "act": [], "dve": [], "pool": [], "sp": []}

    class Rec:
        def __init__(self, en):
            self.en = en
        def __getattr__(self, name):
            en = self.en
            def call(*a, **k):
                h = RecIns()
                streams[en].append((name, a, k, h))
                return h
            return call

    class RecIns:
        def __init__(self):
            self.post = []
        def _wait_ge(self, s, v):
            self.post.append(("_wait_ge", (s, v)))
            return self
        def then_inc(self, s, v):
            self.post.append(("then_inc", (s, v)))
            return self

    engs = {k: Rec(k) for k in streams}
    tk = Trk(nc, engs, sems)

    def act(out, in_, func, reads, writes, bias=None, scale=None):
        kw = {}
        if bias is not None:
            kw["bias"] = bias
        if scale is not None:
            kw["scale"] = scale
        tk.op("act", lambda e: e.activation(out=out, in_=in_, func=func, **kw), reads, writes)

    def tt(en, out, in0, in1, op, reads, writes):
        tk.op(en, lambda e: e.tensor_tensor(out=out, in0=in0, in1=in1, op=op), reads, writes)

    def ts(en, out, in0, s1, s2, op0, op1, reads, writes):
        if s2 is None:
            tk.op(en, lambda e: e.tensor_scalar(out=out, in0=in0, scalar1=s1, scalar2=None, op0=op0), reads, writes)
        else:
            tk.op(en, lambda e: e.tensor_scalar(out=out, in0=in0, scalar1=s1, scalar2=s2, op0=op0, op1=op1), reads, writes)

    def stt(out, in0, scalar, in1, op0, op1, reads, writes):
        tk.op("dve", lambda e: e.scalar_tensor_tensor(out=out, in0=in0, scalar=scalar, in1=in1, op0=op0, op1=op1), reads, writes)

    def cpy(en, out, in_, reads, writes):
        if en == "act":
            tk.op("act", lambda e: e.copy(out=out, in_=in_), reads, writes)
        else:
            tk.op(en, lambda e: e.tensor_copy(out=out, in_=in_), reads, writes)

    def mmg(lst, reads, writes):
        fns = []
        cost = 0.0
        for (o, l_, r_, st, sp) in lst:
            fns.append((lambda o=o, l_=l_, r_=r_, st=st, sp=sp: (lambda e: e.matmul(o, l_, r_, start=st, stop=sp)))())
            n_mov = int(r_.shape[-1])
            cost += max(n_mov, 64) / 1900.0 * (4.0 if r_.dtype == F32 else 1.0) + 0.03
        tk.mm(fns, reads, writes, cost=cost)

    setup_bufs = [B_pv, B_const]
    tk.dma("sp", "d_setup", pvec[:], pvec_d[:, :], [], [B_pv])
    tk.dma("sp", "d_setup", cst[:], cst_d[:, 0:896], [], [B_const])
    tk.dma("sp", "d_setup", gb[:], gb_d[:, :], [], [B_const])
    tk.dma("sp", "d_setup", sel[:], cst_d[0:4, 896:1408].rearrange("p (j m) -> p j m", j=4), [], [B_const])
    tk.dma("sp", "d_setup", maskrep[:], cst_d[0:64, 1408:1664].rearrange("p (h t) -> p h t", h=4), [], [B_const])
    tk.dma("pool", "d_setup2", bd[:], bd_d[:, :, :], [], [B_const])
    tk.dma("pool", "d_setup2", wif[:], wif_d[:, :, :], [], [B_const])
    tk.dma("pool", "d_setup2", lr2[:], lr2_d[:, :, :], [], [B_const])
    tk.dma("pool", "d_setup2", identb[:], cst_d[:, 0:128], [], [B_const])
    tk.dma("pool", "d_setup2", ones_b[:], cst_d[:, 128:256], [], [B_const])
    for en_ in ("pe", "act", "dve", "pool"):
        for sk_ in ("d_setup", "d_setup2"):
            engs[en_].wait_ge(sems[sk_], tk.dmaval[sk_])
            tk.waited[en_][("d", sk_)] = tk.dmaval[sk_]
    for b_ in (B_pv, B_const):
        b_.w = None

    B_dc = Buf("dcst")
    DC_S1, DC_S2, DC_HBA, DC_HBX, DC_C0, DC_C1, DC_HSK, DC_NB2, DC_LB = 0, 4, 8, 12, 16, 20, 24, 28, 30
    l = 0
    lam = pvec[:, pv(l, "lru_lam"):pv(l, "lru_lam") + 4]
    act(dcst[:, 0:4], lam, AF.Exp, [B_pv], [B_dc], scale=-1.0)
    act(dcst[:, 0:4], dcst[:, 0:4], AF.Ln, [B_dc], [B_dc], bias=1.0)
    ts("dve", dcst[:, 4:8], dcst[:, 0:4], -8.0, None, ALU.mult, None, [B_dc], [B_dc])
    ts("dve", dcst[:, 0:4], dcst[:, 0:4], -4.0, None, ALU.mult, None, [B_dc], [B_dc])
    ts("dve", dcst[:, 8:12], pvec[:, pv(l, "lru_ba"):pv(l, "lru_ba") + 4], 0.5, None, ALU.mult, None, [B_pv], [B_dc])
    ts("dve", dcst[:, 12:16], pvec[:, pv(l, "lru_bx"):pv(l, "lru_bx") + 4], 0.5, None, ALU.mult, None, [B_pv], [B_dc])
    ts("dve", dcst[:, 24:28], pvec[:, pv(l, "m_skip"):pv(l, "m_skip") + 4], 0.5, None, ALU.mult, None, [B_pv], [B_dc])
    ts("dve", dcst[:, 28:30], pvec[:, pv(l, "g_b2"):pv(l, "g_b2") + 2], -1.0, None, ALU.mult, None, [B_pv], [B_dc])
    ts("dve", gbn[:, 0:1], gb[:, 1:2], -1.0, None, ALU.mult, None, [B_const], [B_dc])
    l0 = pvec[:, pv(l, "h_lbl"):pv(l, "h_lbl") + 4]
    l1 = pvec[:, pv(l, "h_lbl1"):pv(l, "h_lbl1") + 4]
    tt("dve", dcst[:, 30:34], l1, l0, ALU.subtract, [B_pv], [B_dc])
    act(dcst[:, 30:34], dcst[:, 30:34], AF.Tanh, [B_dc], [B_dc], scale=0.5)
    ts("dve", dcst[:, 30:34], dcst[:, 30:34], 0.5, 0.5, ALU.mult, ALU.add, [B_dc], [B_dc])
    ts("dve", dcst[:, 30:34], dcst[:, 30:34], pvec[:, pv(l, "flag_b"):pv(l, "flag_b") + 1], None, ALU.mult, None, [B_dc, B_pv], [B_dc])
    ts("dve", dcst[:, 20:24], dcst[:, 30:34], -0.5, 0.5, ALU.mult, ALU.add, [B_dc], [B_dc])
    tt("dve", dcst[:, 16:20], dcst[:, 30:34], dcst[:, 20:24], ALU.add, [B_dc], [B_dc])
    tk.op("dve", lambda e: e.memset(st_conv[:], 0.0), [], [B_stconv])
    tk.op("dve", lambda e: e.memset(st_lru[:], 0.0), [], [B_stlru])
    allC = [b for l in B_stC for b in l]; allH = [b for l in B_stH for b in l]; allG = [b for l in B_stG for b in l]
    tk.op("dve", lambda e: e.memset(st_C[:], 0.0), [], allC)
    tk.op("dve", lambda e: e.memset(st_H[:], 0.0), [], allH)
    tk.op("dve", lambda e: e.memset(st_G[:], 0.0), [], allG)
    tk.op("dve", lambda e: e.memset(onesT[:], 1.0), [], [B_const])
    tk.op("dve", lambda e: e.memset(vflat[:], 0.0), [], [B_vflat])
    tk.op("dve", lambda e: e.memset(ATm[:], 0.0), [], [b_ for l_ in B_ATm for b_ in l_])
    tk.op("dve", lambda e: e.memset(ktok[:], 0.0), [], [b_ for l_ in B_ktok for b_ in l_])
    tk.op("dve", lambda e: e.memset(hT[:], 0.0), [], [B_h])
    tk.op("dve", lambda e: e.memset(mg[:], 0.0), [], B_mgc)
    for i_ in range(NB):
        tk.op("dve", lambda e, i_=i_: e.memset(BtP[i_][:], 0.0), [], [BB[i_]])

    def wgroups(l):
        g = []
        g.append(("in", C_LRUX, 1024))
        g.append(("in", C_MX, 1024))
        g.append(("in", C_MZ, 512))
        g.append(("in", C_HQ, 1024))
        g.append(("in", C_HI, 1024))
        g.append(("in", C_GQ, 1024))
        g.append(("in", C_GLR, 528))
        for n in range(4):
            g.append(("in", C_MERGE + n * 1024, 1024))
        g.append(("out", 0, 1024))
        return g
    wsched = []
    for t in range(NT + 1):
        for l in range(n_layers):
            for gi, g in enumerate(wgroups(l)):
                wsched.append((l, g))
    wstate = {"issued": 0, "consumed": 0, "done": 0}
    def pump():
        while wstate["issued"] < len(wsched) and wstate["issued"] - wstate["done"] < NSLOT:
            i = wstate["issued"]
            l, (kind, a, ncols) = wsched[i]
            slot = i % NSLOT
            if kind == "in":
                src = w_in_d[l].rearrange("(kc p) c -> p kc c", p=128)[:, :, a:a + ncols]
                tk.dma("pool", "d_ring%d" % slot, ring[slot][:, :, 0:ncols], src, [], [B_ring[slot]])
            elif kind == "br":
                src = w_br_d[l, a].rearrange("(kc p) c -> p kc c", p=128)
                tk.dma("pool", "d_ring%d" % slot, ring[slot][:, 0:4, :], src, [], [B_ring[slot]])
            else:
                src = w_out_d[l].rearrange("(kc p) c -> p kc c", p=128)
                tk.dma("pool", "d_ring%d" % slot, ring[slot][:, :, :], src, [], [B_ring[slot]])
            wstate["issued"] += 1
    def issue_brh(l, n, half):
        src = w_br_d[l, n].rearrange("(kc p) c -> p kc c", p=128)[:, :, half * 512:(half + 1) * 512]
        tk.dma("pool", "d_brh%d" % half, brh[half], src, [], B_brh[half])

    def next_weight():
        i = wstate["consumed"]
        assert i < wstate["issued"], "weight group not issued (ring too small for live groups)"
        wstate["consumed"] += 1
        return ring[i % NSLOT], B_ring[i % NSLOT]
    def weights_done(n, defer=False):
        wstate["done"] += n
        if not defer:
            pump()

    def proj_fm(slot, B_slot, off, ncols, bank, extra_reads=()):
        lst = [(ps[bank][0:ncols, :], slot[:, kc, off:off + ncols], hT[:, kc, 0:T], kc == 0, kc == 7) for kc in range(8)]
        mmg(lst, [B_slot, B_h] + list(extra_reads), [B_ps[bank]])

    def bfv(i):
        return Ft[i].bitcast(BF16)[:, 0:T]

    def stats_bcast(src_list, src_bufs, bank, bf=False):
        n = len(src_list)
        lst = [(ps[bank][:, :], ones_b[:, :] if bf else ones_f, src_list[i], i == 0, i == n - 1) for i in range(n)]
        mmg(lst, [B_const] + list(src_bufs), [B_ps[bank]])

    def rstd_from(bank, nfeat, out_f, out_buf):
        act(out_f, ps[bank][:, :], AF.Ln, [B_ps[bank]], [out_buf], bias=EPS, scale=1.0 / nfeat)
        act(out_f, out_f, AF.Exp, [out_buf], [out_buf], scale=-0.5)

    def rmsnorm_to_h(gcol0):
        sq = Ft[0]
        b = ps_alloc()
        for kc in range(8):
            act(bfv(kc % 2), xT[:, kc, :], AF.Square, [B_x], [BF[kc % 2]])
            tk.mm([lambda e, kc=kc: e.matmul(ps[b][:, :], ones_b[:, :], bfv(kc % 2), start=(kc == 0), stop=(kc == 7))],
                  [B_const, BF[kc % 2]], [B_ps[b]], cost=0.3)
        rstd_from(b, D, Ft[2][:, :], BF[2])
        for kc in range(8):
            stt(hT[:, kc, 0:T], xT[:, kc, :], pvec[:, gcol0 + kc:gcol0 + kc + 1], Ft[2][:, :], ALU.mult, ALU.mult,
                [B_x, B_pv, BF[2]], [B_h])

    def conv_fm(bank, l, br, c, cw0, cb, out_f, out_buf, xp=None, Bxp=None):
        if xp is None:
            xp, Bxp = xpad, B_xpad
        si = (l * 2 + br) * 4 + c
        cpy("dve", xp[:, 0:3], st_conv[:, si, :], [B_stconv], [Bxp])
        cpy("act", xp[:, 3:3 + T], ps[bank][:, :], [B_ps[bank]], [Bxp])
        cpy("act", st_conv[:, si, :], xp[:, T:T + 3], [Bxp], [B_stconv])
        ts("dve", out_f, xp[:, 0:T], pvec[:, cw0:cw0 + 1], pvec[:, cb:cb + 1], ALU.mult, ALU.add,
           [Bxp, B_pv], [out_buf])
        for j in range(1, 4):
            stt(out_f, xp[:, j:j + T], pvec[:, cw0 + j:cw0 + j + 1], out_f, ALU.mult, ALU.add,
                [Bxp, B_pv, out_buf], [out_buf])

    def run_interleaved(gens):
        live = [[g, 0.0, i] for i, g in enumerate(gens)]
        while live:
            item = min(live, key=lambda it: (it[1], it[2]))
            tk.last_fin = None
            try:
                next(item[0])
            except StopIteration:
                live.remove(item)
                continue
            if tk.last_fin is not None:
                item[1] = tk.last_fin

    B_ATm1 = Buf("ATm"); B_ktok1 = Buf("ktok"); B_Sb1 = Buf("Sbf")

    def chunk_attn(heads, vtok, qT, kT, kP, B_q, B_k, vcols, state_all, state_f, B_state, local, decb, dec, B_dec, obanks, dbanks=None, ubank=None):
        nh = len(heads)
        vs = {h: hi for hi, h in enumerate(heads)}
        Bst = [B_state[h] for h in heads]
        Bq = [B_q[h] for h in heads]
        Bk = list({id(B_k[h]): B_k[h] for h in heads}.values())
        Bd = list({id(B_dec[h]): B_dec[h] for h in heads}.values())
        Sb = Sbf[:, 0:nh, 0:vcols]
        ureg_all = ps[ubank][:, 0:nh * vcols].rearrange("p (h e) -> p h e", h=nh)
        if not local:
            fns = []
            for hi, h in enumerate(heads):
                fns.append(lambda e, hi=hi, h=h: e.matmul(ps[ubank][:, hi * vcols:(hi + 1) * vcols], ident_f, state_f[h],
                                                       start=(hi == 0), stop=(hi == nh - 1), skip_group_check=True))
            tk.mm(fns, [B_const] + Bst, [B_ps[ubank]])
        cpy("act", Sb, state_all, Bst, [B_Sb1])

        def stage1(j):
            cs = slice(j * L, (j + 1) * L)
            fns = []
            for hi, h in enumerate(heads):
                fns.append(lambda e, hi=hi, h=h: e.matmul(ps[6][:, hi * 64:(hi + 1) * 64], kP[h][:, j * L:j * L + 128], qT[h][:, cs], start=True, stop=True))
            tk.mm(fns, Bk + Bq, [B_ps[6]])
            fns = []
            for hi, h in enumerate(heads):
                fns.append(lambda e, hi=hi, h=h: e.matmul(ps[7][:, hi * 128:(hi + 1) * 128], kP[h][:, j * L:j * L + 128], identb[:, :], start=True, stop=True))
            tk.mm(fns, Bk + [B_const], [B_ps[7]])
            tt("dve", ATm[0:64, 0:nh, 0, :], ps[6][0:64, 0:nh * 64].rearrange("p (h t) -> p h t", h=nh), maskrep[:, 0:nh, :], ALU.mult,
               [B_ps[6], B_const], [B_ATm1])
            cpy("act", ktok[0:64, 0:nh, 0, :], ps[7][0:64, 0:nh * 128].rearrange("p (h e) -> p h e", h=nh), [B_ps[7]], [B_ktok1])

        def stage2_pe(j):
            cs = slice(j * L, (j + 1) * L)
            for hi, h in enumerate(heads):
                lst = [(ps[obanks[hi]][:, cs], vtok[:, hi, j, 0:128], ATm[:, hi, 0, :], True, False),
                       (ps[obanks[hi]][:, cs], Sbf[:, hi, 0:128], qT[h][:, cs], False, True)]
                mmg(lst, [B_vflat, B_ATm1, B_Sb1, B_q[h]], [B_ps[obanks[hi]]])
                if vcols == 256:
                    lst = [(ps[dbanks[hi]][:, cs], vtok[:, hi, j, 128:256], ATm[:, hi, 0, :], True, False),
                           (ps[dbanks[hi]][:, cs], Sbf[:, hi, 128:256], qT[h][:, cs], False, True)]
                    mmg(lst, [B_vflat, B_ATm1, B_Sb1, B_q[h]], [B_ps[dbanks[hi]]])
            fns = []
            for hi, h in enumerate(heads):
                if local:
                    fns.append(lambda e, hi=hi: e.matmul(ps[ubank][:, hi * vcols:(hi + 1) * vcols], ktok[:, hi, 0, :], vtok[:, hi, j, 0:vcols],
                                                        start=True, stop=True, skip_group_check=True))
                else:
                    fns.append(lambda e, hi=hi: e.matmul(ps[ubank][:, hi * vcols:(hi + 1) * vcols], ktok[:, hi, 0, :], vtok[:, hi, j, 0:vcols],
                                                        start=False, stop=(j == NCH - 1), skip_group_check=True))
            tk.mm(fns, [B_ktok1, B_vflat], [B_ps[ubank]])

        def stage2_state(j):
            if local:
                tt("dve", state_all, state_all, ureg_all, ALU.add, Bst + [B_ps[ubank]], Bst)
                if j < NCH - 1:
                    tt("dve", Sb, state_all, decb(j), ALU.mult, Bst + Bd, [B_Sb1])
                tt("dve", state_all, state_all, decb(j), ALU.mult, Bst + Bd, Bst)
            else:
                if j < NCH - 1:
                    cpy("act", Sb, ureg_all, [B_ps[ubank]], [B_Sb1])
                else:
                    for hi, h in enumerate(heads):
                        dcol = dec[h][:, T - 1:T]
                        ts("dve", state_f[h], ps[ubank][:, hi * vcols:(hi + 1) * vcols], dcol, None, ALU.mult, None,
                           [B_ps[ubank], B_dec[h]], [B_state[h]])

        stage1(0)
        for j in range(NCH):
            stage2_pe(j)
            if j + 1 < NCH:
                stage1(j + 1)
            stage2_state(j)

    pump()
    keepc = pvec[:, pv(0, "keep"):pv(0, "keep") + 1]
    flagb = pvec[:, pv(0, "flag_b"):pv(0, "flag_b") + 1]
    for step in range(NT + 1):
        t = min(step, NT - 1)
        tsl = slice(t * T, (t + 1) * T)
        tk.dma("sp", "d_x", xT[:, :, :], xT_d.rearrange("(kc p) s -> p kc s", p=128)[:, :, tsl], [], [B_x])
        if step >= 1:
            tk.dma("sp", "d_recv", acc[:, :, :], recv_d[0:D, :].rearrange("(kc p) s -> p kc s", p=128), [B_recv], B_acc)
            for kc in range(8):
                stt(xT[:, kc, :], acc[:, kc, :], flagb, xT[:, kc, :], ALU.mult, ALU.add, [B_acc[kc], B_pv, B_x], [B_x])
        if step == 1:
            ts("dve", st_conv[:, :, :], st_conv[:, :, :], keepc, None, ALU.mult, None, [B_stconv, B_pv], [B_stconv])
            ts("dve", st_lru[:, :], st_lru[:, :], keepc, None, ALU.mult, None, [B_stlru, B_pv], [B_stlru])
            for h in range(4):
                ts("dve", st_C[:, h, :], st_C[:, h, :], keepc, None, ALU.mult, None, [B_stC[0][h], B_pv], [B_stC[0][h]])
                ts("dve", st_H[:, h, :], st_H[:, h, :], keepc, None, ALU.mult, None, [B_stH[0][h], B_pv], [B_stH[0][h]])
                ts("dve", st_G[:, h, :], st_G[:, h, :], keepc, None, ALU.mult, None, [B_stG[0][h], B_pv], [B_stG[0][h]])

        for l in range(n_layers):
            rmsnorm_to_h(pv(l, "norm_g"))

            slotA, B_slotA = next_weight()
            slot, B_slot = next_weight()
            slot2, B_slot2 = next_weight()
            def gen_lru(slot=slotA, B_slot=B_slotA):
                for c in range(4):
                    b1 = ps_alloc(True)
                    proj_fm(slot, B_slot, c * 128, 128, b1)
                    yield
                    xa, Bxa = Ft[0], BF[0]
                    conv_fm(b1, l, 0, c, pv(l, "lru_cw", c * 4), pv(l, "lru_cb", c), xa[:, :], Bxa)
                    ps_free(b1)
                    yield
                    cpy("act", Bt[0][:, :], xa[:, :], [Bxa], [BB[0]])
                    yield
                    b2 = ps_alloc(True); b3 = ps_alloc(True)
                    mmg([(ps[b2][:, :], bd[:, l * 20 + c, :], Bt[0][:, :], True, True)], [B_const, BB[0]], [B_ps[b2]])
                    yield
                    mmg([(ps[b3][:, :], bd[:, l * 20 + 4 + c, :], Bt[0][:, :], True, True)], [B_const, BB[0]], [B_ps[b3]])
                    yield
                    r_, Br = Ft[1], BF[1]
                    i_, Bi = Ft[2], BF[2]
                    act(r_[:, :], ps[b2][:, :], AF.Tanh, [B_ps[b2], B_dc], [Br], bias=dcst[:, DC_HBA + c:DC_HBA + c + 1], scale=0.5)
                    ps_free(b2)
                    yield
                    act(i_[:, :], ps[b3][:, :], AF.Tanh, [B_ps[b3], B_dc], [Bi], bias=dcst[:, DC_HBX + c:DC_HBX + c + 1], scale=0.5)
                    ps_free(b3)
                    yield
                    a_, Ba = Ft[3], BF[3]
                    m_, Bm = Ft[4], BF[4]
                    act(a_[:, :], r_[:, :], AF.Exp, [Br, B_dc], [Ba], scale=dcst[:, DC_S1 + c:DC_S1 + c + 1], bias=dcst[:, DC_S1 + c:DC_S1 + c + 1])
                    yield
                    act(m_[:, :], r_[:, :], AF.Exp, [Br, B_dc], [Bm], scale=dcst[:, DC_S2 + c:DC_S2 + c + 1], bias=dcst[:, DC_S2 + c:DC_S2 + c + 1])
                    yield
                    act(m_[:, :], m_[:, :], AF.Ln, [Bm], [Bm], bias=1.0, scale=-1.0)
                    yield
                    act(m_[:, :], m_[:, :], AF.Exp, [Bm], [Bm], scale=0.5)
                    yield
                    stt(i_[:, :], i_[:, :], 1.0, xa[:, :], ALU.add, ALU.mult, [Bi, Bxa], [Bi])
                    yield
                    stt(i_[:, :], i_[:, :], 0.5, m_[:, :], ALU.mult, ALU.mult, [Bi, Bm], [Bi])
                    yield
                    hs, Bhs = Ft[5], BF[5]
                    sc = l * 4 + c
                    tk.op("dve", lambda e, sc=sc: e.tensor_tensor_scan(out=hs[:, :], data0=a_[:, :], data1=i_[:, :],
                                                                       initial=st_lru[:, sc:sc + 1], op0=ALU.mult, op1=ALU.add),
                          [Ba, Bi, B_stlru], [Bhs])
                    cpy("act", st_lru[:, sc:sc + 1], hs[:, T - 1:T], [Bhs], [B_stlru])
                    yield
                    b4 = ps_alloc(True)
                    proj_fm(slot, B_slot, 512 + c * 128, 128, b4)
                    yield
                    sz, Bsz = Ft[6], BF[6]
                    act(sz[:, :], ps[b4][:, :], AF.Tanh, [B_ps[b4]], [Bsz], scale=0.5)
                    yield
                    stt(sz[:, :], sz[:, :], 1.0, ps[b4][:, :], ALU.add, ALU.mult, [Bsz, B_ps[b4]], [Bsz])
                    ps_free(b4)
                    yield
                    stt(yT[:, 0 + c, :], sz[:, :], 0.5, hs[:, :], ALU.mult, ALU.mult, [Bhs, Bsz], [B_y[0 + c]])
                    yield

            def gen_mprep():
                for h in range(4):
                    b1 = ps_alloc(True)
                    proj_fm(slot, B_slot, h * 128, 128, b1)
                    yield
                    cpy("dve", mx_b[:, h, :], ps[b1][:, :], [B_ps[b1]], [B_mxb[h]])
                    yield
                    xc, Bxc = Ft[7], BF[7]
                    conv_fm(b1, l, 1, h, pv(l, "m_cw", h * 4), pv(l, "m_cb", h), xc[:, :], Bxc, xp=xpad2, Bxp=B_xpad2)
                    ps_free(b1)
                    yield
                    act(xm_f[:, h, :], xc[:, :], AF.Tanh, [Bxc], [B_xm[h]], scale=0.5)
                    yield
                    stt(xm_f[:, h, :], xm_f[:, h, :], 1.0, xc[:, :], ALU.add, ALU.mult, [B_xm[h], Bxc], [B_xm[h]])
                    yield
                    act(xm_b[:, h, :], xm_f[:, h, :], AF.Identity, [B_xm[h]], [B_xmb[h]], scale=0.5)
                    yield
                    for qi, (srcb, Bsrc) in enumerate([(xm_b, B_xmb), (xm_b, B_xmb), (mx_b, B_mxb)]):
                        b2 = ps_alloc(True)
                        mmg([(ps[b2][:, :], bd[:, l * 20 + 8 + qi * 4 + h, :], srcb[:, h, :], True, True)], [B_const, Bsrc[h]], [B_ps[b2]])
                        cpy("act" if qi != 1 else "dve", qkv_b[:, qi * 4 + h, :], ps[b2][:, :], [B_ps[b2]], [B_qkv[qi * 4 + h]])
                        ps_free(b2)
                        yield

            run_interleaved([gen_lru(), gen_mprep()])
            weights_done(1)

            bi_ = ps_alloc(); bf_ = ps_alloc()
            for gi, bank in ((0, bi_), (1, bf_)):
                lst = [(ps[bank][0:4, :], wif[:, (l * 2 + gi) * 12 + ci, :], qkv_b[:, ci, :], ci == 0, ci == 11) for ci in range(12)]
                mmg(lst, [B_const] + B_qkv, [B_ps[bank]])
            li, lf, G, eG, wk = g4[0], g4[1], g4[2], g4[3], g4[0]
            act(li[:, :], ps[bi_][0:4, :], AF.Identity, [B_ps[bi_], B_const], [B_g4[0]], bias=gb[:, l * 2:l * 2 + 1])
            act(lf[:, :], ps[bf_][0:4, :], AF.Exp, [B_ps[bf_], B_dc], [B_g4[1]], bias=gbn[:, 0:1], scale=-1.0)
            act(lf[:, :], lf[:, :], AF.Ln, [B_g4[1]], [B_g4[1]], bias=1.0)
            tk.op("dve", lambda e: e.tensor_tensor_scan(out=G[:, :], data0=ones4, data1=lf[:, :], initial=0.0,
                                                        op0=ALU.mult, op1=ALU.add), [B_g4[1], B_const], [B_g4[2]])
            act(eG[:, :], G[:, :], AF.Exp, [B_g4[2]], [B_g4[3]], scale=-1.0)
            tt("dve", wk[:, :], li[:, :], G[:, :], ALU.add, [B_g4[0], B_g4[2]], [B_g4[0]])
            act(wk[:, :], wk[:, :], AF.Exp, [B_g4[0]], [B_g4[0]])
            GO = acc[:, 4:8, :]; B_go = B_acc[4:8]
            for h in range(4):
                b9 = ps_alloc()
                proj_fm(slot, B_slot, 512 + h * 128, 128, b9)
                act(GO[:, h, :], ps[b9][:, :], AF.Tanh, [B_ps[b9]], [B_go[h]], scale=0.5)
                b12 = ps_alloc()
                proj_fm(slot2, B_slot2, h * 128, 128, b12)
                act(GZ[:, h, :], ps[b12][:, :], AF.Tanh, [B_ps[b12]], [B_gz[h]], scale=0.5)
                stt(GZ[:, h, :], GZ[:, h, :], 1.0, ps[b12][:, :], ALU.add, ALU.mult, [B_gz[h], B_ps[b12]], [B_gz[h]])
            weights_done(2)
            for pair in range(2):
                hp = [pair * 2, pair * 2 + 1]
                qT_ = {}; kT_ = {}; kP_ = {}; Bq_ = {}; Bk_ = {}; dec_ = {}; Bdec_ = {}
                tk.op("pool", lambda e: e.memset(vtok2[0:64, :, :, 128:256], 1.0), [], [B_vflat])
                for hi, h in enumerate(hp):
                    for half in range(2):
                        b3 = ps_alloc()
                        fns = []
                        for jj in range(4):
                            j = half * 4 + jj
                            fns.append(lambda e, j=j, jj=jj, b3=b3, h=h: e.matmul(ps[b3][:, jj * 128:(jj + 1) * 128], mx_bP[:, h, j * L:j * L + 128],
                                                                                 bd[:, l * 20 + 16 + h, :], start=True, stop=True))
                        tk.mm(fns, [B_mxb[h], B_const], [B_ps[b3]])
                        cpy("act", vtok2[0:64, hi, half * 4:half * 4 + 4, 0:128], ps[b3][0:64, :].rearrange("p (j e) -> p j e", j=4),
                            [B_ps[b3]], [B_vflat])
                    b5 = ps_alloc(); b6 = ps_alloc()
                    mmg([(ps[b5][:, :], sel[:, h, :], eG[:, :], True, True)], [B_const, B_g4[3]], [B_ps[b5]])
                    mmg([(ps[b6][:, :], sel[:, h, :], wk[:, :], True, True)], [B_const, B_g4[0]], [B_ps[b6]])
                    eGb, BeGb = Ft[6 + hi], BF[6 + hi]
                    wkb, Bwkb = Ft[8 + hi], BF[8 + hi]
                    cpy("act", eGb[:, :], ps[b5][:, :], [B_ps[b5]], [BeGb])
                    cpy("act", wkb[:, :], ps[b6][:, :], [B_ps[b6]], [Bwkb])
                    b7 = ps_alloc(); b8 = ps_alloc()
                    mmg([(ps[b7][:, :], bd[:, l * 20 + 8 + h, :], xm_b[:, h, :], True, True)], [B_const, B_xmb[h]], [B_ps[b7]])
                    mmg([(ps[b8][:, :], bd[:, l * 20 + 12 + h, :], xm_b[:, h, :], True, True)], [B_const, B_xmb[h]], [B_ps[b8]])
                    stt(Bt[1 + hi][:, :], ps[b7][:, :], 128.0 ** -0.5, eGb[:, :], ALU.mult, ALU.mult, [B_ps[b7], BeGb], [BB[1 + hi]])
                    tt("dve", Bt[3 + hi][:, :], ps[b8][:, :], wkb[:, :], ALU.mult, [B_ps[b8], Bwkb], [BB[3 + hi]])
                    qT_[h] = Bt[1 + hi]; kT_[h] = Bt[3 + hi]; kP_[h] = BtP[3 + hi]; Bq_[h] = BB[1 + hi]; Bk_[h] = BB[3 + hi]
                    dec_[h] = eGb; Bdec_[h] = BeGb
                held = ps_hold(5)
                ob = held[0:2]; db = held[2:4]; ub = held[4]
                sfl = {h: st_C[:, l * 4 + h, :] for h in hp}
                chunk_attn(hp, vtok2, qT_, kT_, kP_, Bq_, Bk_, 256, st_C[:, l * 4 + pair * 2:l * 4 + pair * 2 + 2, :], sfl, {h: B_stC[l][h] for h in hp}, False, None, dec_, Bdec_, ob, db, ub)
                for hi, h in enumerate(hp):
                    dn, Bdn = Ft[hi], BF[hi]
                    ts("dve", dn[:, :], ps[db[hi]][:, :], -1.0, 1.0, ALU.mult, ALU.max, [B_ps[db[hi]]], [Bdn])
                    tt("dve", dn[:, :], dn[:, :], ps[db[hi]][:, :], ALU.max, [Bdn, B_ps[db[hi]]], [Bdn])
                for hi, h in enumerate(hp):
                    act(Ft[hi][:, :], Ft[hi][:, :], AF.Ln, [BF[hi]], [BF[hi]])
                for hi, h in enumerate(hp):
                    act(Ft[hi][:, :], Ft[hi][:, :], AF.Exp, [BF[hi]], [BF[hi]], scale=-1.0)
                for hi, h in enumerate(hp):
                    stt(Ft[10 + hi][:, :], ps[ob[hi]][:, :], 0.5, Ft[hi][:, :], ALU.mult, ALU.mult, [B_ps[ob[hi]], BF[hi]], [BF[10 + hi]])
                ps_release(held)

                def gen_post(hi, h):
                    hm, Bhm = Ft[10 + hi], BF[10 + hi]
                    base = 2 + 4 * hi
                    stt(hm[:, :], GO[:, h, :], 1.0, hm[:, :], ALU.add, ALU.mult, [Bhm, B_go[h]], [Bhm])
                    yield
                    b10 = ps_alloc(True)
                    stats_bcast([hm[:, :]], [Bhm], b10)
                    yield
                    xcn, Bxcn = Ft[base], BF[base]
                    stt(xcn[:, :], ps[b10][:, :], -1.0 / 128.0, hm[:, :], ALU.mult, ALU.add, [B_ps[b10], Bhm], [Bxcn])
                    ps_free(b10)
                    yield
                    sq, Bsq = Ft[base + 1], BF[base + 1]
                    act(sq.bitcast(BF16)[:, 0:T], xcn[:, :], AF.Square, [Bxcn], [Bsq])
                    yield
                    b11 = ps_alloc(True)
                    stats_bcast([sq.bitcast(BF16)[:, 0:T]], [Bsq], b11, bf=True)
                    yield
                    rs, Brs = Ft[base + 2], BF[base + 2]
                    act(rs[:, :], ps[b11][:, :], AF.Ln, [B_ps[b11]], [Brs], bias=EPS, scale=1.0 / 128)
                    ps_free(b11)
                    yield
                    act(rs[:, :], rs[:, :], AF.Exp, [Brs], [Brs], scale=-0.5)
                    yield
                    tt("dve", xcn[:, :], xcn[:, :], rs[:, :], ALU.mult, [Bxcn, Brs], [Bxcn])
                    yield
                    sk, Bsk = Ft[base + 3], BF[base + 3]
                    act(sk[:, :], xm_f[:, h, :], AF.Identity, [B_xm[h], B_dc], [Bsk], scale=dcst[:, DC_HSK + h:DC_HSK + h + 1])
                    yield
                    stt(xcn[:, :], xcn[:, :], pvec[:, pv(l, "m_nw", h):pv(l, "m_nw", h) + 1], sk[:, :], ALU.mult, ALU.add,
                        [Bxcn, B_pv, Bsk], [Bxcn])
                    yield
                    stt(yT[:, 4 + h, :], GZ[:, h, :], 0.5, xcn[:, :], ALU.mult, ALU.mult, [Bxcn, B_gz[h]], [B_y[4 + h]])
                    yield
                run_interleaved([gen_post(hi, h) for hi, h in enumerate(hp)])

            slot, B_slot = next_weight()
            slot2, B_slot2 = next_weight()
            qT_ = {}; kT_ = {}; kP_ = {}; Bq_ = {}; Bk_ = {}; dec_ = {}; Bdec_ = {}
            def gen_hprep(heads_, tset):
                F_f, F_lg, F_G, F_en, F_qs, F_kk = tset
                for h in heads_:
                    bfk = ps_alloc(True)
                    proj_fm(slot, B_slot, 512 + h * 128, 128, bfk)
                    yield
                    f_, Bf_ = Ft[F_f], BF[F_f]
                    act(f_[:, :], ps[bfk][:, :], AF.Tanh, [B_ps[bfk]], [Bf_], scale=0.5)
                    ps_free(bfk)
                    yield
                    ts("dve", f_[:, :], f_[:, :], dcst[:, DC_C1 + h:DC_C1 + h + 1], dcst[:, DC_C0 + h:DC_C0 + h + 1],
                       ALU.mult, ALU.add, [Bf_, B_dc], [Bf_])
                    yield
                    lg, Blg = Ft[F_lg], BF[F_lg]
                    act(lg[:, :], f_[:, :], AF.Ln, [Bf_], [Blg])
                    yield
                    Gc, BGc = Ft[F_G], BF[F_G]
                    tk.op("dve", lambda e: e.tensor_tensor_scan(out=Gc[:, :], data0=rmask, data1=lg[:, :], initial=0.0, op0=ALU.mult, op1=ALU.add),
                          [Blg, B_const], [BGc])
                    yield
                    eGc, BeGc = Ft[3 + h], BF[3 + h]
                    act(eGc[:, :], Gc[:, :], AF.Exp, [BGc], [BeGc])
                    yield
                    enG, BenG = Ft[F_en], BF[F_en]
                    act(enG[:, :], Gc[:, :], AF.Exp, [BGc], [BenG], scale=-1.0)
                    yield
                    bq = ps_alloc(True)
                    proj_fm(slot, B_slot, h * 128, 128, bq)
                    yield
                    qs, Bqs = Ft[F_qs], BF[F_qs]
                    act(qs[:, :], ps[bq][:, :], AF.Tanh, [B_ps[bq]], [Bqs], scale=0.5)
                    yield
                    stt(qs[:, :], qs[:, :], 1.0, ps[bq][:, :], ALU.add, ALU.mult, [Bqs, B_ps[bq]], [Bqs])
                    ps_free(bq)
                    yield
                    stt(Bt[h][:, :], qs[:, :], 0.5 * 128.0 ** -0.5, eGc[:, :], ALU.mult, ALU.mult, [Bqs, BeGc], [BB[h]])
                    yield
                    kk, Bkk = Ft[F_kk], BF[F_kk]
                    ts("dve", kk[:, :], f_[:, :], -1.0, 1.0, ALU.mult, ALU.add, [Bf_], [Bkk])
                    yield
                    tt("dve", Bt[4 + h][:, :], kk[:, :], enG[:, :], ALU.mult, [Bkk, BenG], [BB[4 + h]])
                    yield
                    qT_[h] = Bt[h]; kT_[h] = Bt[4 + h]; kP_[h] = BtP[4 + h]; Bq_[h] = BB[h]; Bk_[h] = BB[4 + h]
                    dec_[h] = eGc; Bdec_[h] = BeGc

            def gen_hvz():
                for h in range(4):
                    b12 = ps_alloc(True)
                    proj_fm(slot2, B_slot2, 512 + h * 128, 128, b12)
                    yield
                    act(GZ[:, h, :], ps[b12][:, :], AF.Tanh, [B_ps[b12]], [B_gz[h]], scale=0.5)
                    yield
                    stt(GZ[:, h, :], GZ[:, h, :], 1.0, ps[b12][:, :], ALU.add, ALU.mult, [B_gz[h], B_ps[b12]], [B_gz[h]])
                    ps_free(b12)
                    yield
                for j in range(NCH):
                    b3 = ps_alloc(True)
                    lst = [(ps[b3][:, :], hT[:, kc, j * L:j * L + 128], slot2[:, kc, 0:512], kc == 0, kc == 7) for kc in range(8)]
                    mmg(lst, [B_h, B_slot2], [B_ps[b3]])
                    yield
                    cpy("act", vtok4[0:64, :, j, :], ps[b3][0:64, :].rearrange("p (h e) -> p h e", h=4), [B_ps[b3]], [B_vflat])
                    ps_free(b3)
                    yield

            run_interleaved([gen_hprep([0, 2], (0, 1, 2, 7, 8, 9)), gen_hprep([1, 3], (10, 11, 12, 13, 14, 15)), gen_hvz()])
            weights_done(2)
            held = ps_hold(5)
            ob = held[0:4]; ub = held[4]
            sfl = {h: st_H[:, l * 4 + h, :] for h in range(4)}
            chunk_attn([0, 1, 2, 3], vtok4, qT_, kT_, kP_, Bq_, Bk_, 128, st_H[:, l * 4:l * 4 + 4, :], sfl, {h: B_stH[l][h] for h in range(4)}, True,
                       lambda j: EG4[:, :, j * L + L - 1:j * L + L].to_broadcast([128, 4, 128]), dec_, Bdec_, ob, None, ub)
            for h in range(4):
                cpy("act", Ft[10 + h][:, :], ps[ob[h]][:, :], [B_ps[ob[h]]], [BF[10 + h]])
            ps_release(held)
            pb_ = []
            for h in range(4):
                act(bfv(h), Ft[10 + h][:, :], AF.Square, [BF[10 + h]], [BF[h]])
                b10 = ps_alloc(); pb_.append(b10)
                stats_bcast([bfv(h)], [BF[h]], b10, bf=True)
            for h in range(4):
                act(Ft[h][:, :], ps[pb_[h]][:, :], AF.Ln, [B_ps[pb_[h]]], [BF[h]], bias=EPS, scale=1.0 / 128)
            for h in range(4):
                act(Ft[h][:, :], Ft[h][:, :], AF.Exp, [BF[h]], [BF[h]], scale=-0.5)
            for h in range(4):
                o_, Bo_ = Ft[10 + h], BF[10 + h]
                stt(o_[:, :], o_[:, :], pvec[:, pv(l, "h_nw"):pv(l, "h_nw") + 1], Ft[h][:, :], ALU.mult, ALU.mult, [Bo_, B_pv, BF[h]], [Bo_])
                stt(yT[:, 8 + h, :], GZ[:, h, :], 0.5, o_[:, :], ALU.mult, ALU.mult, [Bo_, B_gz[h]], [B_y[8 + h]])

            slot, B_slot = next_weight()
            slot2, B_slot2 = next_weight()
            bl = ps_alloc()
            proj_fm(slot2, B_slot2, 0, 16, bl)
            cpy("act", glr[:, :], ps[bl][0:16, :], [B_ps[bl]], [B_glr])
            qT_ = {}; kT_ = {}; kP_ = {}; Bq_ = {}; Bk_ = {}; dec_ = {}; Bdec_ = {}
            def gen_gprep(cc, tset):
                F_lg, F_G, F_en, F_qe = tset
                bg = ps_alloc(True)
                mmg([(ps[bg][:, :], lr2[:, l, cc * 128:(cc + 1) * 128], glr[:, :], True, True)], [B_const, B_glr], [B_ps[bg]])
                yield
                lg, Blg = Ft[F_lg], BF[F_lg]
                act(lg[:, :], ps[bg][:, :], AF.Exp, [B_ps[bg], B_dc], [Blg], bias=dcst[:, DC_NB2 + cc:DC_NB2 + cc + 1], scale=-1.0)
                ps_free(bg)
                yield
                act(lg[:, :], lg[:, :], AF.Ln, [Blg], [Blg], bias=1.0)
                yield
                Gc, BGc = Ft[F_G], BF[F_G]
                tk.op("dve", lambda e: e.tensor_tensor_scan(out=Gc[:, :], data0=ones_T, data1=lg[:, :], initial=0.0, op0=ALU.mult, op1=ALU.add),
                      [Blg, B_const], [BGc])
                yield
                eGc, BeGc = Ft[2 + cc], BF[2 + cc]
                act(eGc[:, :], Gc[:, :], AF.Exp, [BGc], [BeGc], scale=-1.0 / 16.0)
                yield
                enG, BenG = Ft[F_en], BF[F_en]
                act(enG[:, :], Gc[:, :], AF.Exp, [BGc], [BenG], scale=1.0 / 16.0)
                yield
                bq = ps_alloc(True)
                proj_fm(slot, B_slot, cc * 128, 128, bq)
                yield
                qe, Bqe = Ft[F_qe], BF[F_qe]
                stt(qe[:, :], ps[bq][:, :], 64.0 ** -0.5, eGc[:, :], ALU.mult, ALU.mult, [B_ps[bq], BeGc], [Bqe])
                ps_free(bq)
                yield
                for hh in range(2):
                    h = cc * 2 + hh
                    ts("dve", Bt[h][:, :], qe[:, :], rowmask[:, hh:hh + 1], None, ALU.mult, None, [Bqe, B_const], [BB[h]])
                    yield
                    qT_[h] = Bt[h]; Bq_[h] = BB[h]
                    kT_[h] = Bt[4 + cc]; kP_[h] = BtP[4 + cc]; Bk_[h] = BB[4 + cc]
                    dec_[h] = eGc; Bdec_[h] = BeGc
                bk = ps_alloc(True)
                proj_fm(slot, B_slot, 256 + cc * 128, 128, bk)
                yield
                tt("dve", Bt[4 + cc][:, :], ps[bk][:, :], enG[:, :], ALU.mult, [B_ps[bk], BenG], [BB[4 + cc]])
                ps_free(bk)
                yield

            def gen_gvz():
                for h in range(4):
                    b12 = ps_alloc(True)
                    proj_fm(slot2, B_slot2, 16 + h * 128, 128, b12)
                    yield
                    act(GZ[:, h, :], ps[b12][:, :], AF.Tanh, [B_ps[b12]], [B_gz[h]], scale=0.5)
                    yield
                    stt(GZ[:, h, :], GZ[:, h, :], 1.0, ps[b12][:, :], ALU.add, ALU.mult, [B_gz[h], B_ps[b12]], [B_gz[h]])
                    ps_free(b12)
                    yield
                for j in range(NCH):
                    b3 = ps_alloc(True)
                    lst = [(ps[b3][:, :], hT[:, kc, j * L:j * L + 128], slot[:, kc, 512:1024], kc == 0, kc == 7) for kc in range(8)]
                    mmg(lst, [B_h, B_slot], [B_ps[b3]])
                    yield
                    cpy("act", vtok4[0:64, :, j, :], ps[b3][0:64, :].rearrange("p (h e) -> p h e", h=4), [B_ps[b3]], [B_vflat])
                    ps_free(b3)
                    yield

            run_interleaved([gen_gprep(0, (0, 1, 4, 5)), gen_gprep(1, (6, 7, 8, 9)), gen_gvz()])
            issue_brh(l, 0, 0)
            issue_brh(l, 0, 1)
            weights_done(2)
            held = ps_hold(5)
            ob = held[0:4]; ub = held[4]
            sfl = {h: st_G[:, l * 4 + h, :] for h in range(4)}
            chunk_attn([0, 1, 2, 3], vtok4, qT_, kT_, kP_, Bq_, Bk_, 128, st_G[:, l * 4:l * 4 + 4, :], sfl, {h: B_stG[l][h] for h in range(4)}, False, None, dec_, Bdec_, ob, None, ub)
            for h in range(4):
                cpy("act", Ft[10 + h][:, :], ps[ob[h]][:, :], [B_ps[ob[h]]], [BF[10 + h]])
            ps_release(held)
            pb_ = []
            for h in range(4):
                act(bfv(h), Ft[10 + h][:, :], AF.Square, [BF[10 + h]], [BF[h]])
                b10 = ps_alloc(); pb_.append(b10)
                stats_bcast([bfv(h)], [BF[h]], b10, bf=True)
            for h in range(4):
                act(Ft[h][:, :], ps[pb_[h]][:, :], AF.Ln, [B_ps[pb_[h]]], [BF[h]], bias=EPS, scale=1.0 / 128)
            for h in range(4):
                act(Ft[h][:, :], Ft[h][:, :], AF.Exp, [BF[h]], [BF[h]], scale=-0.5)
            for h in range(4):
                o_, Bo_ = Ft[10 + h], BF[10 + h]
                stt(o_[:, :], o_[:, :], pvec[:, pv(l, "g_nw"):pv(l, "g_nw") + 1], Ft[h][:, :], ALU.mult, ALU.mult, [Bo_, B_pv, BF[h]], [Bo_])
                stt(yT[:, 12 + h, :], GZ[:, h, :], 0.5, o_[:, :], ALU.mult, ALU.mult, [Bo_, B_gz[h]], [B_y[12 + h]])

            for n in range(4):
                slot, B_slot = next_weight()
                for dc in range(8):
                    bgate = ps_alloc(); bpr = ps_alloc()
                    proj_fm(slot, B_slot, dc * 128, 128, bgate)
                    hb = dc // 4
                    lst = [(ps[bpr][:, :], brh[hb][:, wc, (dc % 4) * 128:(dc % 4 + 1) * 128], yT[:, n * 4 + wc, :], wc == 0, wc == 3) for wc in range(4)]
                    mmg(lst, B_brh[hb] + B_y[n * 4:n * 4 + 4], [B_ps[bpr]])
                    if n < 3 and dc % 4 == 3:
                        issue_brh(l, n + 1, hb)
                    sg, Bsg = Ft[dc % 2], BF[dc % 2]
                    act(sg[:, :], ps[bgate][:, :], AF.Tanh, [B_ps[bgate]], [Bsg], scale=0.5)
                    if n == 0:
                        stt(acc[:, dc, :], sg[:, :], 1.0, ps[bpr][:, :], ALU.add, ALU.mult, [B_ps[bpr], Bsg], [B_acc[dc]])
                    else:
                        pr, Bpr = Ft[2 + dc % 2], BF[2 + dc % 2]
                        stt(pr[:, :], sg[:, :], 1.0, ps[bpr][:, :], ALU.add, ALU.mult, [B_ps[bpr], Bsg], [Bpr])
                        if n < 3:
                            tt("pool", acc[:, dc, :], acc[:, dc, :], pr[:, :], ALU.add, [B_acc[dc], Bpr], [B_acc[dc]])
                        else:
                            tt("pool", mg[:, dc, 0:T], acc[:, dc, :], pr[:, :], ALU.add, [B_acc[dc], Bpr], [B_mgc[dc]])
                weights_done(1, defer=(n == 3 and step < NT))
            slot, B_slot = next_weight()
            for ec in range(8):
                bo = ps_alloc()
                lst = [(ps[bo][:, :], slot[:, dc, ec * 128:(ec + 1) * 128], mg[:, dc, 0:T], dc == 0, dc == 7) for dc in range(8)]
                mmg(lst, [B_slot] + B_mgc, [B_ps[bo]])
                stt(xT[:, ec, :], ps[bo][:, :], 0.5, xT[:, ec, :], ALU.mult, ALU.add, [B_x, B_ps[bo]], [B_x])
            weights_done(1, defer=(step < NT))

        if step < NT:
            tk.dma("sp", "d_send", send_d.rearrange("(kc p) s -> p kc s", p=128), xT[:, :, :], [B_x], [B_send])
            tk.coll("pool", "cc", lambda e: e.collective_compute("AllGather", ALU.bypass, replica_groups=[[2 * i, 2 * i + 1] for i in range(n_pairs)],
                                                                ins=[send_d], outs=[recv_d]), [B_send], [B_recv])
            pump()
        if step >= 1:
            b = ps_alloc()
            for kc in range(8):
                act(bfv(kc % 2), xT[:, kc, :], AF.Square, [B_x], [BF[kc % 2]])
                tk.mm([lambda e, kc=kc, b=b: e.matmul(ps[b][:, :], ones_b[:, :], bfv(kc % 2), start=(kc == 0), stop=(kc == 7))],
                      [B_const, BF[kc % 2]], [B_ps[b]], cost=0.3)
            rstd_from(b, D, Ft[2][:, :], BF[2])
            for kc in range(8):
                stt(osb[:, kc, :], xT[:, kc, :], pvec[:, pv(0, "final_g") + kc:pv(0, "final_g") + kc + 1], Ft[2][:, :], ALU.mult, ALU.mult,
                    [B_x, B_pv, BF[2]], [B_acc[kc]])
            osl = slice((step - 1) * T, step * T)
            tk.dma("sp", "d_out", outT_d.rearrange("(kc p) s -> p kc s", p=128)[:, :, osl], osb[:, :, :], B_acc, [])

    tk.final_wait("sp", "d_out")

    def replay(en):
        def f(e):
            for (name, a, k, h) in streams[en]:
                ins = getattr(e, name)(*a, **k)
                for (pn, pa) in h.post:
                    getattr(ins, pn)(*pa)
        return f
    block.tensor(replay("pe"))
    block.scalar(replay("act"))
    block.vector(replay("dve"))
    block.gpsimd(replay("pool"))
    block.sync(replay("sp"))
    es.close()
    return nc, tk


def _host_pack(inputs, l, is_b):
    f = lambda k: np.asarray(inputs[k], dtype=np.float32)
    pvec = np.zeros((128, PV_COLS), np.float32)
    def put(name, vec, nchunks, stride=1, off=0):
        for c in range(nchunks):
            pvec[:, pv(0, name) + c * stride + off] = vec[c * 128:(c + 1) * 128]
    for j in range(4):
        put("lru_cw", f("lru_conv_w")[l, j], 4, stride=4, off=j)
        put("m_cw", f("m_conv_w")[l, j], 4, stride=4, off=j)
    put("lru_cb", f("lru_conv_b")[l], 4)
    put("lru_ba", f("lru_ba")[l], 4)
    put("lru_bx", f("lru_bx")[l], 4)
    put("lru_lam", f("lru_lambda")[l], 4)
    put("m_cb", f("m_conv_b")[l], 4)
    put("m_nw", f("m_norm_w")[l], 4)
    put("m_skip", f("m_skip")[l], 4)
    put("h_lbl", f("h_lb_logits")[0], 4)
    put("h_lbl1", f("h_lb_logits")[1], 4)
    put("h_nw", f("h_norm_w")[l], 1)
    put("g_b2", f("g_b_lr2")[l], 2)
    put("g_nw", f("g_norm_w")[l], 1)
    put("norm_g", f("norm_g")[l], 8)
    put("final_g", f("final_g"), 8)
    pvec[:, pv(0, "flag_b")] = 1.0 if is_b else 0.0
    pvec[:, pv(0, "keep")] = 0.0 if is_b else 1.0
    bd = np.zeros((128, 20, 128), np.float32)
    for gi, key in enumerate(["lru_wa", "lru_wx"]):
        w = f(key)[l]
        for c in range(4):
            for b in range(2):
                bd[b * 64:(b + 1) * 64, gi * 4 + c, b * 64:(b + 1) * 64] = w[2 * c + b]
    for gi, key in enumerate(["m_wq", "m_wk", "m_wv"]):
        w = f(key)[l]
        for h in range(4):
            for b in range(32):
                bd[b * 4:(b + 1) * 4, 8 + gi * 4 + h, b * 4:(b + 1) * 4] = w[32 * h + b]
    wif = np.zeros((128, 2 * 12, 4), np.float32)
    gb = np.zeros((4, 2), np.float32)
    for gi, key in enumerate(["m_wi", "m_wf"]):
        w = f(key)[l]
        for ci in range(12):
            wif[:, gi * 12 + ci, :] = w[ci * 128:(ci + 1) * 128, :]
    gb[:, 0] = f("m_bi")[l]
    gb[:, 1] = f("m_bf")[l]
    lr2 = np.ascontiguousarray(f("g_w_lr2")[l][:, None, :])
    cst = np.zeros((128, 1664), np.float32)
    cst[:, 0:128] = np.eye(128, dtype=np.float32)
    cst[:, 128:256] = 1.0
    cst[0:64, 256:320] = np.triu(np.ones((64, 64), np.float32))
    rm = np.ones((T,), np.float32); rm[::L] = 0.0
    cst[:, 320:832] = rm[None, :]
    cst[0:64, 832] = 1.0
    cst[64:128, 833] = 1.0
    for j in range(4):
        cst[j, 896 + j * 128:896 + (j + 1) * 128] = 1.0
        cst[0:64, 1408 + j * 64:1408 + (j + 1) * 64] = np.triu(np.ones((64, 64), np.float32))
    return dict(w_in=np.ascontiguousarray(f("w_in")[l:l + 1]), w_branch=np.ascontiguousarray(f("w_branch")[l:l + 1]),
                w_out=np.ascontiguousarray(f("w_out")[l:l + 1]), pvec=pvec, bd=bd, wif=wif, gbias=gb, lr2=lr2, cst=cst)


_PROG_CACHE = {}


def kernel(**inputs):
    x = np.asarray(inputs["x"], dtype=np.float32)
    B, S, _ = x.shape
    packs = [_host_pack(inputs, 0, False), _host_pack(inputs, 1, True)]
    if (S, B) not in _PROG_CACHE:
        _PROG_CACHE[(S, B)] = build_program(S, n_pairs=B)[0]
    nc = _PROG_CACHE[(S, B)]
    zeros = np.zeros((D, S), np.float32)
    in_maps = []
    for b in range(B):
        ma = dict(packs[0]); ma["xT"] = np.ascontiguousarray(x[b].T)
        mb = dict(packs[1]); mb["xT"] = zeros
        in_maps += [ma, mb]
    res = run_bass_kernel_spmd(nc, in_maps, core_ids=list(range(2 * B)))
    out = np.stack([np.ascontiguousarray(res.results[2 * b + 1]["outT"].T) for b in range(B)], axis=0)
    return out.astype(np.float32)
```

```python
import numpy as np
import concourse.bass as bass
import concourse.mybir as mybir
from concourse.bass_utils import run_bass_kernel_spmd

F32 = mybir.dt.float32
BF16 = mybir.dt.bfloat16
AF = mybir.ActivationFunctionType
ALU = mybir.AluOpType

D = 1024
W = 512
D_IN = 10256
DEPTH = 2
T = 512
L = 64
NCH = T // L
EPS = 1e-6
C_LRUX, C_LRUZ = 0, 512
C_MX, C_MO, C_MZ = 1024, 1536, 2048
C_HQ, C_HF, C_HI, C_HZ = 2560, 3072, 3584, 4096
C_GQ, C_GK, C_GV, C_GLR, C_GZ = 4608, 4864, 5120, 5632, 5648
C_MERGE = 6160

PV_PER_LAYER = 96
def pv(l, name, c=0):
    base = l * PV_PER_LAYER
    table = {
        "lru_cw": 0,
        "lru_cb": 16,
        "lru_ba": 20,
        "lru_bx": 24,
        "lru_lam": 28,
        "m_cw": 32,
        "m_cb": 48,
        "m_nw": 52,
        "m_skip": 56,
        "h_lbl": 60,
        "h_nw": 64,
        "g_b2": 65,
        "g_nw": 67,
        "norm_g": 68,
        "final_g": 76,
        "flag_b": 84,
        "keep": 85,
        "h_lbl1": 86,
    }
    return base + table[name] + c
SAME_ENGINE_WINDOW = 3
PL = 1
PV_COLS = PL * PV_PER_LAYER


class Buf:
    __slots__ = ("w", "r", "name", "excl")
    def __init__(self, name="", excl=False):
        self.w = None
        self.r = []
        self.name = name
        self.excl = excl


class Trk:
    def __init__(self, nc, engs, sems):
        self.nc = nc
        self.E = engs
        self.S = sems
        self.tick = {k: 0 for k in engs}
        self.waited = {k: {} for k in engs}
        self.dmaval = {}
        self.nwait = 0
        self.ninst = 0
        self.efree = {k: 0.0 for k in engs}
        self.fin = {}
        self.last_fin = None
        self.COST = {"act": 0.6, "dve": 0.65, "pool": 1.6, "pe": 0.27, "sp": 0.1}

    def _time(self, en, cost, ndma=None):
        ready = 0.0
        for (kind, key), val in self._need.items():
            t = self.fin.get((kind, key, val), 0.0)
            if kind == "e" and key != en:
                t += 0.15
            if t > ready:
                ready = t
        start = max(self.efree[en], ready)
        fin = start + cost
        self.efree[en] = fin
        return fin

    def _deps(self, en, reads, writes):
        need = {}
        def add(dep):
            if dep is None:
                return
            kind, key, val = dep
            k = (kind, key)
            if need.get(k, 0) < val:
                need[k] = val
        for b in reads:
            add(b.w)
            if b.excl:
                for r in b.r:
                    add(r)
        for b in writes:
            add(b.w)
            for r in b.r:
                add(r)
        out = []
        self._need = need
        for (kind, key), val in need.items():
            if kind == "e" and key == en:
                if en == "pe":
                    continue
                if en != "pool" and val <= self.tick[en] - SAME_ENGINE_WINDOW:
                    continue
            if self.waited[en].get((kind, key), 0) >= val:
                continue
            self.waited[en][(kind, key)] = val
            out.append((self.S[key], val))
        return out

    def op(self, en, fn, reads=(), writes=(), cost=None):
        eng = self.E[en]
        waits = self._deps(en, reads, writes)
        tfin = self._time(en, self.COST[en] if cost is None else cost)
        for (s, v) in waits[1:]:
            eng.wait_ge(s, v)
            self.nwait += 1
        ins = fn(eng)
        if waits:
            ins._wait_ge(waits[0][0], waits[0][1])
        self.tick[en] += 1
        ins.then_inc(self.S[en], 1)
        me = ("e", en, self.tick[en])
        self.fin[me] = tfin
        self.last_fin = tfin
        for b in reads:
            if b.excl:
                b.w = me
                b.r = []
            else:
                b.r.append(me)
        for b in writes:
            b.w = me
            b.r = []
        self.ninst += 1
        return ins

    def mm(self, fns, reads=(), writes=(), cost=None):
        en = "pe"
        eng = self.E[en]
        waits = self._deps(en, reads, writes)
        tfin = self._time(en, (0.1 * len(fns)) if cost is None else cost)
        for (s, v) in waits[1:]:
            eng.wait_ge(s, v)
            self.nwait += 1
        ins = None
        for i, fn in enumerate(fns):
            ins = fn(eng)
            if i == 0 and waits:
                ins._wait_ge(waits[0][0], waits[0][1])
        self.tick[en] += 1
        ins.then_inc(self.S[en], 1)
        me = ("e", en, self.tick[en])
        self.fin[me] = tfin
        self.last_fin = tfin
        for b in reads:
            if b.excl:
                b.w = me
                b.r = []
            else:
                b.r.append(me)
        for b in writes:
            b.w = me
            b.r = []
        self.ninst += len(fns)

    def dma(self, en, semkey, out, in_, reads=(), writes=()):
        eng = self.E[en]
        waits = self._deps(en, reads, writes)
        for (s, v) in waits:
            eng.wait_ge(s, v)
            self.nwait += 1
        ins = eng.dma_start(out=out, in_=in_)
        self.dmaval[semkey] = self.dmaval.get(semkey, 0) + 16
        ins.then_inc(self.S[semkey], 16)
        me = ("d", semkey, self.dmaval[semkey])
        self.fin[me] = self._time(en, 0.1) + 12.0
        for b in reads:
            if b.excl:
                b.w = me
                b.r = []
            else:
                b.r.append(me)
        for b in writes:
            b.w = me
            b.r = []
        self.ninst += 1

    def coll(self, en, semkey, fn, reads=(), writes=()):
        eng = self.E[en]
        waits = self._deps(en, reads, writes)
        for (s, v) in waits:
            eng.wait_ge(s, v)
            self.nwait += 1
        ins = fn(eng)
        self.dmaval[semkey] = self.dmaval.get(semkey, 0) + 1
        ins.then_inc(self.S[semkey], 1)
        me = ("d", semkey, self.dmaval[semkey])
        self.fin[me] = self._time(en, 0.1) + 35.0
        for b in reads:
            if b.excl:
                b.w = me
                b.r = []
            else:
                b.r.append(me)
        for b in writes:
            b.w = me
            b.r = []
        self.ninst += 1

    def final_wait(self, en, semkey):
        self.E[en].wait_ge(self.S[semkey], self.dmaval[semkey])


def build_program(S, n_layers=PL, n_pairs=4):
    assert S % T == 0
    NT = S // T
    nc = bass.Bass("TRN2", target_bir_lowering=False)

    xT_d = nc.dram_tensor("xT", [D, S], F32, kind="ExternalInput").ap()
    w_in_d = nc.dram_tensor("w_in", [PL, D, D_IN], F32, kind="ExternalInput").ap()
    w_br_d = nc.dram_tensor("w_branch", [PL, 4, W, D], F32, kind="ExternalInput").ap()
    w_out_d = nc.dram_tensor("w_out", [PL, D, D], F32, kind="ExternalInput").ap()
    pvec_d = nc.dram_tensor("pvec", [128, PV_COLS], F32, kind="ExternalInput").ap()
    bd_d = nc.dram_tensor("bd", [128, PL * 20, 128], F32, kind="ExternalInput").ap()
    wif_d = nc.dram_tensor("wif", [128, PL * 2 * 12, 4], F32, kind="ExternalInput").ap()
    gb_d = nc.dram_tensor("gbias", [4, PL * 2], F32, kind="ExternalInput").ap()
    lr2_d = nc.dram_tensor("lr2", [16, PL, 256], F32, kind="ExternalInput").ap()
    cst_d = nc.dram_tensor("cst", [128, 1664], F32, kind="ExternalInput").ap()
    outT_d = nc.dram_tensor("outT", [D, S], F32, kind="ExternalOutput").ap()
    send_d = nc.dram_tensor("send", [D, T], F32, kind="Internal").ap()
    recv_d = nc.dram_tensor("recv", [2 * D, T], F32, kind="Internal").ap()
    B_send = Buf("send"); B_recv = Buf("recv")

    from contextlib import ExitStack
    es = ExitStack()
    sb = lambda name, shape, dt: es.enter_context(nc.sbuf_tensor(name, shape, dt))

    xT = sb("xT_s", [128, 8, T], F32); B_x = Buf("xT")
    hT = sb("hT_s", [128, 8, T + L], BF16); B_h = Buf("hT")
    yT = sb("yT_s", [128, 16, T], BF16); B_y = [Buf("y%d" % i) for i in range(16)]
    acc = sb("acc_s", [128, 8, T], F32); B_acc = [Buf("acc%d" % i) for i in range(8)]
    mg = sb("mg_s", [128, 8, T + L], BF16); B_mgc = [Buf("mg%d" % i) for i in range(8)]
    NSLOT = 3
    ring = [sb("ring%d" % i, [128, 8, 1024], BF16) for i in range(NSLOT)]
    B_ring = [Buf("ring%d" % i) for i in range(NSLOT)]
    pvec = sb("pvec_s", [128, PV_COLS], F32); B_pv = Buf("pvec")
    dcst = sb("dcst_s", [128, 40], F32)
    gbn = sb("gbn_s", [4, 1], F32)
    bd = sb("bd_s", [128, PL * 20, 128], BF16)
    wif = sb("wif_s", [128, PL * 2 * 12, 4], BF16)
    gb = sb("gb_s", [4, PL * 2], F32)
    lr2 = sb("lr2_s", [16, PL, 256], BF16)
    cst = sb("cst_s", [128, 896], F32)
    identb = sb("identb_s", [128, 128], BF16)
    ones_b = sb("ones_b_s", [128, 128], BF16)
    B_const = Buf("const")
    ident_f = cst[:, 0:128]
    ones_f = cst[:, 128:256]
    maskT = cst[0:64, 256:320]
    rmask = cst[:, 320:832]
    rowmask = cst[:, 832:834]
    sel = sb("sel_s", [4, 4, 128], F32)
    maskrep = sb("maskrep_s", [64, 4, L], F32)
    onesT = sb("onesT_s", [128, T], F32)
    ones_T = onesT[:, :]
    ones4 = onesT[0:4, :]

    NF = 16
    EG4 = sb("EG4_s", [128, 4, T], F32)
    FB4 = sb("FB4_s", [128, 4, T], F32)
    _fmap = {7: 0, 8: 1, 9: 2, 14: 3}
    Ft = [EG4[:, i - 3, :] if 3 <= i <= 6 else (FB4[:, _fmap[i], :] if i in _fmap else sb("F%d" % i, [128, T], F32)) for i in range(NF)]
    brh = [FB4[:, 0:2, :].bitcast(BF16).rearrange("p a (b c) -> p (a b) c", b=2),
           FB4[:, 2:4, :].bitcast(BF16).rearrange("p a (b c) -> p (a b) c", b=2)]
    BF = [Buf("F%d" % i) for i in range(NF)]
    B_brh = [[Buf("brh0"), BF[7], BF[8]], [Buf("brh1"), BF[9], BF[14]]]
    NB = 8
    BtP = [sb("B%d" % i, [128, T + L], BF16) for i in range(NB)]
    Bt = [t_[:, 0:T] for t_ in BtP]
    BB = [Buf("B%d" % i) for i in range(NB)]
    xpad = sb("xpad_s", [128, T + 3], F32); B_xpad = Buf("xpad")
    xpad2 = sb("xpad2_s", [128, T + 3], F32); B_xpad2 = Buf("xpad2")
    GZ = sb("GZ_s", [128, 4, T], F32); B_gz = [Buf("gz%d" % i) for i in range(4)]
    xm_f = acc[:, 0:4, :]; B_xm = B_acc[0:4]
    xm_b = mg[:, 0:4, 0:T]; B_xmb = B_mgc[0:4]
    mx_b = mg[:, 4:8, 0:T]; B_mxb = B_mgc[4:8]
    mx_bP = mg[:, 4:8, :]
    qkv_b = yT[:, 4:16, :]; B_qkv = B_y[4:16]
    vflat = sb("vflat_s", [128, 4096], BF16); B_vflat = Buf("vflat")
    vtok2 = vflat[:, :].rearrange("p (h j e) -> p h j e", h=2, j=NCH, e=256)
    vtok4 = vflat[:, :].rearrange("p (h j e) -> p h j e", h=4, j=NCH, e=128)
    B_vtok = [B_vflat] * 4
    g4 = [Ft[12 + i][0:4, :] for i in range(4)]; B_g4 = [BF[12 + i] for i in range(4)]
    glr = sb("glr_s", [16, T], BF16); B_glr = Buf("glr")
    ATm = sb("ATm_s", [128, 4, 2, L], BF16); B_ATm = [[Buf("ATm%d_%d" % (i, p)) for p in range(2)] for i in range(4)]
    ktok = sb("ktok_s", [128, 4, 2, 128], BF16); B_ktok = [[Buf("ktok%d_%d" % (i, p)) for p in range(2)] for i in range(4)]
    Sbf = sb("Sbf_s", [128, 4, 256], BF16); B_Sbf = [Buf("Sbf%d" % i) for i in range(4)]
    Stmp = sb("Stmp_s", [128, 4, 128], F32); B_Stmp = [Buf("Stmp%d" % i) for i in range(4)]
    st_conv = sb("st_conv", [128, PL * 2 * 4, 3], F32); B_stconv = Buf("stconv")
    st_lru = sb("st_lru", [128, PL * 4], F32); B_stlru = Buf("stlru")
    st_C = sb("st_C", [128, PL * 4, 256], F32); B_stC = [[Buf("stC%d_%d" % (l, h)) for h in range(4)] for l in range(PL)]
    st_H = sb("st_H", [128, PL * 4, 128], F32); B_stH = [[Buf("stH%d_%d" % (l, h)) for h in range(4)] for l in range(PL)]
    st_G = sb("st_G", [128, PL * 4, 128], F32); B_stG = [[Buf("stG%d_%d" % (l, h)) for h in range(4)] for l in range(PL)]
    osb = acc

    ps = [es.enter_context(nc.psum_tensor("ps%d" % i, [128, 512], F32)) for i in range(8)]
    B_ps = [Buf("ps%d" % i, excl=True) for i in range(8)]
    B_psA = [[Buf("psA%d_%d" % (i, p)) for p in range(2)] for i in range(4)]
    B_psT = [[Buf("psT%d_%d" % (i, p)) for p in range(2)] for i in range(4)]
    pool_banks = [0, 1, 2, 3, 4, 5]
    ps_state = {"next": 0, "held": set(), "live": set()}
    def ps_alloc(keep=False):
        for _ in range(2 * len(pool_banks)):
            b = pool_banks[ps_state["next"] % len(pool_banks)]
            ps_state["next"] += 1
            if b not in ps_state["held"] and b not in ps_state["live"]:
                if keep:
                    ps_state["live"].add(b)
                return b
        raise RuntimeError("out of PSUM banks")
    def ps_free(*bs):
        for b in bs:
            ps_state["live"].discard(b)
    def ps_hold(n):
        out = []
        for _ in range(n):
            b = ps_alloc()
            ps_state["held"].add(b)
            out.append(b)
        return out
    def ps_release(bs):
        for b in bs:
            ps_state["held"].discard(b)

    sem_names = ["pe", "act", "dve", "pool", "sp", "d_setup", "d_setup2", "d_x", "d_out", "d_send", "d_recv", "cc", "d_brh0", "d_brh1"] + ["d_ring%d" % i for i in range(NSLOT)]
    sems = {n: es.enter_context(nc.semaphore("s_" + n)) for n in sem_names}
    block = es.enter_context(nc.Block())
    prog = []

    streams = {"pe": [], "act": [], "dve": [], "pool": [], "sp": []}

    class Rec:
        def __init__(self, en):
            self.en = en
        def __getattr__(self, name):
            en = self.en
            def call(*a, **k):
                h = RecIns()
                streams[en].append((name, a, k, h))
                return h
            return call

    class RecIns:
        def __init__(self):
            self.post = []
        def _wait_ge(self, s, v):
            self.post.append(("_wait_ge", (s, v)))
            return self
        def then_inc(self, s, v):
            self.post.append(("then_inc", (s, v)))
            return self

    engs = {k: Rec(k) for k in streams}
    tk = Trk(nc, engs, sems)

    def act(out, in_, func, reads, writes, bias=None, scale=None):
        kw = {}
        if bias is not None:
            kw["bias"] = bias
        if scale is not None:
            kw["scale"] = scale
        tk.op("act", lambda e: e.activation(out=out, in_=in_, func=func, **kw), reads, writes)

    def tt(en, out, in0, in1, op, reads, writes):
        tk.op(en, lambda e: e.tensor_tensor(out=out, in0=in0, in1=in1, op=op), reads, writes)

    def ts(en, out, in0, s1, s2, op0, op1, reads, writes):
        if s2 is None:
            tk.op(en, lambda e: e.tensor_scalar(out=out, in0=in0, scalar1=s1, scalar2=None, op0=op0), reads, writes)
        else:
            tk.op(en, lambda e: e.tensor_scalar(out=out, in0=in0, scalar1=s1, scalar2=s2, op0=op0, op1=op1), reads, writes)

    def stt(out, in0, scalar, in1, op0, op1, reads, writes):
        tk.op("dve", lambda e: e.scalar_tensor_tensor(out=out, in0=in0, scalar=scalar, in1=in1, op0=op0, op1=op1), reads, writes)

    def cpy(en, out, in_, reads, writes):
        if en == "act":
            tk.op("act", lambda e: e.copy(out=out, in_=in_), reads, writes)
        else:
            tk.op(en, lambda e: e.tensor_copy(out=out, in_=in_), reads, writes)

    def mmg(lst, reads, writes):
        fns = []
        cost = 0.0
        for (o, l_, r_, st, sp) in lst:
            fns.append((lambda o=o, l_=l_, r_=r_, st=st, sp=sp: (lambda e: e.matmul(o, l_, r_, start=st, stop=sp)))())
            n_mov = int(r_.shape[-1])
            cost += max(n_mov, 64) / 1900.0 * (4.0 if r_.dtype == F32 else 1.0) + 0.03
        tk.mm(fns, reads, writes, cost=cost)

    setup_bufs = [B_pv, B_const]
    tk.dma("sp", "d_setup", pvec[:], pvec_d[:, :], [], [B_pv])
    tk.dma("sp", "d_setup", cst[:], cst_d[:, 0:896], [], [B_const])
    tk.dma("sp", "d_setup", gb[:], gb_d[:, :], [], [B_const])
    tk.dma("sp", "d_setup", sel[:], cst_d[0:4, 896:1408].rearrange("p (j m) -> p j m", j=4), [], [B_const])
    tk.dma("sp", "d_setup", maskrep[:], cst_d[0:64, 1408:1664].rearrange("p (h t) -> p h t", h=4), [], [B_const])
    tk.dma("pool", "d_setup2", bd[:], bd_d[:, :, :], [], [B_const])
    tk.dma("pool", "d_setup2", wif[:], wif_d[:, :, :], [], [B_const])
    tk.dma("pool", "d_setup2", lr2[:], lr2_d[:, :, :], [], [B_const])
    tk.dma("pool", "d_setup2", identb[:], cst_d[:, 0:128], [], [B_const])
    tk.dma("pool", "d_setup2", ones_b[:], cst_d[:, 128:256], [], [B_const])
    for en_ in ("pe", "act", "dve", "pool"):
        for sk_ in ("d_setup", "d_setup2"):
            engs[en_].wait_ge(sems[sk_], tk.dmaval[sk_])
            tk.waited[en_][("d", sk_)] = tk.dmaval[sk_]
    for b_ in (B_pv, B_const):
        b_.w = None

    B_dc = Buf("dcst")
    DC_S1, DC_S2, DC_HBA, DC_HBX, DC_C0, DC_C1, DC_HSK, DC_NB2, DC_LB = 0, 4, 8, 12, 16, 20, 24, 28, 30
    l = 0
    lam = pvec[:, pv(l, "lru_lam"):pv(l, "lru_lam") + 4]
    act(dcst[:, 0:4], lam, AF.Exp, [B_pv], [B_dc], scale=-1.0)
    act(dcst[:, 0:4], dcst[:, 0:4], AF.Ln, [B_dc], [B_dc], bias=1.0)
    ts("dve", dcst[:, 4:8], dcst[:, 0:4], -8.0, None, ALU.mult, None, [B_dc], [B_dc])
    ts("dve", dcst[:, 0:4], dcst[:, 0:4], -4.0, None, ALU.mult, None, [B_dc], [B_dc])
    ts("dve", dcst[:, 8:12], pvec[:, pv(l, "lru_ba"):pv(l, "lru_ba") + 4], 0.5, None, ALU.mult, None, [B_pv], [B_dc])
    ts("dve", dcst[:, 12:16], pvec[:, pv(l, "lru_bx"):pv(l, "lru_bx") + 4], 0.5, None, ALU.mult, None, [B_pv], [B_dc])
    ts("dve", dcst[:, 24:28], pvec[:, pv(l, "m_skip"):pv(l, "m_skip") + 4], 0.5, None, ALU.mult, None, [B_pv], [B_dc])
    ts("dve", dcst[:, 28:30], pvec[:, pv(l, "g_b2"):pv(l, "g_b2") + 2], -1.0, None, ALU.mult, None, [B_pv], [B_dc])
    ts("dve", gbn[:, 0:1], gb[:, 1:2], -1.0, None, ALU.mult, None, [B_const], [B_dc])
    l0 = pvec[:, pv(l, "h_lbl"):pv(l, "h_lbl") + 4]
    l1 = pvec[:, pv(l, "h_lbl1"):pv(l, "h_lbl1") + 4]
    tt("dve", dcst[:, 30:34], l1, l0, ALU.subtract, [B_pv], [B_dc])
    act(dcst[:, 30:34], dcst[:, 30:34], AF.Tanh, [B_dc], [B_dc], scale=0.5)
    ts("dve", dcst[:, 30:34], dcst[:, 30:34], 0.5, 0.5, ALU.mult, ALU.add, [B_dc], [B_dc])
    ts("dve", dcst[:, 30:34], dcst[:, 30:34], pvec[:, pv(l, "flag_b"):pv(l, "flag_b") + 1], None, ALU.mult, None, [B_dc, B_pv], [B_dc])
    ts("dve", dcst[:, 20:24], dcst[:, 30:34], -0.5, 0.5, ALU.mult, ALU.add, [B_dc], [B_dc])
    tt("dve", dcst[:, 16:20], dcst[:, 30:34], dcst[:, 20:24], ALU.add, [B_dc], [B_dc])
    tk.op("dve", lambda e: e.memset(st_conv[:], 0.0), [], [B_stconv])
    tk.op("dve", lambda e: e.memset(st_lru[:], 0.0), [], [B_stlru])
    allC = [b for l in B_stC for b in l]; allH = [b for l in B_stH for b in l]; allG = [b for l in B_stG for b in l]
    tk.op("dve", lambda e: e.memset(st_C[:], 0.0), [], allC)
    tk.op("dve", lambda e: e.memset(st_H[:], 0.0), [], allH)
    tk.op("dve", lambda e: e.memset(st_G[:], 0.0), [], allG)
    tk.op("dve", lambda e: e.memset(onesT[:], 1.0), [], [B_const])
    tk.op("dve", lambda e: e.memset(vflat[:], 0.0), [], [B_vflat])
    tk.op("dve", lambda e: e.memset(ATm[:], 0.0), [], [b_ for l_ in B_ATm for b_ in l_])
    tk.op("dve", lambda e: e.memset(ktok[:], 0.0), [], [b_ for l_ in B_ktok for b_ in l_])
    tk.op("dve", lambda e: e.memset(hT[:], 0.0), [], [B_h])
    tk.op("dve", lambda e: e.memset(mg[:], 0.0), [], B_mgc)
    for i_ in range(NB):
        tk.op("dve", lambda e, i_=i_: e.memset(BtP[i_][:], 0.0), [], [BB[i_]])

    def wgroups(l):
        g = []
        g.append(("in", C_LRUX, 1024))
        g.append(("in", C_MX, 1024))
        g.append(("in", C_MZ, 512))
        g.append(("in", C_HQ, 1024))
        g.append(("in", C_HI, 1024))
        g.append(("in", C_GQ, 1024))
        g.append(("in", C_GLR, 528))
        for n in range(4):
            g.append(("in", C_MERGE + n * 1024, 1024))
        g.append(("out", 0, 1024))
        return g
    wsched = []
    for t in range(NT + 1):
        for l in range(n_layers):
            for gi, g in enumerate(wgroups(l)):
                wsched.append((l, g))
    wstate = {"issued": 0, "consumed": 0, "done": 0}
    def pump():
        while wstate["issued"] < len(wsched) and wstate["issued"] - wstate["done"] < NSLOT:
            i = wstate["issued"]
            l, (kind, a, ncols) = wsched[i]
            slot = i % NSLOT
            if kind == "in":
                src = w_in_d[l].rearrange("(kc p) c -> p kc c", p=128)[:, :, a:a + ncols]
                tk.dma("pool", "d_ring%d" % slot, ring[slot][:, :, 0:ncols], src, [], [B_ring[slot]])
            elif kind == "br":
                src = w_br_d[l, a].rearrange("(kc p) c -> p kc c", p=128)
                tk.dma("pool", "d_ring%d" % slot, ring[slot][:, 0:4, :], src, [], [B_ring[slot]])
            else:
                src = w_out_d[l].rearrange("(kc p) c -> p kc c", p=128)
                tk.dma("pool", "d_ring%d" % slot, ring[slot][:, :, :], src, [], [B_ring[slot]])
            wstate["issued"] += 1
    def issue_brh(l, n, half):
        src = w_br_d[l, n].rearrange("(kc p) c -> p kc c", p=128)[:, :, half * 512:(half + 1) * 512]
        tk.dma("pool", "d_brh%d" % half, brh[half], src, [], B_brh[half])

    def next_weight():
        i = wstate["consumed"]
        assert i < wstate["issued"], "weight group not issued (ring too small for live groups)"
        wstate["consumed"] += 1
        return ring[i % NSLOT], B_ring[i % NSLOT]
    def weights_done(n, defer=False):
        wstate["done"] += n
        if not defer:
            pump()

    def proj_fm(slot, B_slot, off, ncols, bank, extra_reads=()):
        lst = [(ps[bank][0:ncols, :], slot[:, kc, off:off + ncols], hT[:, kc, 0:T], kc == 0, kc == 7) for kc in range(8)]
        mmg(lst, [B_slot, B_h] + list(extra_reads), [B_ps[bank]])

    def bfv(i):
        return Ft[i].bitcast(BF16)[:, 0:T]

    def stats_bcast(src_list, src_bufs, bank, bf=False):
        n = len(src_list)
        lst = [(ps[bank][:, :], ones_b[:, :] if bf else ones_f, src_list[i], i == 0, i == n - 1) for i in range(n)]
        mmg(lst, [B_const] + list(src_bufs), [B_ps[bank]])

    def rstd_from(bank, nfeat, out_f, out_buf):
        act(out_f, ps[bank][:, :], AF.Ln, [B_ps[bank]], [out_buf], bias=EPS, scale=1.0 / nfeat)
        act(out_f, out_f, AF.Exp, [out_buf], [out_buf], scale=-0.5)

    def rmsnorm_to_h(gcol0):
        sq = Ft[0]
        b = ps_alloc()
        for kc in range(8):
            act(bfv(kc % 2), xT[:, kc, :], AF.Square, [B_x], [BF[kc % 2]])
            tk.mm([lambda e, kc=kc: e.matmul(ps[b][:, :], ones_b[:, :], bfv(kc % 2), start=(kc == 0), stop=(kc == 7))],
                  [B_const, BF[kc % 2]], [B_ps[b]], cost=0.3)
        rstd_from(b, D, Ft[2][:, :], BF[2])
        for kc in range(8):
            stt(hT[:, kc, 0:T], xT[:, kc, :], pvec[:, gcol0 + kc:gcol0 + kc + 1], Ft[2][:, :], ALU.mult, ALU.mult,
                [B_x, B_pv, BF[2]], [B_h])

    def conv_fm(bank, l, br, c, cw0, cb, out_f, out_buf, xp=None, Bxp=None):
        if xp is None:
            xp, Bxp = xpad, B_xpad
        si = (l * 2 + br) * 4 + c
        cpy("dve", xp[:, 0:3], st_conv[:, si, :], [B_stconv], [Bxp])
        cpy("act", xp[:, 3:3 + T], ps[bank][:, :], [B_ps[bank]], [Bxp])
        cpy("act", st_conv[:, si, :], xp[:, T:T + 3], [Bxp], [B_stconv])
        ts("dve", out_f, xp[:, 0:T], pvec[:, cw0:cw0 + 1], pvec[:, cb:cb + 1], ALU.mult, ALU.add,
           [Bxp, B_pv], [out_buf])
        for j in range(1, 4):
            stt(out_f, xp[:, j:j + T], pvec[:, cw0 + j:cw0 + j + 1], out_f, ALU.mult, ALU.add,
                [Bxp, B_pv, out_buf], [out_buf])

    def run_interleaved(gens):
        live = [[g, 0.0, i] for i, g in enumerate(gens)]
        while live:
            item = min(live, key=lambda it: (it[1], it[2]))
            tk.last_fin = None
            try:
                next(item[0])
            except StopIteration:
                live.remove(item)
                continue
            if tk.last_fin is not None:
                item[1] = tk.last_fin

    B_ATm1 = Buf("ATm"); B_ktok1 = Buf("ktok"); B_Sb1 = Buf("Sbf")

    def chunk_attn(heads, vtok, qT, kT, kP, B_q, B_k, vcols, state_all, state_f, B_state, local, decb, dec, B_dec, obanks, dbanks=None, ubank=None):
        nh = len(heads)
        vs = {h: hi for hi, h in enumerate(heads)}
        Bst = [B_state[h] for h in heads]
        Bq = [B_q[h] for h in heads]
        Bk = list({id(B_k[h]): B_k[h] for h in heads}.values())
        Bd = list({id(B_dec[h]): B_dec[h] for h in heads}.values())
        Sb = Sbf[:, 0:nh, 0:vcols]
        ureg_all = ps[ubank][:, 0:nh * vcols].rearrange("p (h e) -> p h e", h=nh)
        if not local:
            fns = []
            for hi, h in enumerate(heads):
                fns.append(lambda e, hi=hi, h=h: e.matmul(ps[ubank][:, hi * vcols:(hi + 1) * vcols], ident_f, state_f[h],
                                                       start=(hi == 0), stop=(hi == nh - 1), skip_group_check=True))
            tk.mm(fns, [B_const] + Bst, [B_ps[ubank]])
        cpy("act", Sb, state_all, Bst, [B_Sb1])

        def stage1(j):
            cs = slice(j * L, (j + 1) * L)
            fns = []
            for hi, h in enumerate(heads):
                fns.append(lambda e, hi=hi, h=h: e.matmul(ps[6][:, hi * 64:(hi + 1) * 64], kP[h][:, j * L:j * L + 128], qT[h][:, cs], start=True, stop=True))
            tk.mm(fns, Bk + Bq, [B_ps[6]])
            fns = []
            for hi, h in enumerate(heads):
                fns.append(lambda e, hi=hi, h=h: e.matmul(ps[7][:, hi * 128:(hi + 1) * 128], kP[h][:, j * L:j * L + 128], identb[:, :], start=True, stop=True))
            tk.mm(fns, Bk + [B_const], [B_ps[7]])
            tt("dve", ATm[0:64, 0:nh, 0, :], ps[6][0:64, 0:nh * 64].rearrange("p (h t) -> p h t", h=nh), maskrep[:, 0:nh, :], ALU.mult,
               [B_ps[6], B_const], [B_ATm1])
            cpy("act", ktok[0:64, 0:nh, 0, :], ps[7][0:64, 0:nh * 128].rearrange("p (h e) -> p h e", h=nh), [B_ps[7]], [B_ktok1])

        def stage2_pe(j):
            cs = slice(j * L, (j + 1) * L)
            for hi, h in enumerate(heads):
                lst = [(ps[obanks[hi]][:, cs], vtok[:, hi, j, 0:128], ATm[:, hi, 0, :], True, False),
                       (ps[obanks[hi]][:, cs], Sbf[:, hi, 0:128], qT[h][:, cs], False, True)]
                mmg(lst, [B_vflat, B_ATm1, B_Sb1, B_q[h]], [B_ps[obanks[hi]]])
                if vcols == 256:
                    lst = [(ps[dbanks[hi]][:, cs], vtok[:, hi, j, 128:256], ATm[:, hi, 0, :], True, False),
                           (ps[dbanks[hi]][:, cs], Sbf[:, hi, 128:256], qT[h][:, cs], False, True)]
                    mmg(lst, [B_vflat, B_ATm1, B_Sb1, B_q[h]], [B_ps[dbanks[hi]]])
            fns = []
            for hi, h in enumerate(heads):
                if local:
                    fns.append(lambda e, hi=hi: e.matmul(ps[ubank][:, hi * vcols:(hi + 1) * vcols], ktok[:, hi, 0, :], vtok[:, hi, j, 0:vcols],
                                                        start=True, stop=True, skip_group_check=True))
                else:
                    fns.append(lambda e, hi=hi: e.matmul(ps[ubank][:, hi * vcols:(hi + 1) * vcols], ktok[:, hi, 0, :], vtok[:, hi, j, 0:vcols],
                                                        start=False, stop=(j == NCH - 1), skip_group_check=True))
            tk.mm(fns, [B_ktok1, B_vflat], [B_ps[ubank]])

        def stage2_state(j):
            if local:
                tt("dve", state_all, state_all, ureg_all, ALU.add, Bst + [B_ps[ubank]], Bst)
                if j < NCH - 1:
                    tt("dve", Sb, state_all, decb(j), ALU.mult, Bst + Bd, [B_Sb1])
                tt("dve", state_all, state_all, decb(j), ALU.mult, Bst + Bd, Bst)
            else:
                if j < NCH - 1:
                    cpy("act", Sb, ureg_all, [B_ps[ubank]], [B_Sb1])
                else:
                    for hi, h in enumerate(heads):
                        dcol = dec[h][:, T - 1:T]
                        ts("dve", state_f[h], ps[ubank][:, hi * vcols:(hi + 1) * vcols], dcol, None, ALU.mult, None,
                           [B_ps[ubank], B_dec[h]], [B_state[h]])

        stage1(0)
        for j in range(NCH):
            stage2_pe(j)
            if j + 1 < NCH:
                stage1(j + 1)
            stage2_state(j)

    pump()
    keepc = pvec[:, pv(0, "keep"):pv(0, "keep") + 1]
    flagb = pvec[:, pv(0, "flag_b"):pv(0, "flag_b") + 1]
    for step in range(NT + 1):
        t = min(step, NT - 1)
        tsl = slice(t * T, (t + 1) * T)
        tk.dma("sp", "d_x", xT[:, :, :], xT_d.rearrange("(kc p) s -> p kc s", p=128)[:, :, tsl], [], [B_x])
        if step >= 1:
            tk.dma("sp", "d_recv", acc[:, :, :], recv_d[0:D, :].rearrange("(kc p) s -> p kc s", p=128), [B_recv], B_acc)
            for kc in range(8):
                stt(xT[:, kc, :], acc[:, kc, :], flagb, xT[:, kc, :], ALU.mult, ALU.add, [B_acc[kc], B_pv, B_x], [B_x])
        if step == 1:
            ts("dve", st_conv[:, :, :], st_conv[:, :, :], keepc, None, ALU.mult, None, [B_stconv, B_pv], [B_stconv])
            ts("dve", st_lru[:, :], st_lru[:, :], keepc, None, ALU.mult, None, [B_stlru, B_pv], [B_stlru])
            for h in range(4):
                ts("dve", st_C[:, h, :], st_C[:, h, :], keepc, None, ALU.mult, None, [B_stC[0][h], B_pv], [B_stC[0][h]])
                ts("dve", st_H[:, h, :], st_H[:, h, :], keepc, None, ALU.mult, None, [B_stH[0][h], B_pv], [B_stH[0][h]])
                ts("dve", st_G[:, h, :], st_G[:, h, :], keepc, None, ALU.mult, None, [B_stG[0][h], B_pv], [B_stG[0][h]])

        for l in range(n_layers):
            rmsnorm_to_h(pv(l, "norm_g"))

            slotA, B_slotA = next_weight()
            slot, B_slot = next_weight()
            slot2, B_slot2 = next_weight()
            def gen_lru(chunks, tset, bti, slot=slotA, B_slot=B_slotA):
                t_xa, t_r, t_i, t_a, t_m, t_hs, t_sz = tset
                for c in chunks:
                    b1 = ps_alloc(True)
                    proj_fm(slot, B_slot, c * 128, 128, b1)
                    yield
                    xa, Bxa = Ft[t_xa], BF[t_xa]
                    conv_fm(b1, l, 0, c, pv(l, "lru_cw", c * 4), pv(l, "lru_cb", c), xa[:, :], Bxa)
                    ps_free(b1)
                    yield
                    cpy("act", Bt[bti][:, :], xa[:, :], [Bxa], [BB[bti]])
                    yield
                    b2 = ps_alloc(True); b3 = ps_alloc(True)
                    mmg([(ps[b2][:, :], bd[:, l * 20 + c, :], Bt[bti][:, :], True, True)], [B_const, BB[bti]], [B_ps[b2]])
                    yield
                    mmg([(ps[b3][:, :], bd[:, l * 20 + 4 + c, :], Bt[bti][:, :], True, True)], [B_const, BB[bti]], [B_ps[b3]])
                    yield
                    r_, Br = Ft[t_r], BF[t_r]
                    i_, Bi = Ft[t_i], BF[t_i]
                    act(r_[:, :], ps[b2][:, :], AF.Tanh, [B_ps[b2], B_dc], [Br], bias=dcst[:, DC_HBA + c:DC_HBA + c + 1], scale=0.5)
                    ps_free(b2)
                    yield
                    act(i_[:, :], ps[b3][:, :], AF.Tanh, [B_ps[b3], B_dc], [Bi], bias=dcst[:, DC_HBX + c:DC_HBX + c + 1], scale=0.5)
                    ps_free(b3)
                    yield
                    a_, Ba = Ft[t_a], BF[t_a]
                    m_, Bm = Ft[t_m], BF[t_m]
                    act(a_[:, :], r_[:, :], AF.Exp, [Br, B_dc], [Ba], scale=dcst[:, DC_S1 + c:DC_S1 + c + 1], bias=dcst[:, DC_S1 + c:DC_S1 + c + 1])
                    yield
                    act(m_[:, :], r_[:, :], AF.Exp, [Br, B_dc], [Bm], scale=dcst[:, DC_S2 + c:DC_S2 + c + 1], bias=dcst[:, DC_S2 + c:DC_S2 + c + 1])
                    yield
                    act(m_[:, :], m_[:, :], AF.Ln, [Bm], [Bm], bias=1.0, scale=-1.0)
                    yield
                    act(m_[:, :], m_[:, :], AF.Exp, [Bm], [Bm], scale=0.5)
                    yield
                    stt(i_[:, :], i_[:, :], 1.0, xa[:, :], ALU.add, ALU.mult, [Bi, Bxa], [Bi])
                    yield
                    stt(i_[:, :], i_[:, :], 0.5, m_[:, :], ALU.mult, ALU.mult, [Bi, Bm], [Bi])
                    yield
                    hs, Bhs = Ft[t_hs], BF[t_hs]
                    sc = l * 4 + c
                    tk.op("dve", lambda e, sc=sc: e.tensor_tensor_scan(out=hs[:, :], data0=a_[:, :], data1=i_[:, :],
                                                                       initial=st_lru[:, sc:sc + 1], op0=ALU.mult, op1=ALU.add),
                          [Ba, Bi, B_stlru], [Bhs])
                    cpy("act", st_lru[:, sc:sc + 1], hs[:, T - 1:T], [Bhs], [B_stlru])
                    yield
                    b4 = ps_alloc(True)
                    proj_fm(slot, B_slot, 512 + c * 128, 128, b4)
                    yield
                    sz, Bsz = Ft[t_sz], BF[t_sz]
                    act(sz[:, :], ps[b4][:, :], AF.Tanh, [B_ps[b4]], [Bsz], scale=0.5)
                    yield
                    stt(sz[:, :], sz[:, :], 1.0, ps[b4][:, :], ALU.add, ALU.mult, [Bsz, B_ps[b4]], [Bsz])
                    ps_free(b4)
                    yield
                    stt(yT[:, 0 + c, :], sz[:, :], 0.5, hs[:, :], ALU.mult, ALU.mult, [Bhs, Bsz], [B_y[0 + c]])
                    yield

            def gen_mprep():
                for h in range(4):
                    b1 = ps_alloc(True)
                    proj_fm(slot, B_slot, h * 128, 128, b1)
                    yield
                    cpy("dve", mx_b[:, h, :], ps[b1][:, :], [B_ps[b1]], [B_mxb[h]])
                    yield
                    xc, Bxc = Ft[7], BF[7]
                    conv_fm(b1, l, 1, h, pv(l, "m_cw", h * 4), pv(l, "m_cb", h), xc[:, :], Bxc, xp=xpad2, Bxp=B_xpad2)
                    ps_free(b1)
                    yield
                    act(xm_f[:, h, :], xc[:, :], AF.Tanh, [Bxc], [B_xm[h]], scale=0.5)
                    yield
                    stt(xm_f[:, h, :], xm_f[:, h, :], 1.0, xc[:, :], ALU.add, ALU.mult, [B_xm[h], Bxc], [B_xm[h]])
                    yield
                    act(xm_b[:, h, :], xm_f[:, h, :], AF.Identity, [B_xm[h]], [B_xmb[h]], scale=0.5)
                    yield
                    for qi, (srcb, Bsrc) in enumerate([(xm_b, B_xmb), (xm_b, B_xmb), (mx_b, B_mxb)]):
                        b2 = ps_alloc(True)
                        mmg([(ps[b2][:, :], bd[:, l * 20 + 8 + qi * 4 + h, :], srcb[:, h, :], True, True)], [B_const, Bsrc[h]], [B_ps[b2]])
                        cpy("act" if qi != 1 else "dve", qkv_b[:, qi * 4 + h, :], ps[b2][:, :], [B_ps[b2]], [B_qkv[qi * 4 + h]])
                        ps_free(b2)
                        yield

            run_interleaved([gen_lru([0, 1], (0, 1, 2, 3, 4, 5, 6), 0), gen_lru([2, 3], (8, 9, 10, 11, 12, 13, 15), 1), gen_mprep()])
            weights_done(1)

            bi_ = ps_alloc(); bf_ = ps_alloc()
            for gi, bank in ((0, bi_), (1, bf_)):
                lst = [(ps[bank][0:4, :], wif[:, (l * 2 + gi) * 12 + ci, :], qkv_b[:, ci, :], ci == 0, ci == 11) for ci in range(12)]
                mmg(lst, [B_const] + B_qkv, [B_ps[bank]])
            li, lf, G, eG, wk = g4[0], g4[1], g4[2], g4[3], g4[0]
            act(li[:, :], ps[bi_][0:4, :], AF.Identity, [B_ps[bi_], B_const], [B_g4[0]], bias=gb[:, l * 2:l * 2 + 1])
            act(lf[:, :], ps[bf_][0:4, :], AF.Exp, [B_ps[bf_], B_dc], [B_g4[1]], bias=gbn[:, 0:1], scale=-1.0)
            act(lf[:, :], lf[:, :], AF.Ln, [B_g4[1]], [B_g4[1]], bias=1.0)
            tk.op("dve", lambda e: e.tensor_tensor_scan(out=G[:, :], data0=ones4, data1=lf[:, :], initial=0.0,
                                                        op0=ALU.mult, op1=ALU.add), [B_g4[1], B_const], [B_g4[2]])
            act(eG[:, :], G[:, :], AF.Exp, [B_g4[2]], [B_g4[3]], scale=-1.0)
            tt("dve", wk[:, :], li[:, :], G[:, :], ALU.add, [B_g4[0], B_g4[2]], [B_g4[0]])
            act(wk[:, :], wk[:, :], AF.Exp, [B_g4[0]], [B_g4[0]])
            GO = acc[:, 4:8, :]; B_go = B_acc[4:8]
            for h in range(4):
                b9 = ps_alloc()
                proj_fm(slot, B_slot, 512 + h * 128, 128, b9)
                act(GO[:, h, :], ps[b9][:, :], AF.Tanh, [B_ps[b9]], [B_go[h]], scale=0.5)
                b12 = ps_alloc()
                proj_fm(slot2, B_slot2, h * 128, 128, b12)
                act(GZ[:, h, :], ps[b12][:, :], AF.Tanh, [B_ps[b12]], [B_gz[h]], scale=0.5)
                stt(GZ[:, h, :], GZ[:, h, :], 1.0, ps[b12][:, :], ALU.add, ALU.mult, [B_gz[h], B_ps[b12]], [B_gz[h]])
            weights_done(2)
            for pair in range(2):
                hp = [pair * 2, pair * 2 + 1]
                qT_ = {}; kT_ = {}; kP_ = {}; Bq_ = {}; Bk_ = {}; dec_ = {}; Bdec_ = {}
                tk.op("pool", lambda e: e.memset(vtok2[0:64, :, :, 128:256], 1.0), [], [B_vflat])
                for hi, h in enumerate(hp):
                    for half in range(2):
                        b3 = ps_alloc()
                        fns = []
                        for jj in range(4):
                            j = half * 4 + jj
                            fns.append(lambda e, j=j, jj=jj, b3=b3, h=h: e.matmul(ps[b3][:, jj * 128:(jj + 1) * 128], mx_bP[:, h, j * L:j * L + 128],
                                                                                 bd[:, l * 20 + 16 + h, :], start=True, stop=True))
                        tk.mm(fns, [B_mxb[h], B_const], [B_ps[b3]])
                        cpy("act", vtok2[0:64, hi, half * 4:half * 4 + 4, 0:128], ps[b3][0:64, :].rearrange("p (j e) -> p j e", j=4),
                            [B_ps[b3]], [B_vflat])
                    b5 = ps_alloc(); b6 = ps_alloc()
                    mmg([(ps[b5][:, :], sel[:, h, :], eG[:, :], True, True)], [B_const, B_g4[3]], [B_ps[b5]])
                    mmg([(ps[b6][:, :], sel[:, h, :], wk[:, :], True, True)], [B_const, B_g4[0]], [B_ps[b6]])
                    eGb, BeGb = Ft[6 + hi], BF[6 + hi]
                    wkb, Bwkb = Ft[8 + hi], BF[8 + hi]
                    cpy("act", eGb[:, :], ps[b5][:, :], [B_ps[b5]], [BeGb])
                    cpy("act", wkb[:, :], ps[b6][:, :], [B_ps[b6]], [Bwkb])
                    b7 = ps_alloc(); b8 = ps_alloc()
                    mmg([(ps[b7][:, :], bd[:, l * 20 + 8 + h, :], xm_b[:, h, :], True, True)], [B_const, B_xmb[h]], [B_ps[b7]])
                    mmg([(ps[b8][:, :], bd[:, l * 20 + 12 + h, :], xm_b[:, h, :], True, True)], [B_const, B_xmb[h]], [B_ps[b8]])
                    stt(Bt[1 + hi][:, :], ps[b7][:, :], 128.0 ** -0.5, eGb[:, :], ALU.mult, ALU.mult, [B_ps[b7], BeGb], [BB[1 + hi]])
                    tt("dve", Bt[3 + hi][:, :], ps[b8][:, :], wkb[:, :], ALU.mult, [B_ps[b8], Bwkb], [BB[3 + hi]])
                    qT_[h] = Bt[1 + hi]; kT_[h] = Bt[3 + hi]; kP_[h] = BtP[3 + hi]; Bq_[h] = BB[1 + hi]; Bk_[h] = BB[3 + hi]
                    dec_[h] = eGb; Bdec_[h] = BeGb
                held = ps_hold(5)
                ob = held[0:2]; db = held[2:4]; ub = held[4]
                sfl = {h: st_C[:, l * 4 + h, :] for h in hp}
                chunk_attn(hp, vtok2, qT_, kT_, kP_, Bq_, Bk_, 256, st_C[:, l * 4 + pair * 2:l * 4 + pair * 2 + 2, :], sfl, {h: B_stC[l][h] for h in hp}, False, None, dec_, Bdec_, ob, db, ub)
                for hi, h in enumerate(hp):
                    dn, Bdn = Ft[hi], BF[hi]
                    ts("dve", dn[:, :], ps[db[hi]][:, :], -1.0, 1.0, ALU.mult, ALU.max, [B_ps[db[hi]]], [Bdn])
                    tt("dve", dn[:, :], dn[:, :], ps[db[hi]][:, :], ALU.max, [Bdn, B_ps[db[hi]]], [Bdn])
                for hi, h in enumerate(hp):
                    act(Ft[hi][:, :], Ft[hi][:, :], AF.Ln, [BF[hi]], [BF[hi]])
                for hi, h in enumerate(hp):
                    act(Ft[hi][:, :], Ft[hi][:, :], AF.Exp, [BF[hi]], [BF[hi]], scale=-1.0)
                for hi, h in enumerate(hp):
                    stt(Ft[10 + hi][:, :], ps[ob[hi]][:, :], 0.5, Ft[hi][:, :], ALU.mult, ALU.mult, [B_ps[ob[hi]], BF[hi]], [BF[10 + hi]])
                ps_release(held)

                def gen_post(hi, h):
                    hm, Bhm = Ft[10 + hi], BF[10 + hi]
                    base = 2 + 4 * hi
                    stt(hm[:, :], GO[:, h, :], 1.0, hm[:, :], ALU.add, ALU.mult, [Bhm, B_go[h]], [Bhm])
                    yield
                    b10 = ps_alloc(True)
                    stats_bcast([hm[:, :]], [Bhm], b10)
                    yield
                    xcn, Bxcn = Ft[base], BF[base]
                    stt(xcn[:, :], ps[b10][:, :], -1.0 / 128.0, hm[:, :], ALU.mult, ALU.add, [B_ps[b10], Bhm], [Bxcn])
                    ps_free(b10)
                    yield
                    sq, Bsq = Ft[base + 1], BF[base + 1]
                    act(sq.bitcast(BF16)[:, 0:T], xcn[:, :], AF.Square, [Bxcn], [Bsq])
                    yield
                    b11 = ps_alloc(True)
                    stats_bcast([sq.bitcast(BF16)[:, 0:T]], [Bsq], b11, bf=True)
                    yield
                    rs, Brs = Ft[base + 2], BF[base + 2]
                    act(rs[:, :], ps[b11][:, :], AF.Ln, [B_ps[b11]], [Brs], bias=EPS, scale=1.0 / 128)
                    ps_free(b11)
                    yield
                    act(rs[:, :], rs[:, :], AF.Exp, [Brs], [Brs], scale=-0.5)
                    yield
                    tt("dve", xcn[:, :], xcn[:, :], rs[:, :], ALU.mult, [Bxcn, Brs], [Bxcn])
                    yield
                    sk, Bsk = Ft[base + 3], BF[base + 3]
                    act(sk[:, :], xm_f[:, h, :], AF.Identity, [B_xm[h], B_dc], [Bsk], scale=dcst[:, DC_HSK + h:DC_HSK + h + 1])
                    yield
                    stt(xcn[:, :], xcn[:, :], pvec[:, pv(l, "m_nw", h):pv(l, "m_nw", h) + 1], sk[:, :], ALU.mult, ALU.add,
                        [Bxcn, B_pv, Bsk], [Bxcn])
                    yield
                    stt(yT[:, 4 + h, :], GZ[:, h, :], 0.5, xcn[:, :], ALU.mult, ALU.mult, [Bxcn, B_gz[h]], [B_y[4 + h]])
                    yield
                run_interleaved([gen_post(hi, h) for hi, h in enumerate(hp)])

            slot, B_slot = next_weight()
            slot2, B_slot2 = next_weight()
            qT_ = {}; kT_ = {}; kP_ = {}; Bq_ = {}; Bk_ = {}; dec_ = {}; Bdec_ = {}
            def gen_hprep(heads_, tset):
                F_f, F_lg, F_G, F_en, F_qs, F_kk = tset
                for h in heads_:
                    bfk = ps_alloc(True)
                    proj_fm(slot, B_slot, 512 + h * 128, 128, bfk)
                    yield
                    f_, Bf_ = Ft[F_f], BF[F_f]
                    act(f_[:, :], ps[bfk][:, :], AF.Tanh, [B_ps[bfk]], [Bf_], scale=0.5)
                    ps_free(bfk)
                    yield
                    ts("dve", f_[:, :], f_[:, :], dcst[:, DC_C1 + h:DC_C1 + h + 1], dcst[:, DC_C0 + h:DC_C0 + h + 1],
                       ALU.mult, ALU.add, [Bf_, B_dc], [Bf_])
                    yield
                    lg, Blg = Ft[F_lg], BF[F_lg]
                    act(lg[:, :], f_[:, :], AF.Ln, [Bf_], [Blg])
                    yield
                    Gc, BGc = Ft[F_G], BF[F_G]
                    tk.op("dve", lambda e: e.tensor_tensor_scan(out=Gc[:, :], data0=rmask, data1=lg[:, :], initial=0.0, op0=ALU.mult, op1=ALU.add),
                          [Blg, B_const], [BGc])
                    yield
                    eGc, BeGc = Ft[3 + h], BF[3 + h]
                    act(eGc[:, :], Gc[:, :], AF.Exp, [BGc], [BeGc])
                    yield
                    enG, BenG = Ft[F_en], BF[F_en]
                    act(enG[:, :], Gc[:, :], AF.Exp, [BGc], [BenG], scale=-1.0)
                    yield
                    bq = ps_alloc(True)
                    proj_fm(slot, B_slot, h * 128, 128, bq)
                    yield
                    qs, Bqs = Ft[F_qs], BF[F_qs]
                    act(qs[:, :], ps[bq][:, :], AF.Tanh, [B_ps[bq]], [Bqs], scale=0.5)
                    yield
                    stt(qs[:, :], qs[:, :], 1.0, ps[bq][:, :], ALU.add, ALU.mult, [Bqs, B_ps[bq]], [Bqs])
                    ps_free(bq)
                    yield
                    stt(Bt[h][:, :], qs[:, :], 0.5 * 128.0 ** -0.5, eGc[:, :], ALU.mult, ALU.mult, [Bqs, BeGc], [BB[h]])
                    yield
                    kk, Bkk = Ft[F_kk], BF[F_kk]
                    ts("dve", kk[:, :], f_[:, :], -1.0, 1.0, ALU.mult, ALU.add, [Bf_], [Bkk])
                    yield
                    tt("dve", Bt[4 + h][:, :], kk[:, :], enG[:, :], ALU.mult, [Bkk, BenG], [BB[4 + h]])
                    yield
                    qT_[h] = Bt[h]; kT_[h] = Bt[4 + h]; kP_[h] = BtP[4 + h]; Bq_[h] = BB[h]; Bk_[h] = BB[4 + h]
                    dec_[h] = eGc; Bdec_[h] = BeGc

            def gen_hvz():
                for h in range(4):
                    b12 = ps_alloc(True)
                    proj_fm(slot2, B_slot2, 512 + h * 128, 128, b12)
                    yield
                    act(GZ[:, h, :], ps[b12][:, :], AF.Tanh, [B_ps[b12]], [B_gz[h]], scale=0.5)
                    yield
                    stt(GZ[:, h, :], GZ[:, h, :], 1.0, ps[b12][:, :], ALU.add, ALU.mult, [B_gz[h], B_ps[b12]], [B_gz[h]])
                    ps_free(b12)
                    yield
                for j in range(NCH):
                    b3 = ps_alloc(True)
                    lst = [(ps[b3][:, :], hT[:, kc, j * L:j * L + 128], slot2[:, kc, 0:512], kc == 0, kc == 7) for kc in range(8)]
                    mmg(lst, [B_h, B_slot2], [B_ps[b3]])
                    yield
                    cpy("act", vtok4[0:64, :, j, :], ps[b3][0:64, :].rearrange("p (h e) -> p h e", h=4), [B_ps[b3]], [B_vflat])
                    ps_free(b3)
                    yield

            run_interleaved([gen_hprep([0, 2], (0, 1, 2, 7, 8, 9)), gen_hprep([1, 3], (10, 11, 12, 13, 14, 15)), gen_hvz()])
            weights_done(2)
            held = ps_hold(5)
            ob = held[0:4]; ub = held[4]
            sfl = {h: st_H[:, l * 4 + h, :] for h in range(4)}
            chunk_attn([0, 1, 2, 3], vtok4, qT_, kT_, kP_, Bq_, Bk_, 128, st_H[:, l * 4:l * 4 + 4, :], sfl, {h: B_stH[l][h] for h in range(4)}, True,
                       lambda j: EG4[:, :, j * L + L - 1:j * L + L].to_broadcast([128, 4, 128]), dec_, Bdec_, ob, None, ub)
            for h in range(4):
                cpy("act", Ft[10 + h][:, :], ps[ob[h]][:, :], [B_ps[ob[h]]], [BF[10 + h]])
            ps_release(held)
            pb_ = []
            for h in range(4):
                act(bfv(h), Ft[10 + h][:, :], AF.Square, [BF[10 + h]], [BF[h]])
                b10 = ps_alloc(); pb_.append(b10)
                stats_bcast([bfv(h)], [BF[h]], b10, bf=True)
            for h in range(4):
                act(Ft[h][:, :], ps[pb_[h]][:, :], AF.Ln, [B_ps[pb_[h]]], [BF[h]], bias=EPS, scale=1.0 / 128)
            for h in range(4):
                act(Ft[h][:, :], Ft[h][:, :], AF.Exp, [BF[h]], [BF[h]], scale=-0.5)
            for h in range(4):
                o_, Bo_ = Ft[10 + h], BF[10 + h]
                stt(o_[:, :], o_[:, :], pvec[:, pv(l, "h_nw"):pv(l, "h_nw") + 1], Ft[h][:, :], ALU.mult, ALU.mult, [Bo_, B_pv, BF[h]], [Bo_])
                stt(yT[:, 8 + h, :], GZ[:, h, :], 0.5, o_[:, :], ALU.mult, ALU.mult, [Bo_, B_gz[h]], [B_y[8 + h]])

            slot, B_slot = next_weight()
            slot2, B_slot2 = next_weight()
            bl = ps_alloc()
            proj_fm(slot2, B_slot2, 0, 16, bl)
            cpy("act", glr[:, :], ps[bl][0:16, :], [B_ps[bl]], [B_glr])
            qT_ = {}; kT_ = {}; kP_ = {}; Bq_ = {}; Bk_ = {}; dec_ = {}; Bdec_ = {}
            def gen_gprep(cc, tset):
                F_lg, F_G, F_en, F_qe = tset
                bg = ps_alloc(True)
                mmg([(ps[bg][:, :], lr2[:, l, cc * 128:(cc + 1) * 128], glr[:, :], True, True)], [B_const, B_glr], [B_ps[bg]])
                yield
                lg, Blg = Ft[F_lg], BF[F_lg]
                act(lg[:, :], ps[bg][:, :], AF.Exp, [B_ps[bg], B_dc], [Blg], bias=dcst[:, DC_NB2 + cc:DC_NB2 + cc + 1], scale=-1.0)
                ps_free(bg)
                yield
                act(lg[:, :], lg[:, :], AF.Ln, [Blg], [Blg], bias=1.0)
                yield
                Gc, BGc = Ft[F_G], BF[F_G]
                tk.op("dve", lambda e: e.tensor_tensor_scan(out=Gc[:, :], data0=ones_T, data1=lg[:, :], initial=0.0, op0=ALU.mult, op1=ALU.add),
                      [Blg, B_const], [BGc])
                yield
                eGc, BeGc = Ft[2 + cc], BF[2 + cc]
                act(eGc[:, :], Gc[:, :], AF.Exp, [BGc], [BeGc], scale=-1.0 / 16.0)
                yield
                enG, BenG = Ft[F_en], BF[F_en]
                act(enG[:, :], Gc[:, :], AF.Exp, [BGc], [BenG], scale=1.0 / 16.0)
                yield
                bq = ps_alloc(True)
                proj_fm(slot, B_slot, cc * 128, 128, bq)
                yield
                qe, Bqe = Ft[F_qe], BF[F_qe]
                stt(qe[:, :], ps[bq][:, :], 64.0 ** -0.5, eGc[:, :], ALU.mult, ALU.mult, [B_ps[bq], BeGc], [Bqe])
                ps_free(bq)
                yield
                for hh in range(2):
                    h = cc * 2 + hh
                    ts("dve", Bt[h][:, :], qe[:, :], rowmask[:, hh:hh + 1], None, ALU.mult, None, [Bqe, B_const], [BB[h]])
                    yield
                    qT_[h] = Bt[h]; Bq_[h] = BB[h]
                    kT_[h] = Bt[4 + cc]; kP_[h] = BtP[4 + cc]; Bk_[h] = BB[4 + cc]
                    dec_[h] = eGc; Bdec_[h] = BeGc
                bk = ps_alloc(True)
                proj_fm(slot, B_slot, 256 + cc * 128, 128, bk)
                yield
                tt("dve", Bt[4 + cc][:, :], ps[bk][:, :], enG[:, :], ALU.mult, [B_ps[bk], BenG], [BB[4 + cc]])
                ps_free(bk)
                yield

            def gen_gvz():
                for h in range(4):
                    b12 = ps_alloc(True)
                    proj_fm(slot2, B_slot2, 16 + h * 128, 128, b12)
                    yield
                    act(GZ[:, h, :], ps[b12][:, :], AF.Tanh, [B_ps[b12]], [B_gz[h]], scale=0.5)
                    yield
                    stt(GZ[:, h, :], GZ[:, h, :], 1.0, ps[b12][:, :], ALU.add, ALU.mult, [B_gz[h], B_ps[b12]], [B_gz[h]])
                    ps_free(b12)
                    yield
                for j in range(NCH):
                    b3 = ps_alloc(True)
                    lst = [(ps[b3][:, :], hT[:, kc, j * L:j * L + 128], slot[:, kc, 512:1024], kc == 0, kc == 7) for kc in range(8)]
                    mmg(lst, [B_h, B_slot], [B_ps[b3]])
                    yield
                    cpy("act", vtok4[0:64, :, j, :], ps[b3][0:64, :].rearrange("p (h e) -> p h e", h=4), [B_ps[b3]], [B_vflat])
                    ps_free(b3)
                    yield

            run_interleaved([gen_gprep(0, (0, 1, 4, 5)), gen_gprep(1, (6, 7, 8, 9)), gen_gvz()])
            issue_brh(l, 0, 0)
            issue_brh(l, 0, 1)
            weights_done(2)
            held = ps_hold(5)
            ob = held[0:4]; ub = held[4]
            sfl = {h: st_G[:, l * 4 + h, :] for h in range(4)}
            chunk_attn([0, 1, 2, 3], vtok4, qT_, kT_, kP_, Bq_, Bk_, 128, st_G[:, l * 4:l * 4 + 4, :], sfl, {h: B_stG[l][h] for h in range(4)}, False, None, dec_, Bdec_, ob, None, ub)
            for h in range(4):
                cpy("act", Ft[10 + h][:, :], ps[ob[h]][:, :], [B_ps[ob[h]]], [BF[10 + h]])
            ps_release(held)
            pb_ = []
            for h in range(4):
                act(bfv(h), Ft[10 + h][:, :], AF.Square, [BF[10 + h]], [BF[h]])
                b10 = ps_alloc(); pb_.append(b10)
                stats_bcast([bfv(h)], [BF[h]], b10, bf=True)
            for h in range(4):
                act(Ft[h][:, :], ps[pb_[h]][:, :], AF.Ln, [B_ps[pb_[h]]], [BF[h]], bias=EPS, scale=1.0 / 128)
            for h in range(4):
                act(Ft[h][:, :], Ft[h][:, :], AF.Exp, [BF[h]], [BF[h]], scale=-0.5)
            for h in range(4):
                o_, Bo_ = Ft[10 + h], BF[10 + h]
                stt(o_[:, :], o_[:, :], pvec[:, pv(l, "g_nw"):pv(l, "g_nw") + 1], Ft[h][:, :], ALU.mult, ALU.mult, [Bo_, B_pv, BF[h]], [Bo_])
                stt(yT[:, 12 + h, :], GZ[:, h, :], 0.5, o_[:, :], ALU.mult, ALU.mult, [Bo_, B_gz[h]], [B_y[12 + h]])

            for n in range(4):
                slot, B_slot = next_weight()
                for dc in range(8):
                    bgate = ps_alloc(); bpr = ps_alloc()
                    proj_fm(slot, B_slot, dc * 128, 128, bgate)
                    hb = dc // 4
                    lst = [(ps[bpr][:, :], brh[hb][:, wc, (dc % 4) * 128:(dc % 4 + 1) * 128], yT[:, n * 4 + wc, :], wc == 0, wc == 3) for wc in range(4)]
                    mmg(lst, B_brh[hb] + B_y[n * 4:n * 4 + 4], [B_ps[bpr]])
                    if n < 3 and dc % 4 == 3:
                        issue_brh(l, n + 1, hb)
                    sg, Bsg = Ft[dc % 2], BF[dc % 2]
                    act(sg[:, :], ps[bgate][:, :], AF.Tanh, [B_ps[bgate]], [Bsg], scale=0.5)
                    if n == 0:
                        stt(acc[:, dc, :], sg[:, :], 1.0, ps[bpr][:, :], ALU.add, ALU.mult, [B_ps[bpr], Bsg], [B_acc[dc]])
                    else:
                        pr, Bpr = Ft[2 + dc % 2], BF[2 + dc % 2]
                        stt(pr[:, :], sg[:, :], 1.0, ps[bpr][:, :], ALU.add, ALU.mult, [B_ps[bpr], Bsg], [Bpr])
                        if n < 3:
                            tt("pool", acc[:, dc, :], acc[:, dc, :], pr[:, :], ALU.add, [B_acc[dc], Bpr], [B_acc[dc]])
                        else:
                            tt("pool", mg[:, dc, 0:T], acc[:, dc, :], pr[:, :], ALU.add, [B_acc[dc], Bpr], [B_mgc[dc]])
                weights_done(1, defer=(n == 3 and step < NT))
            slot, B_slot = next_weight()
            for ec in range(8):
                bo = ps_alloc()
                lst = [(ps[bo][:, :], slot[:, dc, ec * 128:(ec + 1) * 128], mg[:, dc, 0:T], dc == 0, dc == 7) for dc in range(8)]
                mmg(lst, [B_slot] + B_mgc, [B_ps[bo]])
                stt(xT[:, ec, :], ps[bo][:, :], 0.5, xT[:, ec, :], ALU.mult, ALU.add, [B_x, B_ps[bo]], [B_x])
            weights_done(1, defer=(step < NT))

        if step < NT:
            tk.dma("sp", "d_send", send_d.rearrange("(kc p) s -> p kc s", p=128), xT[:, :, :], [B_x], [B_send])
            tk.coll("pool", "cc", lambda e: e.collective_compute("AllGather", ALU.bypass, replica_groups=[[2 * i, 2 * i + 1] for i in range(n_pairs)],
                                                                ins=[send_d], outs=[recv_d]), [B_send], [B_recv])
            pump()
        if step >= 1:
            b = ps_alloc()
            for kc in range(8):
                act(bfv(kc % 2), xT[:, kc, :], AF.Square, [B_x], [BF[kc % 2]])
                tk.mm([lambda e, kc=kc, b=b: e.matmul(ps[b][:, :], ones_b[:, :], bfv(kc % 2), start=(kc == 0), stop=(kc == 7))],
                      [B_const, BF[kc % 2]], [B_ps[b]], cost=0.3)
            rstd_from(b, D, Ft[2][:, :], BF[2])
            for kc in range(8):
                stt(osb[:, kc, :], xT[:, kc, :], pvec[:, pv(0, "final_g") + kc:pv(0, "final_g") + kc + 1], Ft[2][:, :], ALU.mult, ALU.mult,
                    [B_x, B_pv, BF[2]], [B_acc[kc]])
            osl = slice((step - 1) * T, step * T)
            tk.dma("sp", "d_out", outT_d.rearrange("(kc p) s -> p kc s", p=128)[:, :, osl], osb[:, :, :], B_acc, [])

    tk.final_wait("sp", "d_out")

    def replay(en):
        def f(e):
            for (name, a, k, h) in streams[en]:
                ins = getattr(e, name)(*a, **k)
                for (pn, pa) in h.post:
                    getattr(ins, pn)(*pa)
        return f
    block.tensor(replay("pe"))
    block.scalar(replay("act"))
    block.vector(replay("dve"))
    block.gpsimd(replay("pool"))
    block.sync(replay("sp"))
    es.close()
    return nc, tk


def _host_pack(inputs, l, is_b):
    f = lambda k: np.asarray(inputs[k], dtype=np.float32)
    pvec = np.zeros((128, PV_COLS), np.float32)
    def put(name, vec, nchunks, stride=1, off=0):
        for c in range(nchunks):
            pvec[:, pv(0, name) + c * stride + off] = vec[c * 128:(c + 1) * 128]
    for j in range(4):
        put("lru_cw", f("lru_conv_w")[l, j], 4, stride=4, off=j)
        put("m_cw", f("m_conv_w")[l, j], 4, stride=4, off=j)
    put("lru_cb", f("lru_conv_b")[l], 4)
    put("lru_ba", f("lru_ba")[l], 4)
    put("lru_bx", f("lru_bx")[l], 4)
    put("lru_lam", f("lru_lambda")[l], 4)
    put("m_cb", f("m_conv_b")[l], 4)
    put("m_nw", f("m_norm_w")[l], 4)
    put("m_skip", f("m_skip")[l], 4)
    put("h_lbl", f("h_lb_logits")[0], 4)
    put("h_lbl1", f("h_lb_logits")[1], 4)
    put("h_nw", f("h_norm_w")[l], 1)
    put("g_b2", f("g_b_lr2")[l], 2)
    put("g_nw", f("g_norm_w")[l], 1)
    put("norm_g", f("norm_g")[l], 8)
    put("final_g", f("final_g"), 8)
    pvec[:, pv(0, "flag_b")] = 1.0 if is_b else 0.0
    pvec[:, pv(0, "keep")] = 0.0 if is_b else 1.0
    bd = np.zeros((128, 20, 128), np.float32)
    for gi, key in enumerate(["lru_wa", "lru_wx"]):
        w = f(key)[l]
        for c in range(4):
            for b in range(2):
                bd[b * 64:(b + 1) * 64, gi * 4 + c, b * 64:(b + 1) * 64] = w[2 * c + b]
    for gi, key in enumerate(["m_wq", "m_wk", "m_wv"]):
        w = f(key)[l]
        for h in range(4):
            for b in range(32):
                bd[b * 4:(b + 1) * 4, 8 + gi * 4 + h, b * 4:(b + 1) * 4] = w[32 * h + b]
    wif = np.zeros((128, 2 * 12, 4), np.float32)
    gb = np.zeros((4, 2), np.float32)
    for gi, key in enumerate(["m_wi", "m_wf"]):
        w = f(key)[l]
        for ci in range(12):
            wif[:, gi * 12 + ci, :] = w[ci * 128:(ci + 1) * 128, :]
    gb[:, 0] = f("m_bi")[l]
    gb[:, 1] = f("m_bf")[l]
    lr2 = np.ascontiguousarray(f("g_w_lr2")[l][:, None, :])
    cst = np.zeros((128, 1664), np.float32)
    cst[:, 0:128] = np.eye(128, dtype=np.float32)
    cst[:, 128:256] = 1.0
    cst[0:64, 256:320] = np.triu(np.ones((64, 64), np.float32))
    rm = np.ones((T,), np.float32); rm[::L] = 0.0
    cst[:, 320:832] = rm[None, :]
    cst[0:64, 832] = 1.0
    cst[64:128, 833] = 1.0
    for j in range(4):
        cst[j, 896 + j * 128:896 + (j + 1) * 128] = 1.0
        cst[0:64, 1408 + j * 64:1408 + (j + 1) * 64] = np.triu(np.ones((64, 64), np.float32))
    return dict(w_in=np.ascontiguousarray(f("w_in")[l:l + 1]), w_branch=np.ascontiguousarray(f("w_branch")[l:l + 1]),
                w_out=np.ascontiguousarray(f("w_out")[l:l + 1]), pvec=pvec, bd=bd, wif=wif, gbias=gb, lr2=lr2, cst=cst)


_PROG_CACHE = {}


def kernel(**inputs):
    x = np.asarray(inputs["x"], dtype=np.float32)
    B, S, _ = x.shape
    packs = [_host_pack(inputs, 0, False), _host_pack(inputs, 1, True)]
    if (S, B) not in _PROG_CACHE:
        _PROG_CACHE[(S, B)] = build_program(S, n_pairs=B)[0]
    nc = _PROG_CACHE[(S, B)]
    zeros = np.zeros((D, S), np.float32)
    in_maps = []
    for b in range(B):
        ma = dict(packs[0]); ma["xT"] = np.ascontiguousarray(x[b].T)
        mb = dict(packs[1]); mb["xT"] = zeros
        in_maps += [ma, mb]
    res = run_bass_kernel_spmd(nc, in_maps, core_ids=list(range(2 * B)))
    out = np.stack([np.ascontiguousarray(res.results[2 * b + 1]["outT"].T) for b in range(B)], axis=0)
    return out.astype(np.float32)
```

```python
import numpy as np
import concourse.bass as bass
import concourse.mybir as mybir
from concourse.bass_utils import run_bass_kernel_spmd

F32 = mybir.dt.float32
BF16 = mybir.dt.bfloat16
AF = mybir.ActivationFunctionType
ALU = mybir.AluOpType

D = 1024
W = 512
D_IN = 10256
DEPTH = 2
T = 512
L = 64
NCH = T // L
EPS = 1e-6
C_LRUX, C_LRUZ = 0, 512
C_MX, C_MO, C_MZ = 1024, 1536, 2048
C_HQ, C_HF, C_HI, C_HZ = 2560, 3072, 3584, 4096
C_GQ, C_GK, C_GV, C_GLR, C_GZ = 4608, 4864, 5120, 5632, 5648
C_MERGE = 6160

PV_PER_LAYER = 96
def pv(l, name, c=0):
    base = l * PV_PER_LAYER
    table = {
        "lru_cw": 0,
        "lru_cb": 16,
        "lru_ba": 20,
        "lru_bx": 24,
        "lru_lam": 28,
        "m_cw": 32,
        "m_cb": 48,
        "m_nw": 52,
        "m_skip": 56,
        "h_lbl": 60,
        "h_nw": 64,
        "g_b2": 65,
        "g_nw": 67,
        "norm_g": 68,
        "final_g": 76,
        "flag_b": 84,
        "keep": 85,
        "h_lbl1": 86,
    }
    return base + table[name] + c
SAME_ENGINE_WINDOW = 3
PL = 1
PV_COLS = PL * PV_PER_LAYER


class Buf:
    __slots__ = ("w", "r", "name", "excl")
    def __init__(self, name="", excl=False):
        self.w = None
        self.r = []
        self.name = name
        self.excl = excl


class Trk:
    def __init__(self, nc, engs, sems):
        self.nc = nc
        self.E = engs
        self.S = sems
        self.tick = {k: 0 for k in engs}
        self.waited = {k: {} for k in engs}
        self.dmaval = {}
        self.nwait = 0
        self.ninst = 0
        self.efree = {k: 0.0 for k in engs}
        self.fin = {}
        self.last_fin = None
        self.COST = {"act": 0.6, "dve": 0.65, "pool": 1.6, "pe": 0.27, "sp": 0.1}

    def _time(self, en, cost, ndma=None):
        ready = 0.0
        for (kind, key), val in self._need.items():
            t = self.fin.get((kind, key, val), 0.0)
            if kind == "e" and key != en:
                t += 0.15
            if t > ready:
                ready = t
        start = max(self.efree[en], ready)
        fin = start + cost
        self.efree[en] = fin
        return fin

    def _deps(self, en, reads, writes):
        need = {}
        def add(dep):
            if dep is None:
                return
            kind, key, val = dep
            k = (kind, key)
            if need.get(k, 0) < val:
                need[k] = val
        for b in reads:
            add(b.w)
            if b.excl:
                for r in b.r:
                    add(r)
        for b in writes:
            add(b.w)
            for r in b.r:
                add(r)
        out = []
        self._need = need
        for (kind, key), val in need.items():
            if kind == "e" and key == en:
                if en == "pe":
                    continue
                if en != "pool" and val <= self.tick[en] - SAME_ENGINE_WINDOW:
                    continue
            if self.waited[en].get((kind, key), 0) >= val:
                continue
            self.waited[en][(kind, key)] = val
            out.append((self.S[key], val))
        return out

    def op(self, en, fn, reads=(), writes=(), cost=None):
        eng = self.E[en]
        waits = self._deps(en, reads, writes)
        tfin = self._time(en, self.COST[en] if cost is None else cost)
        for (s, v) in waits[1:]:
            eng.wait_ge(s, v)
            self.nwait += 1
        ins = fn(eng)
        if waits:
            ins._wait_ge(waits[0][0], waits[0][1])
        self.tick[en] += 1
        ins.then_inc(self.S[en], 1)
        me = ("e", en, self.tick[en])
        self.fin[me] = tfin
        self.last_fin = tfin
        for b in reads:
            if b.excl:
                b.w = me
                b.r = []
            else:
                b.r.append(me)
        for b in writes:
            b.w = me
            b.r = []
        self.ninst += 1
        return ins

    def mm(self, fns, reads=(), writes=(), cost=None):
        en = "pe"
        eng = self.E[en]
        waits = self._deps(en, reads, writes)
        tfin = self._time(en, (0.1 * len(fns)) if cost is None else cost)
        for (s, v) in waits[1:]:
            eng.wait_ge(s, v)
            self.nwait += 1
        ins = None
        for i, fn in enumerate(fns):
            ins = fn(eng)
            if i == 0 and waits:
                ins._wait_ge(waits[0][0], waits[0][1])
        self.tick[en] += 1
        ins.then_inc(self.S[en], 1)
        me = ("e", en, self.tick[en])
        self.fin[me] = tfin
        self.last_fin = tfin
        for b in reads:
            if b.excl:
                b.w = me
                b.r = []
            else:
                b.r.append(me)
        for b in writes:
            b.w = me
            b.r = []
        self.ninst += len(fns)

    def dma(self, en, semkey, out, in_, reads=(), writes=()):
        eng = self.E[en]
        waits = self._deps(en, reads, writes)
        for (s, v) in waits:
            eng.wait_ge(s, v)
            self.nwait += 1
        ins = eng.dma_start(out=out, in_=in_)
        self.dmaval[semkey] = self.dmaval.get(semkey, 0) + 16
        ins.then_inc(self.S[semkey], 16)
        me = ("d", semkey, self.dmaval[semkey])
        self.fin[me] = self._time(en, 0.1) + 12.0
        for b in reads:
            if b.excl:
                b.w = me
                b.r = []
            else:
                b.r.append(me)
        for b in writes:
            b.w = me
            b.r = []
        self.ninst += 1

    def coll(self, en, semkey, fn, reads=(), writes=()):
        eng = self.E[en]
        waits = self._deps(en, reads, writes)
        for (s, v) in waits:
            eng.wait_ge(s, v)
            self.nwait += 1
        ins = fn(eng)
        self.dmaval[semkey] = self.dmaval.get(semkey, 0) + 1
        ins.then_inc(self.S[semkey], 1)
        me = ("d", semkey, self.dmaval[semkey])
        self.fin[me] = self._time(en, 0.1) + 35.0
        for b in reads:
            if b.excl:
                b.w = me
                b.r = []
            else:
                b.r.append(me)
        for b in writes:
            b.w = me
            b.r = []
        self.ninst += 1

    def final_wait(self, en, semkey):
        self.E[en].wait_ge(self.S[semkey], self.dmaval[semkey])


def build_program(S, n_layers=PL, n_pairs=4):
    assert S % T == 0
    NT = S // T
    nc = bass.Bass("TRN2", target_bir_lowering=False)

    xT_d = nc.dram_tensor("xT", [D, S], F32, kind="ExternalInput").ap()
    w_in_d = nc.dram_tensor("w_in", [PL, D, D_IN], F32, kind="ExternalInput").ap()
    w_br_d = nc.dram_tensor("w_branch", [PL, 4, W, D], F32, kind="ExternalInput").ap()
    w_out_d = nc.dram_tensor("w_out", [PL, D, D], F32, kind="ExternalInput").ap()
    pvec_d = nc.dram_tensor("pvec", [128, PV_COLS], F32, kind="ExternalInput").ap()
    bd_d = nc.dram_tensor("bd", [128, PL * 20, 128], F32, kind="ExternalInput").ap()
    wif_d = nc.dram_tensor("wif", [128, PL * 2 * 12, 4], F32, kind="ExternalInput").ap()
    gb_d = nc.dram_tensor("gbias", [4, PL * 2], F32, kind="ExternalInput").ap()
    lr2_d = nc.dram_tensor("lr2", [16, PL, 256], F32, kind="ExternalInput").ap()
    cst_d = nc.dram_tensor("cst", [128, 1664], F32, kind="ExternalInput").ap()
    outT_d = nc.dram_tensor("outT", [D, S], F32, kind="ExternalOutput").ap()
    send_d = nc.dram_tensor("send", [D, T], F32, kind="Internal").ap()
    recv_d = nc.dram_tensor("recv", [2 * D, T], F32, kind="Internal").ap()
    B_send = Buf("send"); B_recv = Buf("recv")

    from contextlib import ExitStack
    es = ExitStack()
    sb = lambda name, shape, dt: es.enter_context(nc.sbuf_tensor(name, shape, dt))

    xT = sb("xT_s", [128, 8, T], F32); B_x = Buf("xT")
    hT = sb("hT_s", [128, 8, T + L], BF16); B_h = Buf("hT")
    yT = sb("yT_s", [128, 16, T], BF16); B_y = [Buf("y%d" % i) for i in range(16)]
    acc = sb("acc_s", [128, 8, T], F32); B_acc = [Buf("acc%d" % i) for i in range(8)]
    mg = sb("mg_s", [128, 8, T + L], BF16); B_mgc = [Buf("mg%d" % i) for i in range(8)]
    NSLOT = 3
    ring = [sb("ring%d" % i, [128, 8, 1024], BF16) for i in range(NSLOT)]
    B_ring = [Buf("ring%d" % i) for i in range(NSLOT)]
    pvec = sb("pvec_s", [128, PV_COLS], F32); B_pv = Buf("pvec")
    dcst = sb("dcst_s", [128, 40], F32)
    gbn = sb("gbn_s", [4, 1], F32)
    bd = sb("bd_s", [128, PL * 20, 128], BF16)
    wif = sb("wif_s", [128, PL * 2 * 12, 4], BF16)
    gb = sb("gb_s", [4, PL * 2], F32)
    lr2 = sb("lr2_s", [16, PL, 256], BF16)
    cst = sb("cst_s", [128, 896], F32)
    identb = sb("identb_s", [128, 128], BF16)
    ones_b = sb("ones_b_s", [128, 128], BF16)
    B_const = Buf("const")
    ident_f = cst[:, 0:128]
    ones_f = cst[:, 128:256]
    maskT = cst[0:64, 256:320]
    rmask = cst[:, 320:832]
    rowmask = cst[:, 832:834]
    sel = sb("sel_s", [4, 4, 128], F32)
    maskrep = sb("maskrep_s", [64, 4, L], F32)
    onesT = sb("onesT_s", [128, T], F32)
    ones_T = onesT[:, :]
    ones4 = onesT[0:4, :]

    NF = 16
    EG4 = sb("EG4_s", [128, 4, T], F32)
    FB4 = sb("FB4_s", [128, 4, T], F32)
    _fmap = {7: 0, 8: 1, 9: 2, 14: 3}
    Ft = [EG4[:, i - 3, :] if 3 <= i <= 6 else (FB4[:, _fmap[i], :] if i in _fmap else sb("F%d" % i, [128, T], F32)) for i in range(NF)]
    brh = [FB4[:, 0:2, :].bitcast(BF16).rearrange("p a (b c) -> p (a b) c", b=2),
           FB4[:, 2:4, :].bitcast(BF16).rearrange("p a (b c) -> p (a b) c", b=2)]
    BF = [Buf("F%d" % i) for i in range(NF)]
    B_brh = [[Buf("brh0"), BF[7], BF[8]], [Buf("brh1"), BF[9], BF[14]]]
    NB = 8
    BtP = [sb("B%d" % i, [128, T + L], BF16) for i in range(NB)]
    Bt = [t_[:, 0:T] for t_ in BtP]
    BB = [Buf("B%d" % i) for i in range(NB)]
    xpad = sb("xpad_s", [128, T + 3], F32); B_xpad = Buf("xpad")
    xpad2 = sb("xpad2_s", [128, T + 3], F32); B_xpad2 = Buf("xpad2")
    GZ = sb("GZ_s", [128, 4, T], F32); B_gz = [Buf("gz%d" % i) for i in range(4)]
    xm_f = acc[:, 0:4, :]; B_xm = B_acc[0:4]
    xm_b = mg[:, 0:4, 0:T]; B_xmb = B_mgc[0:4]
    mx_b = mg[:, 4:8, 0:T]; B_mxb = B_mgc[4:8]
    mx_bP = mg[:, 4:8, :]
    qkv_b = yT[:, 4:16, :]; B_qkv = B_y[4:16]
    vflat = sb("vflat_s", [128, 4096], BF16); B_vflat = Buf("vflat")
    vtok2 = vflat[:, :].rearrange("p (h j e) -> p h j e", h=2, j=NCH, e=256)
    vtok4 = vflat[:, :].rearrange("p (h j e) -> p h j e", h=4, j=NCH, e=128)
    B_vtok = [B_vflat] * 4
    g4 = [Ft[12 + i][0:4, :] for i in range(4)]; B_g4 = [BF[12 + i] for i in range(4)]
    glr = sb("glr_s", [16, T], BF16); B_glr = Buf("glr")
    ATm = sb("ATm_s", [128, 4, 2, L], BF16); B_ATm = [[Buf("ATm%d_%d" % (i, p)) for p in range(2)] for i in range(4)]
    ktok = sb("ktok_s", [128, 4, 2, 128], BF16); B_ktok = [[Buf("ktok%d_%d" % (i, p)) for p in range(2)] for i in range(4)]
    Sbf = sb("Sbf_s", [128, 4, 256], BF16); B_Sbf = [Buf("Sbf%d" % i) for i in range(4)]
    Stmp = sb("Stmp_s", [128, 4, 128], F32); B_Stmp = [Buf("Stmp%d" % i) for i in range(4)]
    st_conv = sb("st_conv", [128, PL * 2 * 4, 3], F32); B_stconv = Buf("stconv")
    st_lru = sb("st_lru", [128, PL * 4], F32); B_stlru = Buf("stlru")
    st_C = sb("st_C", [128, PL * 4, 256], F32); B_stC = [[Buf("stC%d_%d" % (l, h)) for h in range(4)] for l in range(PL)]
    st_H = sb("st_H", [128, PL * 4, 128], F32); B_stH = [[Buf("stH%d_%d" % (l, h)) for h in range(4)] for l in range(PL)]
    st_G = sb("st_G", [128, PL * 4, 128], F32); B_stG = [[Buf("stG%d_%d" % (l, h)) for h in range(4)] for l in range(PL)]
    osb = acc

    ps = [es.enter_context(nc.psum_tensor("ps%d" % i, [128, 512], F32)) for i in range(8)]
    B_ps = [Buf("ps%d" % i, excl=True) for i in range(8)]
    B_psA = [[Buf("psA%d_%d" % (i, p)) for p in range(2)] for i in range(4)]
    B_psT = [[Buf("psT%d_%d" % (i, p)) for p in range(2)] for i in range(4)]
    pool_banks = [0, 1, 2, 3, 4, 5]
    ps_state = {"next": 0, "held": set(), "live": set()}
    def ps_alloc(keep=False):
        for _ in range(2 * len(pool_banks)):
            b = pool_banks[ps_state["next"] % len(pool_banks)]
            ps_state["next"] += 1
            if b not in ps_state["held"] and b not in ps_state["live"]:
                if keep:
                    ps_state["live"].add(b)
                return b
        raise RuntimeError("out of PSUM banks")
    def ps_free(*bs):
        for b in bs:
            ps_state["live"].discard(b)
    def ps_hold(n):
        out = []
        for _ in range(n):
            b = ps_alloc()
            ps_state["held"].add(b)
            out.append(b)
        return out
    def ps_release(bs):
        for b in bs:
            ps_state["held"].discard(b)

    sem_names = ["pe", "act", "dve", "pool", "sp", "d_setup", "d_setup2", "d_x", "d_out", "d_send", "d_recv", "cc", "d_brh0", "d_brh1"] + ["d_ring%d" % i for i in range(NSLOT)]
    sems = {n: es.enter_context(nc.semaphore("s_" + n)) for n in sem_names}
    block = es.enter_context(nc.Block())
    prog = []

    streams = {"pe": [], "act": [], "dve": [], "pool": [], "sp": []}

    class Rec:
        def __init__(self, en):
            self.en = en
        def __getattr__(self, name):
            en = self.en
            def call(*a, **k):
                h = RecIns()
                streams[en].append((name, a, k, h))
                return h
            return call

    class RecIns:
        def __init__(self):
            self.post = []
        def _wait_ge(self, s, v):
            self.post.append(("_wait_ge", (s, v)))
            return self
        def then_inc(self, s, v):
            self.post.append(("then_inc", (s, v)))
            return self

    engs = {k: Rec(k) for k in streams}
    tk = Trk(nc, engs, sems)

    def act(out, in_, func, reads, writes, bias=None, scale=None):
        kw = {}
        if bias is not None:
            kw["bias"] = bias
        if scale is not None:
            kw["scale"] = scale
        tk.op("act", lambda e: e.activation(out=out, in_=in_, func=func, **kw), reads, writes)

    def tt(en, out, in0, in1, op, reads, writes):
        tk.op(en, lambda e: e.tensor_tensor(out=out, in0=in0, in1=in1, op=op), reads, writes)

    def ts(en, out, in0, s1, s2, op0, op1, reads, writes):
        if s2 is None:
            tk.op(en, lambda e: e.tensor_scalar(out=out, in0=in0, scalar1=s1, scalar2=None, op0=op0), reads, writes)
        else:
            tk.op(en, lambda e: e.tensor_scalar(out=out, in0=in0, scalar1=s1, scalar2=s2, op0=op0, op1=op1), reads, writes)

    def stt(out, in0, scalar, in1, op0, op1, reads, writes):
        tk.op("dve", lambda e: e.scalar_tensor_tensor(out=out, in0=in0, scalar=scalar, in1=in1, op0=op0, op1=op1), reads, writes)

    def cpy(en, out, in_, reads, writes):
        if en == "act":
            tk.op("act", lambda e: e.copy(out=out, in_=in_), reads, writes)
        else:
            tk.op(en, lambda e: e.tensor_copy(out=out, in_=in_), reads, writes)

    def mmg(lst, reads, writes):
        fns = []
        cost = 0.0
        for (o, l_, r_, st, sp) in lst:
            fns.append((lambda o=o, l_=l_, r_=r_, st=st, sp=sp: (lambda e: e.matmul(o, l_, r_, start=st, stop=sp)))())
            n_mov = int(r_.shape[-1])
            cost += max(n_mov, 64) / 1900.0 * (4.0 if r_.dtype == F32 else 1.0) + 0.03
        tk.mm(fns, reads, writes, cost=cost)

    setup_bufs = [B_pv, B_const]
    tk.dma("sp", "d_setup", pvec[:], pvec_d[:, :], [], [B_pv])
    tk.dma("sp", "d_setup", cst[:], cst_d[:, 0:896], [], [B_const])
    tk.dma("sp", "d_setup", gb[:], gb_d[:, :], [], [B_const])
    tk.dma("sp", "d_setup", sel[:], cst_d[0:4, 896:1408].rearrange("p (j m) -> p j m", j=4), [], [B_const])
    tk.dma("sp", "d_setup", maskrep[:], cst_d[0:64, 1408:1664].rearrange("p (h t) -> p h t", h=4), [], [B_const])
    tk.dma("pool", "d_setup2", bd[:], bd_d[:, :, :], [], [B_const])
    tk.dma("pool", "d_setup2", wif[:], wif_d[:, :, :], [], [B_const])
    tk.dma("pool", "d_setup2", lr2[:], lr2_d[:, :, :], [], [B_const])
    tk.dma("pool", "d_setup2", identb[:], cst_d[:, 0:128], [], [B_const])
    tk.dma("pool", "d_setup2", ones_b[:], cst_d[:, 128:256], [], [B_const])
    for en_ in ("pe", "act", "dve", "pool"):
        for sk_ in ("d_setup", "d_setup2"):
            engs[en_].wait_ge(sems[sk_], tk.dmaval[sk_])
            tk.waited[en_][("d", sk_)] = tk.dmaval[sk_]
    for b_ in (B_pv, B_const):
        b_.w = None

    B_dc = Buf("dcst")
    DC_S1, DC_S2, DC_HBA, DC_HBX, DC_C0, DC_C1, DC_HSK, DC_NB2, DC_LB = 0, 4, 8, 12, 16, 20, 24, 28, 30
    l = 0
    lam = pvec[:, pv(l, "lru_lam"):pv(l, "lru_lam") + 4]
    act(dcst[:, 0:4], lam, AF.Exp, [B_pv], [B_dc], scale=-1.0)
    act(dcst[:, 0:4], dcst[:, 0:4], AF.Ln, [B_dc], [B_dc], bias=1.0)
    ts("dve", dcst[:, 4:8], dcst[:, 0:4], -8.0, None, ALU.mult, None, [B_dc], [B_dc])
    ts("dve", dcst[:, 0:4], dcst[:, 0:4], -4.0, None, ALU.mult, None, [B_dc], [B_dc])
    ts("dve", dcst[:, 8:12], pvec[:, pv(l, "lru_ba"):pv(l, "lru_ba") + 4], 0.5, None, ALU.mult, None, [B_pv], [B_dc])
    ts("dve", dcst[:, 12:16], pvec[:, pv(l, "lru_bx"):pv(l, "lru_bx") + 4], 0.5, None, ALU.mult, None, [B_pv], [B_dc])
    ts("dve", dcst[:, 24:28], pvec[:, pv(l, "m_skip"):pv(l, "m_skip") + 4], 0.5, None, ALU.mult, None, [B_pv], [B_dc])
    ts("dve", dcst[:, 28:30], pvec[:, pv(l, "g_b2"):pv(l, "g_b2") + 2], -1.0, None, ALU.mult, None, [B_pv], [B_dc])
    ts("dve", gbn[:, 0:1], gb[:, 1:2], -1.0, None, ALU.mult, None, [B_const], [B_dc])
    l0 = pvec[:, pv(l, "h_lbl"):pv(l, "h_lbl") + 4]
    l1 = pvec[:, pv(l, "h_lbl1"):pv(l, "h_lbl1") + 4]
    tt("dve", dcst[:, 30:34], l1, l0, ALU.subtract, [B_pv], [B_dc])
    act(dcst[:, 30:34], dcst[:, 30:34], AF.Tanh, [B_dc], [B_dc], scale=0.5)
    ts("dve", dcst[:, 30:34], dcst[:, 30:34], 0.5, 0.5, ALU.mult, ALU.add, [B_dc], [B_dc])
    ts("dve", dcst[:, 30:34], dcst[:, 30:34], pvec[:, pv(l, "flag_b"):pv(l, "flag_b") + 1], None, ALU.mult, None, [B_dc, B_pv], [B_dc])
    ts("dve", dcst[:, 20:24], dcst[:, 30:34], -0.5, 0.5, ALU.mult, ALU.add, [B_dc], [B_dc])
    tt("dve", dcst[:, 16:20], dcst[:, 30:34], dcst[:, 20:24], ALU.add, [B_dc], [B_dc])
    tk.op("dve", lambda e: e.memset(st_conv[:], 0.0), [], [B_stconv])
    tk.op("dve", lambda e: e.memset(st_lru[:], 0.0), [], [B_stlru])
    allC = [b for l in B_stC for b in l]; allH = [b for l in B_stH for b in l]; allG = [b for l in B_stG for b in l]
    tk.op("dve", lambda e: e.memset(st_C[:], 0.0), [], allC)
    tk.op("dve", lambda e: e.memset(st_H[:], 0.0), [], allH)
    tk.op("dve", lambda e: e.memset(st_G[:], 0.0), [], allG)
    tk.op("dve", lambda e: e.memset(onesT[:], 1.0), [], [B_const])
    tk.op("dve", lambda e: e.memset(vflat[:], 0.0), [], [B_vflat])
    tk.op("dve", lambda e: e.memset(ATm[:], 0.0), [], [b_ for l_ in B_ATm for b_ in l_])
    tk.op("dve", lambda e: e.memset(ktok[:], 0.0), [], [b_ for l_ in B_ktok for b_ in l_])
    tk.op("dve", lambda e: e.memset(hT[:], 0.0), [], [B_h])
    tk.op("dve", lambda e: e.memset(mg[:], 0.0), [], B_mgc)
    for i_ in range(NB):
        tk.op("dve", lambda e, i_=i_: e.memset(BtP[i_][:], 0.0), [], [BB[i_]])

    def wgroups(l):
        g = []
        g.append(("in", C_LRUX, 1024))
        g.append(("in", C_MX, 1024))
        g.append(("in", C_MZ, 512))
        g.append(("in", C_HQ, 1024))
        g.append(("in", C_HI, 1024))
        g.append(("in", C_GQ, 1024))
        g.append(("in", C_GLR, 528))
        for n in range(4):
            g.append(("in", C_MERGE + n * 1024, 1024))
        g.append(("out", 0, 1024))
        return g
    wsched = []
    for t in range(NT + 1):
        for l in range(n_layers):
            for gi, g in enumerate(wgroups(l)):
                wsched.append((l, g))
    wstate = {"issued": 0, "consumed": 0, "done": 0}
    def pump():
        while wstate["issued"] < len(wsched) and wstate["issued"] - wstate["done"] < NSLOT:
            i = wstate["issued"]
            l, (kind, a, ncols) = wsched[i]
            slot = i % NSLOT
            if kind == "in":
                src = w_in_d[l].rearrange("(kc p) c -> p kc c", p=128)[:, :, a:a + ncols]
                tk.dma("pool", "d_ring%d" % slot, ring[slot][:, :, 0:ncols], src, [], [B_ring[slot]])
            elif kind == "br":
                src = w_br_d[l, a].rearrange("(kc p) c -> p kc c", p=128)
                tk.dma("pool", "d_ring%d" % slot, ring[slot][:, 0:4, :], src, [], [B_ring[slot]])
            else:
                src = w_out_d[l].rearrange("(kc p) c -> p kc c", p=128)
                tk.dma("pool", "d_ring%d" % slot, ring[slot][:, :, :], src, [], [B_ring[slot]])
            wstate["issued"] += 1
    def issue_brh(l, n, half):
        src = w_br_d[l, n].rearrange("(kc p) c -> p kc c", p=128)[:, :, half * 512:(half + 1) * 512]
        tk.dma("pool", "d_brh%d" % half, brh[half], src, [], B_brh[half])

    def next_weight():
        i = wstate["consumed"]
        assert i < wstate["issued"], "weight group not issued (ring too small for live groups)"
        wstate["consumed"] += 1
        return ring[i % NSLOT], B_ring[i % NSLOT]
    def weights_done(n, defer=False):
        wstate["done"] += n
        if not defer:
            pump()

    def proj_fm(slot, B_slot, off, ncols, bank, extra_reads=()):
        lst = [(ps[bank][0:ncols, :], slot[:, kc, off:off + ncols], hT[:, kc, 0:T], kc == 0, kc == 7) for kc in range(8)]
        mmg(lst, [B_slot, B_h] + list(extra_reads), [B_ps[bank]])

    def bfv(i):
        return Ft[i].bitcast(BF16)[:, 0:T]

    def stats_bcast(src_list, src_bufs, bank, bf=False):
        n = len(src_list)
        lst = [(ps[bank][:, :], ones_b[:, :] if bf else ones_f, src_list[i], i == 0, i == n - 1) for i in range(n)]
        mmg(lst, [B_const] + list(src_bufs), [B_ps[bank]])

    def rstd_from(bank, nfeat, out_f, out_buf):
        act(out_f, ps[bank][:, :], AF.Ln, [B_ps[bank]], [out_buf], bias=EPS, scale=1.0 / nfeat)
        act(out_f, out_f, AF.Exp, [out_buf], [out_buf], scale=-0.5)

    def rmsnorm_to_h(gcol0):
        sq = Ft[0]
        b = ps_alloc()
        for kc in range(8):
            act(bfv(kc % 2), xT[:, kc, :], AF.Square, [B_x], [BF[kc % 2]])
            tk.mm([lambda e, kc=kc: e.matmul(ps[b][:, :], ones_b[:, :], bfv(kc % 2), start=(kc == 0), stop=(kc == 7))],
                  [B_const, BF[kc % 2]], [B_ps[b]], cost=0.3)
        rstd_from(b, D, Ft[2][:, :], BF[2])
        for kc in range(8):
            stt(hT[:, kc, 0:T], xT[:, kc, :], pvec[:, gcol0 + kc:gcol0 + kc + 1], Ft[2][:, :], ALU.mult, ALU.mult,
                [B_x, B_pv, BF[2]], [B_h])

    def conv_fm(bank, l, br, c, cw0, cb, out_f, out_buf, xp=None, Bxp=None):
        if xp is None:
            xp, Bxp = xpad, B_xpad
        si = (l * 2 + br) * 4 + c
        cpy("dve", xp[:, 0:3], st_conv[:, si, :], [B_stconv], [Bxp])
        cpy("act", xp[:, 3:3 + T], ps[bank][:, :], [B_ps[bank]], [Bxp])
        cpy("act", st_conv[:, si, :], xp[:, T:T + 3], [Bxp], [B_stconv])
        ts("dve", out_f, xp[:, 0:T], pvec[:, cw0:cw0 + 1], pvec[:, cb:cb + 1], ALU.mult, ALU.add,
           [Bxp, B_pv], [out_buf])
        for j in range(1, 4):
            stt(out_f, xp[:, j:j + T], pvec[:, cw0 + j:cw0 + j + 1], out_f, ALU.mult, ALU.add,
                [Bxp, B_pv, out_buf], [out_buf])

    def run_interleaved(gens):
        live = [[g, 0.0, i] for i, g in enumerate(gens)]
        while live:
            item = min(live, key=lambda it: (it[1], it[2]))
            tk.last_fin = None
            try:
                next(item[0])
            except StopIteration:
                live.remove(item)
                continue
            if tk.last_fin is not None:
                item[1] = tk.last_fin

    B_ATm1 = Buf("ATm"); B_ktok1 = Buf("ktok"); B_Sb1 = Buf("Sbf")

    def chunk_attn(heads, vtok, qT, kT, kP, B_q, B_k, vcols, state_all, state_f, B_state, local, decb, dec, B_dec, obanks, dbanks=None, ubank=None):
        nh = len(heads)
        vs = {h: hi for hi, h in enumerate(heads)}
        Bst = [B_state[h] for h in heads]
        Bq = [B_q[h] for h in heads]
        Bk = list({id(B_k[h]): B_k[h] for h in heads}.values())
        Bd = list({id(B_dec[h]): B_dec[h] for h in heads}.values())
        Sb = Sbf[:, 0:nh, 0:vcols]
        ureg_all = ps[ubank][:, 0:nh * vcols].rearrange("p (h e) -> p h e", h=nh)
        if not local:
            fns = []
            for hi, h in enumerate(heads):
                fns.append(lambda e, hi=hi, h=h: e.matmul(ps[ubank][:, hi * vcols:(hi + 1) * vcols], ident_f, state_f[h],
                                                       start=(hi == 0), stop=(hi == nh - 1), skip_group_check=True))
            tk.mm(fns, [B_const] + Bst, [B_ps[ubank]])
        cpy("act", Sb, state_all, Bst, [B_Sb1])

        def stage1(j):
            cs = slice(j * L, (j + 1) * L)
            fns = []
            for hi, h in enumerate(heads):
                fns.append(lambda e, hi=hi, h=h: e.matmul(ps[6][:, hi * 64:(hi + 1) * 64], kP[h][:, j * L:j * L + 128], qT[h][:, cs], start=True, stop=True))
            tk.mm(fns, Bk + Bq, [B_ps[6]])
            fns = []
            for hi, h in enumerate(heads):
                fns.append(lambda e, hi=hi, h=h: e.matmul(ps[7][:, hi * 128:(hi + 1) * 128], kP[h][:, j * L:j * L + 128], identb[:, :], start=True, stop=True))
            tk.mm(fns, Bk + [B_const], [B_ps[7]])
            tt("dve", ATm[0:64, 0:nh, 0, :], ps[6][0:64, 0:nh * 64].rearrange("p (h t) -> p h t", h=nh), maskrep[:, 0:nh, :], ALU.mult,
               [B_ps[6], B_const], [B_ATm1])
            cpy("act", ktok[0:64, 0:nh, 0, :], ps[7][0:64, 0:nh * 128].rearrange("p (h e) -> p h e", h=nh), [B_ps[7]], [B_ktok1])

        def stage2_pe(j):
            cs = slice(j * L, (j + 1) * L)
            for hi, h in enumerate(heads):
                lst = [(ps[obanks[hi]][:, cs], vtok[:, hi, j, 0:128], ATm[:, hi, 0, :], True, False),
                       (ps[obanks[hi]][:, cs], Sbf[:, hi, 0:128], qT[h][:, cs], False, True)]
                mmg(lst, [B_vflat, B_ATm1, B_Sb1, B_q[h]], [B_ps[obanks[hi]]])
                if vcols == 256:
                    lst = [(ps[dbanks[hi]][:, cs], vtok[:, hi, j, 128:256], ATm[:, hi, 0, :], True, False),
                           (ps[dbanks[hi]][:, cs], Sbf[:, hi, 128:256], qT[h][:, cs], False, True)]
                    mmg(lst, [B_vflat, B_ATm1, B_Sb1, B_q[h]], [B_ps[dbanks[hi]]])
            fns = []
            for hi, h in enumerate(heads):
                if local:
                    fns.append(lambda e, hi=hi: e.matmul(ps[ubank][:, hi * vcols:(hi + 1) * vcols], ktok[:, hi, 0, :], vtok[:, hi, j, 0:vcols],
                                                        start=True, stop=True, skip_group_check=True))
                else:
                    fns.append(lambda e, hi=hi: e.matmul(ps[ubank][:, hi * vcols:(hi + 1) * vcols], ktok[:, hi, 0, :], vtok[:, hi, j, 0:vcols],
                                                        start=False, stop=(j == NCH - 1), skip_group_check=True))
            tk.mm(fns, [B_ktok1, B_vflat], [B_ps[ubank]])

        def stage2_state(j):
            if local:
                tt("dve", state_all, state_all, ureg_all, ALU.add, Bst + [B_ps[ubank]], Bst)
                if j < NCH - 1:
                    tt("dve", Sb, state_all, decb(j), ALU.mult, Bst + Bd, [B_Sb1])
                tt("dve", state_all, state_all, decb(j), ALU.mult, Bst + Bd, Bst)
            else:
                if j < NCH - 1:
                    cpy("act", Sb, ureg_all, [B_ps[ubank]], [B_Sb1])
                else:
                    for hi, h in enumerate(heads):
                        dcol = dec[h][:, T - 1:T]
                        ts("dve", state_f[h], ps[ubank][:, hi * vcols:(hi + 1) * vcols], dcol, None, ALU.mult, None,
                           [B_ps[ubank], B_dec[h]], [B_state[h]])

        stage1(0)
        for j in range(NCH):
            stage2_pe(j)
            if j + 1 < NCH:
                stage1(j + 1)
            stage2_state(j)

    pump()
    keepc = pvec[:, pv(0, "keep"):pv(0, "keep") + 1]
    flagb = pvec[:, pv(0, "flag_b"):pv(0, "flag_b") + 1]
    for step in range(NT + 1):
        t = min(step, NT - 1)
        tsl = slice(t * T, (t + 1) * T)
        tk.dma("sp", "d_x", xT[:, :, :], xT_d.rearrange("(kc p) s -> p kc s", p=128)[:, :, tsl], [], [B_x])
        if step >= 1:
            tk.dma("sp", "d_recv", acc[:, :, :], recv_d[0:D, :].rearrange("(kc p) s -> p kc s", p=128), [B_recv], B_acc)
            for kc in range(8):
                stt(xT[:, kc, :], acc[:, kc, :], flagb, xT[:, kc, :], ALU.mult, ALU.add, [B_acc[kc], B_pv, B_x], [B_x])
        if step == 1:
            ts("dve", st_conv[:, :, :], st_conv[:, :, :], keepc, None, ALU.mult, None, [B_stconv, B_pv], [B_stconv])
            ts("dve", st_lru[:, :], st_lru[:, :], keepc, None, ALU.mult, None, [B_stlru, B_pv], [B_stlru])
            for h in range(4):
                ts("dve", st_C[:, h, :], st_C[:, h, :], keepc, None, ALU.mult, None, [B_stC[0][h], B_pv], [B_stC[0][h]])
                ts("dve", st_H[:, h, :], st_H[:, h, :], keepc, None, ALU.mult, None, [B_stH[0][h], B_pv], [B_stH[0][h]])
                ts("dve", st_G[:, h, :], st_G[:, h, :], keepc, None, ALU.mult, None, [B_stG[0][h], B_pv], [B_stG[0][h]])

        for l in range(n_layers):
            rmsnorm_to_h(pv(l, "norm_g"))

            slotA, B_slotA = next_weight()
            slot, B_slot = next_weight()
            slot2, B_slot2 = next_weight()
            def gen_lru(chunks, tset, bti, slot=slotA, B_slot=B_slotA):
                t_xa, t_r, t_i, t_a, t_m, t_hs, t_sz = tset
                for c in chunks:
                    b1 = ps_alloc(True)
                    proj_fm(slot, B_slot, c * 128, 128, b1)
                    yield
                    xa, Bxa = Ft[t_xa], BF[t_xa]
                    conv_fm(b1, l, 0, c, pv(l, "lru_cw", c * 4), pv(l, "lru_cb", c), xa[:, :], Bxa)
                    ps_free(b1)
                    yield
                    cpy("act", Bt[bti][:, :], xa[:, :], [Bxa], [BB[bti]])
                    yield
                    b2 = ps_alloc(True); b3 = ps_alloc(True)
                    mmg([(ps[b2][:, :], bd[:, l * 20 + c, :], Bt[bti][:, :], True, True)], [B_const, BB[bti]], [B_ps[b2]])
                    yield
                    mmg([(ps[b3][:, :], bd[:, l * 20 + 4 + c, :], Bt[bti][:, :], True, True)], [B_const, BB[bti]], [B_ps[b3]])
                    yield
                    r_, Br = Ft[t_r], BF[t_r]
                    i_, Bi = Ft[t_i], BF[t_i]
                    act(r_[:, :], ps[b2][:, :], AF.Tanh, [B_ps[b2], B_dc], [Br], bias=dcst[:, DC_HBA + c:DC_HBA + c + 1], scale=0.5)
                    ps_free(b2)
                    yield
                    act(i_[:, :], ps[b3][:, :], AF.Tanh, [B_ps[b3], B_dc], [Bi], bias=dcst[:, DC_HBX + c:DC_HBX + c + 1], scale=0.5)
                    ps_free(b3)
                    yield
                    a_, Ba = Ft[t_a], BF[t_a]
                    m_, Bm = Ft[t_m], BF[t_m]
                    act(a_[:, :], r_[:, :], AF.Exp, [Br, B_dc], [Ba], scale=dcst[:, DC_S1 + c:DC_S1 + c + 1], bias=dcst[:, DC_S1 + c:DC_S1 + c + 1])
                    yield
                    act(m_[:, :], r_[:, :], AF.Exp, [Br, B_dc], [Bm], scale=dcst[:, DC_S2 + c:DC_S2 + c + 1], bias=dcst[:, DC_S2 + c:DC_S2 + c + 1])
                    yield
                    act(m_[:, :], m_[:, :], AF.Ln, [Bm], [Bm], bias=1.0, scale=-1.0)
                    yield
                    act(m_[:, :], m_[:, :], AF.Exp, [Bm], [Bm], scale=0.5)
                    yield
                    stt(i_[:, :], i_[:, :], 1.0, xa[:, :], ALU.add, ALU.mult, [Bi, Bxa], [Bi])
                    yield
                    stt(i_[:, :], i_[:, :], 0.5, m_[:, :], ALU.mult, ALU.mult, [Bi, Bm], [Bi])
                    yield
                    hs, Bhs = Ft[t_hs], BF[t_hs]
                    sc = l * 4 + c
                    tk.op("dve", lambda e, sc=sc: e.tensor_tensor_scan(out=hs[:, :], data0=a_[:, :], data1=i_[:, :],
                                                                       initial=st_lru[:, sc:sc + 1], op0=ALU.mult, op1=ALU.add),
                          [Ba, Bi, B_stlru], [Bhs])
                    cpy("act", st_lru[:, sc:sc + 1], hs[:, T - 1:T], [Bhs], [B_stlru])
                    yield
                    b4 = ps_alloc(True)
                    proj_fm(slot, B_slot, 512 + c * 128, 128, b4)
                    yield
                    sz, Bsz = Ft[t_sz], BF[t_sz]
                    act(sz[:, :], ps[b4][:, :], AF.Tanh, [B_ps[b4]], [Bsz], scale=0.5)
                    yield
                    stt(sz[:, :], sz[:, :], 1.0, ps[b4][:, :], ALU.add, ALU.mult, [Bsz, B_ps[b4]], [Bsz])
                    ps_free(b4)
                    yield
                    stt(yT[:, 0 + c, :], sz[:, :], 0.5, hs[:, :], ALU.mult, ALU.mult, [Bhs, Bsz], [B_y[0 + c]])
                    yield

            def gen_mprep():
                for h in range(4):
                    b1 = ps_alloc(True)
                    proj_fm(slot, B_slot, h * 128, 128, b1)
                    yield
                    cpy("dve", mx_b[:, h, :], ps[b1][:, :], [B_ps[b1]], [B_mxb[h]])
                    yield
                    xc, Bxc = Ft[7], BF[7]
                    conv_fm(b1, l, 1, h, pv(l, "m_cw", h * 4), pv(l, "m_cb", h), xc[:, :], Bxc, xp=xpad2, Bxp=B_xpad2)
                    ps_free(b1)
                    yield
                    act(xm_f[:, h, :], xc[:, :], AF.Tanh, [Bxc], [B_xm[h]], scale=0.5)
                    yield
                    stt(xm_f[:, h, :], xm_f[:, h, :], 1.0, xc[:, :], ALU.add, ALU.mult, [B_xm[h], Bxc], [B_xm[h]])
                    yield
                    act(xm_b[:, h, :], xm_f[:, h, :], AF.Identity, [B_xm[h]], [B_xmb[h]], scale=0.5)
                    yield
                    for qi, (srcb, Bsrc) in enumerate([(xm_b, B_xmb), (xm_b, B_xmb), (mx_b, B_mxb)]):
                        b2 = ps_alloc(True)
                        mmg([(ps[b2][:, :], bd[:, l * 20 + 8 + qi * 4 + h, :], srcb[:, h, :], True, True)], [B_const, Bsrc[h]], [B_ps[b2]])
                        cpy("act" if qi != 1 else "dve", qkv_b[:, qi * 4 + h, :], ps[b2][:, :], [B_ps[b2]], [B_qkv[qi * 4 + h]])
                        ps_free(b2)
                        yield

            run_interleaved([gen_lru([0, 1], (0, 1, 2, 3, 4, 5, 6), 0), gen_lru([2, 3], (8, 9, 10, 11, 12, 13, 15), 1), gen_mprep()])
            weights_done(1)

            li, lf, G, eG, wk = g4[0], g4[1], g4[2], g4[3], g4[0]
            GO = acc[:, 4:8, :]; B_go = B_acc[4:8]

            def gen_gates():
                bi_ = ps_alloc(True); bf_ = ps_alloc(True)
                for gi, bank in ((0, bi_), (1, bf_)):
                    lst = [(ps[bank][0:4, :], wif[:, (l * 2 + gi) * 12 + ci, :], qkv_b[:, ci, :], ci == 0, ci == 11) for ci in range(12)]
                    mmg(lst, [B_const] + B_qkv, [B_ps[bank]])
                yield
                act(li[:, :], ps[bi_][0:4, :], AF.Identity, [B_ps[bi_], B_const], [B_g4[0]], bias=gb[:, l * 2:l * 2 + 1])
                ps_free(bi_)
                yield
                act(lf[:, :], ps[bf_][0:4, :], AF.Exp, [B_ps[bf_], B_dc], [B_g4[1]], bias=gbn[:, 0:1], scale=-1.0)
                ps_free(bf_)
                yield
                act(lf[:, :], lf[:, :], AF.Ln, [B_g4[1]], [B_g4[1]], bias=1.0)
                yield
                tk.op("dve", lambda e: e.tensor_tensor_scan(out=G[:, :], data0=ones4, data1=lf[:, :], initial=0.0,
                                                            op0=ALU.mult, op1=ALU.add), [B_g4[1], B_const], [B_g4[2]])
                yield
                act(eG[:, :], G[:, :], AF.Exp, [B_g4[2]], [B_g4[3]], scale=-1.0)
                yield
                tt("dve", wk[:, :], li[:, :], G[:, :], ALU.add, [B_g4[0], B_g4[2]], [B_g4[0]])
                yield
                act(wk[:, :], wk[:, :], AF.Exp, [B_g4[0]], [B_g4[0]])
                yield

            def gen_gogz():
                for h in range(4):
                    b9 = ps_alloc(True)
                    proj_fm(slot, B_slot, 512 + h * 128, 128, b9)
                    yield
                    act(GO[:, h, :], ps[b9][:, :], AF.Tanh, [B_ps[b9]], [B_go[h]], scale=0.5)
                    ps_free(b9)
                    yield
                    b12 = ps_alloc(True)
                    proj_fm(slot2, B_slot2, h * 128, 128, b12)
                    yield
                    act(GZ[:, h, :], ps[b12][:, :], AF.Tanh, [B_ps[b12]], [B_gz[h]], scale=0.5)
                    yield
                    stt(GZ[:, h, :], GZ[:, h, :], 1.0, ps[b12][:, :], ALU.add, ALU.mult, [B_gz[h], B_ps[b12]], [B_gz[h]])
                    ps_free(b12)
                    yield
            run_interleaved([gen_gates(), gen_gogz()])
            weights_done(2)
            for pair in range(2):
                hp = [pair * 2, pair * 2 + 1]
                qT_ = {}; kT_ = {}; kP_ = {}; Bq_ = {}; Bk_ = {}; dec_ = {}; Bdec_ = {}
                tk.op("pool", lambda e: e.memset(vtok2[0:64, :, :, 128:256], 1.0), [], [B_vflat])
                for hi, h in enumerate(hp):
                    for half in range(2):
                        b3 = ps_alloc()
                        fns = []
                        for jj in range(4):
                            j = half * 4 + jj
                            fns.append(lambda e, j=j, jj=jj, b3=b3, h=h: e.matmul(ps[b3][:, jj * 128:(jj + 1) * 128], mx_bP[:, h, j * L:j * L + 128],
                                                                                 bd[:, l * 20 + 16 + h, :], start=True, stop=True))
                        tk.mm(fns, [B_mxb[h], B_const], [B_ps[b3]])
                        cpy("act", vtok2[0:64, hi, half * 4:half * 4 + 4, 0:128], ps[b3][0:64, :].rearrange("p (j e) -> p j e", j=4),
                            [B_ps[b3]], [B_vflat])
                    b5 = ps_alloc(); b6 = ps_alloc()
                    mmg([(ps[b5][:, :], sel[:, h, :], eG[:, :], True, True)], [B_const, B_g4[3]], [B_ps[b5]])
                    mmg([(ps[b6][:, :], sel[:, h, :], wk[:, :], True, True)], [B_const, B_g4[0]], [B_ps[b6]])
                    eGb, BeGb = Ft[6 + hi], BF[6 + hi]
                    wkb, Bwkb = Ft[8 + hi], BF[8 + hi]
                    cpy("act", eGb[:, :], ps[b5][:, :], [B_ps[b5]], [BeGb])
                    cpy("act", wkb[:, :], ps[b6][:, :], [B_ps[b6]], [Bwkb])
                    b7 = ps_alloc(); b8 = ps_alloc()
                    mmg([(ps[b7][:, :], bd[:, l * 20 + 8 + h, :], xm_b[:, h, :], True, True)], [B_const, B_xmb[h]], [B_ps[b7]])
                    mmg([(ps[b8][:, :], bd[:, l * 20 + 12 + h, :], xm_b[:, h, :], True, True)], [B_const, B_xmb[h]], [B_ps[b8]])
                    stt(Bt[1 + hi][:, :], ps[b7][:, :], 128.0 ** -0.5, eGb[:, :], ALU.mult, ALU.mult, [B_ps[b7], BeGb], [BB[1 + hi]])
                    tt("dve", Bt[3 + hi][:, :], ps[b8][:, :], wkb[:, :], ALU.mult, [B_ps[b8], Bwkb], [BB[3 + hi]])
                    qT_[h] = Bt[1 + hi]; kT_[h] = Bt[3 + hi]; kP_[h] = BtP[3 + hi]; Bq_[h] = BB[1 + hi]; Bk_[h] = BB[3 + hi]
                    dec_[h] = eGb; Bdec_[h] = BeGb
                held = ps_hold(5)
                ob = held[0:2]; db = held[2:4]; ub = held[4]
                sfl = {h: st_C[:, l * 4 + h, :] for h in hp}
                chunk_attn(hp, vtok2, qT_, kT_, kP_, Bq_, Bk_, 256, st_C[:, l * 4 + pair * 2:l * 4 + pair * 2 + 2, :], sfl, {h: B_stC[l][h] for h in hp}, False, None, dec_, Bdec_, ob, db, ub)
                for hi, h in enumerate(hp):
                    dn, Bdn = Ft[hi], BF[hi]
                    ts("dve", dn[:, :], ps[db[hi]][:, :], -1.0, 1.0, ALU.mult, ALU.max, [B_ps[db[hi]]], [Bdn])
                    tt("dve", dn[:, :], dn[:, :], ps[db[hi]][:, :], ALU.max, [Bdn, B_ps[db[hi]]], [Bdn])
                for hi, h in enumerate(hp):
                    act(Ft[hi][:, :], Ft[hi][:, :], AF.Ln, [BF[hi]], [BF[hi]])
                for hi, h in enumerate(hp):
                    act(Ft[hi][:, :], Ft[hi][:, :], AF.Exp, [BF[hi]], [BF[hi]], scale=-1.0)
                for hi, h in enumerate(hp):
                    stt(Ft[10 + hi][:, :], ps[ob[hi]][:, :], 0.5, Ft[hi][:, :], ALU.mult, ALU.mult, [B_ps[ob[hi]], BF[hi]], [BF[10 + hi]])
                ps_release(held)

                def gen_post(hi, h):
                    hm, Bhm = Ft[10 + hi], BF[10 + hi]
                    base = 2 + 4 * hi
                    stt(hm[:, :], GO[:, h, :], 1.0, hm[:, :], ALU.add, ALU.mult, [Bhm, B_go[h]], [Bhm])
                    yield
                    b10 = ps_alloc(True)
                    stats_bcast([hm[:, :]], [Bhm], b10)
                    yield
                    xcn, Bxcn = Ft[base], BF[base]
                    stt(xcn[:, :], ps[b10][:, :], -1.0 / 128.0, hm[:, :], ALU.mult, ALU.add, [B_ps[b10], Bhm], [Bxcn])
                    ps_free(b10)
                    yield
                    sq, Bsq = Ft[base + 1], BF[base + 1]
                    act(sq.bitcast(BF16)[:, 0:T], xcn[:, :], AF.Square, [Bxcn], [Bsq])
                    yield
                    b11 = ps_alloc(True)
                    stats_bcast([sq.bitcast(BF16)[:, 0:T]], [Bsq], b11, bf=True)
                    yield
                    rs, Brs = Ft[base + 2], BF[base + 2]
                    act(rs[:, :], ps[b11][:, :], AF.Ln, [B_ps[b11]], [Brs], bias=EPS, scale=1.0 / 128)
                    ps_free(b11)
                    yield
                    act(rs[:, :], rs[:, :], AF.Exp, [Brs], [Brs], scale=-0.5)
                    yield
                    tt("dve", xcn[:, :], xcn[:, :], rs[:, :], ALU.mult, [Bxcn, Brs], [Bxcn])
                    yield
                    sk, Bsk = Ft[base + 3], BF[base + 3]
                    act(sk[:, :], xm_f[:, h, :], AF.Identity, [B_xm[h], B_dc], [Bsk], scale=dcst[:, DC_HSK + h:DC_HSK + h + 1])
                    yield
                    stt(xcn[:, :], xcn[:, :], pvec[:, pv(l, "m_nw", h):pv(l, "m_nw", h) + 1], sk[:, :], ALU.mult, ALU.add,
                        [Bxcn, B_pv, Bsk], [Bxcn])
                    yield
                    stt(yT[:, 4 + h, :], GZ[:, h, :], 0.5, xcn[:, :], ALU.mult, ALU.mult, [Bxcn, B_gz[h]], [B_y[4 + h]])
                    yield
                run_interleaved([gen_post(hi, h) for hi, h in enumerate(hp)])

            slot, B_slot = next_weight()
            slot2, B_slot2 = next_weight()
            qT_ = {}; kT_ = {}; kP_ = {}; Bq_ = {}; Bk_ = {}; dec_ = {}; Bdec_ = {}
            def gen_hprep(heads_, tset):
                F_f, F_lg, F_G, F_en, F_qs, F_kk = tset
                for h in heads_:
                    bfk = ps_alloc(True)
                    proj_fm(slot, B_slot, 512 + h * 128, 128, bfk)
                    yield
                    f_, Bf_ = Ft[F_f], BF[F_f]
                    act(f_[:, :], ps[bfk][:, :], AF.Tanh, [B_ps[bfk]], [Bf_], scale=0.5)
                    ps_free(bfk)
                    yield
                    ts("dve", f_[:, :], f_[:, :], dcst[:, DC_C1 + h:DC_C1 + h + 1], dcst[:, DC_C0 + h:DC_C0 + h + 1],
                       ALU.mult, ALU.add, [Bf_, B_dc], [Bf_])
                    yield
                    lg, Blg = Ft[F_lg], BF[F_lg]
                    act(lg[:, :], f_[:, :], AF.Ln, [Bf_], [Blg])
                    yield
                    Gc, BGc = Ft[F_G], BF[F_G]
                    tk.op("dve", lambda e: e.tensor_tensor_scan(out=Gc[:, :], data0=rmask, data1=lg[:, :], initial=0.0, op0=ALU.mult, op1=ALU.add),
                          [Blg, B_const], [BGc])
                    yield
                    eGc, BeGc = Ft[3 + h], BF[3 + h]
                    act(eGc[:, :], Gc[:, :], AF.Exp, [BGc], [BeGc])
                    yield
                    enG, BenG = Ft[F_en], BF[F_en]
                    act(enG[:, :], Gc[:, :], AF.Exp, [BGc], [BenG], scale=-1.0)
                    yield
                    bq = ps_alloc(True)
                    proj_fm(slot, B_slot, h * 128, 128, bq)
                    yield
                    qs, Bqs = Ft[F_qs], BF[F_qs]
                    act(qs[:, :], ps[bq][:, :], AF.Tanh, [B_ps[bq]], [Bqs], scale=0.5)
                    yield
                    stt(qs[:, :], qs[:, :], 1.0, ps[bq][:, :], ALU.add, ALU.mult, [Bqs, B_ps[bq]], [Bqs])
                    ps_free(bq)
                    yield
                    stt(Bt[h][:, :], qs[:, :], 0.5 * 128.0 ** -0.5, eGc[:, :], ALU.mult, ALU.mult, [Bqs, BeGc], [BB[h]])
                    yield
                    kk, Bkk = Ft[F_kk], BF[F_kk]
                    ts("dve", kk[:, :], f_[:, :], -1.0, 1.0, ALU.mult, ALU.add, [Bf_], [Bkk])
                    yield
                    tt("dve", Bt[4 + h][:, :], kk[:, :], enG[:, :], ALU.mult, [Bkk, BenG], [BB[4 + h]])
                    yield
                    qT_[h] = Bt[h]; kT_[h] = Bt[4 + h]; kP_[h] = BtP[4 + h]; Bq_[h] = BB[h]; Bk_[h] = BB[4 + h]
                    dec_[h] = eGc; Bdec_[h] = BeGc

            def gen_hvz():
                for h in range(4):
                    b12 = ps_alloc(True)
                    proj_fm(slot2, B_slot2, 512 + h * 128, 128, b12)
                    yield
                    act(GZ[:, h, :], ps[b12][:, :], AF.Tanh, [B_ps[b12]], [B_gz[h]], scale=0.5)
                    yield
                    stt(GZ[:, h, :], GZ[:, h, :], 1.0, ps[b12][:, :], ALU.add, ALU.mult, [B_gz[h], B_ps[b12]], [B_gz[h]])
                    ps_free(b12)
                    yield
                for j in range(NCH):
                    b3 = ps_alloc(True)
                    lst = [(ps[b3][:, :], hT[:, kc, j * L:j * L + 128], slot2[:, kc, 0:512], kc == 0, kc == 7) for kc in range(8)]
                    mmg(lst, [B_h, B_slot2], [B_ps[b3]])
                    yield
                    cpy("act", vtok4[0:64, :, j, :], ps[b3][0:64, :].rearrange("p (h e) -> p h e", h=4), [B_ps[b3]], [B_vflat])
                    ps_free(b3)
                    yield

            run_interleaved([gen_hprep([0, 2], (0, 1, 2, 7, 8, 9)), gen_hprep([1, 3], (10, 11, 12, 13, 14, 15)), gen_hvz()])
            weights_done(2)
            held = ps_hold(5)
            ob = held[0:4]; ub = held[4]
            sfl = {h: st_H[:, l * 4 + h, :] for h in range(4)}
            chunk_attn([0, 1, 2, 3], vtok4, qT_, kT_, kP_, Bq_, Bk_, 128, st_H[:, l * 4:l * 4 + 4, :], sfl, {h: B_stH[l][h] for h in range(4)}, True,
                       lambda j: EG4[:, :, j * L + L - 1:j * L + L].to_broadcast([128, 4, 128]), dec_, Bdec_, ob, None, ub)
            for h in range(4):
                cpy("act", Ft[10 + h][:, :], ps[ob[h]][:, :], [B_ps[ob[h]]], [BF[10 + h]])
            ps_release(held)
            pb_ = []
            for h in range(4):
                act(bfv(h), Ft[10 + h][:, :], AF.Square, [BF[10 + h]], [BF[h]])
                b10 = ps_alloc(); pb_.append(b10)
                stats_bcast([bfv(h)], [BF[h]], b10, bf=True)
            for h in range(4):
                act(Ft[h][:, :], ps[pb_[h]][:, :], AF.Ln, [B_ps[pb_[h]]], [BF[h]], bias=EPS, scale=1.0 / 128)
            for h in range(4):
                act(Ft[h][:, :], Ft[h][:, :], AF.Exp, [BF[h]], [BF[h]], scale=-0.5)
            for h in range(4):
                o_, Bo_ = Ft[10 + h], BF[10 + h]
                stt(o_[:, :], o_[:, :], pvec[:, pv(l, "h_nw"):pv(l, "h_nw") + 1], Ft[h][:, :], ALU.mult, ALU.mult, [Bo_, B_pv, BF[h]], [Bo_])
                stt(yT[:, 8 + h, :], GZ[:, h, :], 0.5, o_[:, :], ALU.mult, ALU.mult, [Bo_, B_gz[h]], [B_y[8 + h]])

            slot, B_slot = next_weight()
            slot2, B_slot2 = next_weight()
            bl = ps_alloc()
            proj_fm(slot2, B_slot2, 0, 16, bl)
            cpy("act", glr[:, :], ps[bl][0:16, :], [B_ps[bl]], [B_glr])
            qT_ = {}; kT_ = {}; kP_ = {}; Bq_ = {}; Bk_ = {}; dec_ = {}; Bdec_ = {}
            def gen_gprep(cc, tset):
                F_lg, F_G, F_en, F_qe = tset
                bg = ps_alloc(True)
                mmg([(ps[bg][:, :], lr2[:, l, cc * 128:(cc + 1) * 128], glr[:, :], True, True)], [B_const, B_glr], [B_ps[bg]])
                yield
                lg, Blg = Ft[F_lg], BF[F_lg]
                act(lg[:, :], ps[bg][:, :], AF.Exp, [B_ps[bg], B_dc], [Blg], bias=dcst[:, DC_NB2 + cc:DC_NB2 + cc + 1], scale=-1.0)
                ps_free(bg)
                yield
                act(lg[:, :], lg[:, :], AF.Ln, [Blg], [Blg], bias=1.0)
                yield
                Gc, BGc = Ft[F_G], BF[F_G]
                tk.op("dve", lambda e: e.tensor_tensor_scan(out=Gc[:, :], data0=ones_T, data1=lg[:, :], initial=0.0, op0=ALU.mult, op1=ALU.add),
                      [Blg, B_const], [BGc])
                yield
                eGc, BeGc = Ft[2 + cc], BF[2 + cc]
                act(eGc[:, :], Gc[:, :], AF.Exp, [BGc], [BeGc], scale=-1.0 / 16.0)
                yield
                enG, BenG = Ft[F_en], BF[F_en]
                act(enG[:, :], Gc[:, :], AF.Exp, [BGc], [BenG], scale=1.0 / 16.0)
                yield
                bq = ps_alloc(True)
                proj_fm(slot, B_slot, cc * 128, 128, bq)
                yield
                qe, Bqe = Ft[F_qe], BF[F_qe]
                stt(qe[:, :], ps[bq][:, :], 64.0 ** -0.5, eGc[:, :], ALU.mult, ALU.mult, [B_ps[bq], BeGc], [Bqe])
                ps_free(bq)
                yield
                for hh in range(2):
                    h = cc * 2 + hh
                    ts("dve", Bt[h][:, :], qe[:, :], rowmask[:, hh:hh + 1], None, ALU.mult, None, [Bqe, B_const], [BB[h]])
                    yield
                    qT_[h] = Bt[h]; Bq_[h] = BB[h]
                    kT_[h] = Bt[4 + cc]; kP_[h] = BtP[4 + cc]; Bk_[h] = BB[4 + cc]
                    dec_[h] = eGc; Bdec_[h] = BeGc
                bk = ps_alloc(True)
                proj_fm(slot, B_slot, 256 + cc * 128, 128, bk)
                yield
                tt("dve", Bt[4 + cc][:, :], ps[bk][:, :], enG[:, :], ALU.mult, [B_ps[bk], BenG], [BB[4 + cc]])
                ps_free(bk)
                yield

            def gen_gvz():
                for h in range(4):
                    b12 = ps_alloc(True)
                    proj_fm(slot2, B_slot2, 16 + h * 128, 128, b12)
                    yield
                    act(GZ[:, h, :], ps[b12][:, :], AF.Tanh, [B_ps[b12]], [B_gz[h]], scale=0.5)
                    yield
                    stt(GZ[:, h, :], GZ[:, h, :], 1.0, ps[b12][:, :], ALU.add, ALU.mult, [B_gz[h], B_ps[b12]], [B_gz[h]])
                    ps_free(b12)
                    yield
                for j in range(NCH):
                    b3 = ps_alloc(True)
                    lst = [(ps[b3][:, :], hT[:, kc, j * L:j * L + 128], slot[:, kc, 512:1024], kc == 0, kc == 7) for kc in range(8)]
                    mmg(lst, [B_h, B_slot], [B_ps[b3]])
                    yield
                    cpy("act", vtok4[0:64, :, j, :], ps[b3][0:64, :].rearrange("p (h e) -> p h e", h=4), [B_ps[b3]], [B_vflat])
                    ps_free(b3)
                    yield

            run_interleaved([gen_gprep(0, (0, 1, 4, 5)), gen_gprep(1, (6, 7, 8, 9)), gen_gvz()])
            issue_brh(l, 0, 0)
            issue_brh(l, 0, 1)
            weights_done(2)
            held = ps_hold(5)
            ob = held[0:4]; ub = held[4]
            sfl = {h: st_G[:, l * 4 + h, :] for h in range(4)}
            chunk_attn([0, 1, 2, 3], vtok4, qT_, kT_, kP_, Bq_, Bk_, 128, st_G[:, l * 4:l * 4 + 4, :], sfl, {h: B_stG[l][h] for h in range(4)}, False, None, dec_, Bdec_, ob, None, ub)
            for h in range(4):
                cpy("act", Ft[10 + h][:, :], ps[ob[h]][:, :], [B_ps[ob[h]]], [BF[10 + h]])
            ps_release(held)
            pb_ = []
            for h in range(4):
                act(bfv(h), Ft[10 + h][:, :], AF.Square, [BF[10 + h]], [BF[h]])
                b10 = ps_alloc(); pb_.append(b10)
                stats_bcast([bfv(h)], [BF[h]], b10, bf=True)
            for h in range(4):
                act(Ft[h][:, :], ps[pb_[h]][:, :], AF.Ln, [B_ps[pb_[h]]], [BF[h]], bias=EPS, scale=1.0 / 128)
            for h in range(4):
                act(Ft[h][:, :], Ft[h][:, :], AF.Exp, [BF[h]], [BF[h]], scale=-0.5)
            for h in range(4):
                o_, Bo_ = Ft[10 + h], BF[10 + h]
                stt(o_[:, :], o_[:, :], pvec[:, pv(l, "g_nw"):pv(l, "g_nw") + 1], Ft[h][:, :], ALU.mult, ALU.mult, [Bo_, B_pv, BF[h]], [Bo_])
                stt(yT[:, 12 + h, :], GZ[:, h, :], 0.5, o_[:, :], ALU.mult, ALU.mult, [Bo_, B_gz[h]], [B_y[12 + h]])

            for n in range(4):
                slot, B_slot = next_weight()
                for dc in range(8):
                    bgate = ps_alloc(); bpr = ps_alloc()
                    proj_fm(slot, B_slot, dc * 128, 128, bgate)
                    hb = dc // 4
                    lst = [(ps[bpr][:, :], brh[hb][:, wc, (dc % 4) * 128:(dc % 4 + 1) * 128], yT[:, n * 4 + wc, :], wc == 0, wc == 3) for wc in range(4)]
                    mmg(lst, B_brh[hb] + B_y[n * 4:n * 4 + 4], [B_ps[bpr]])
                    if n < 3 and dc % 4 == 3:
                        issue_brh(l, n + 1, hb)
                    sg, Bsg = Ft[dc % 2], BF[dc % 2]
                    act(sg[:, :], ps[bgate][:, :], AF.Tanh, [B_ps[bgate]], [Bsg], scale=0.5)
                    if n == 0:
                        stt(acc[:, dc, :], sg[:, :], 1.0, ps[bpr][:, :], ALU.add, ALU.mult, [B_ps[bpr], Bsg], [B_acc[dc]])
                    else:
                        pr, Bpr = Ft[2 + dc % 2], BF[2 + dc % 2]
                        stt(pr[:, :], sg[:, :], 1.0, ps[bpr][:, :], ALU.add, ALU.mult, [B_ps[bpr], Bsg], [Bpr])
                        if n < 3:
                            tt("pool", acc[:, dc, :], acc[:, dc, :], pr[:, :], ALU.add, [B_acc[dc], Bpr], [B_acc[dc]])
                        else:
                            tt("pool", mg[:, dc, 0:T], acc[:, dc, :], pr[:, :], ALU.add, [B_acc[dc], Bpr], [B_mgc[dc]])
                weights_done(1, defer=(n == 3 and step < NT))
            slot, B_slot = next_weight()
            for ec in range(8):
                bo = ps_alloc()
                lst = [(ps[bo][:, :], slot[:, dc, ec * 128:(ec + 1) * 128], mg[:, dc, 0:T], dc == 0, dc == 7) for dc in range(8)]
                mmg(lst, [B_slot] + B_mgc, [B_ps[bo]])
                stt(xT[:, ec, :], ps[bo][:, :], 0.5, xT[:, ec, :], ALU.mult, ALU.add, [B_x, B_ps[bo]], [B_x])
            weights_done(1, defer=(step < NT))

        if step < NT:
            tk.dma("sp", "d_send", send_d.rearrange("(kc p) s -> p kc s", p=128), xT[:, :, :], [B_x], [B_send])
            tk.coll("pool", "cc", lambda e: e.collective_compute("AllGather", ALU.bypass, replica_groups=[[2 * i, 2 * i + 1] for i in range(n_pairs)],
                                                                ins=[send_d], outs=[recv_d]), [B_send], [B_recv])
            pump()
        if step >= 1:
            b = ps_alloc()
            for kc in range(8):
                act(bfv(kc % 2), xT[:, kc, :], AF.Square, [B_x], [BF[kc % 2]])
                tk.mm([lambda e, kc=kc, b=b: e.matmul(ps[b][:, :], ones_b[:, :], bfv(kc % 2), start=(kc == 0), stop=(kc == 7))],
                      [B_const, BF[kc % 2]], [B_ps[b]], cost=0.3)
            rstd_from(b, D, Ft[2][:, :], BF[2])
            for kc in range(8):
                stt(osb[:, kc, :], xT[:, kc, :], pvec[:, pv(0, "final_g") + kc:pv(0, "final_g") + kc + 1], Ft[2][:, :], ALU.mult, ALU.mult,
                    [B_x, B_pv, BF[2]], [B_acc[kc]])
            osl = slice((step - 1) * T, step * T)
            tk.dma("sp", "d_out", outT_d.rearrange("(kc p) s -> p kc s", p=128)[:, :, osl], osb[:, :, :], B_acc, [])

    tk.final_wait("sp", "d_out")

    def replay(en):
        def f(e):
            for (name, a, k, h) in streams[en]:
                ins = getattr(e, name)(*a, **k)
                for (pn, pa) in h.post:
                    getattr(ins, pn)(*pa)
        return f
    block.tensor(replay("pe"))
    block.scalar(replay("act"))
    block.vector(replay("dve"))
    block.gpsimd(replay("pool"))
    block.sync(replay("sp"))
    es.close()
    return nc, tk


def _host_pack(inputs, l, is_b):
    f = lambda k: np.asarray(inputs[k], dtype=np.float32)
    pvec = np.zeros((128, PV_COLS), np.float32)
    def put(name, vec, nchunks, stride=1, off=0):
        for c in range(nchunks):
            pvec[:, pv(0, name) + c * stride + off] = vec[c * 128:(c + 1) * 128]
    for j in range(4):
        put("lru_cw", f("lru_conv_w")[l, j], 4, stride=4, off=j)
        put("m_cw", f("m_conv_w")[l, j], 4, stride=4, off=j)
    put("lru_cb", f("lru_conv_b")[l], 4)
    put("lru_ba", f("lru_ba")[l], 4)
    put("lru_bx", f("lru_bx")[l], 4)
    put("lru_lam", f("lru_lambda")[l], 4)
    put("m_cb", f("m_conv_b")[l], 4)
    put("m_nw", f("m_norm_w")[l], 4)
    put("m_skip", f("m_skip")[l], 4)
    put("h_lbl", f("h_lb_logits")[0], 4)
    put("h_lbl1", f("h_lb_logits")[1], 4)
    put("h_nw", f("h_norm_w")[l], 1)
    put("g_b2", f("g_b_lr2")[l], 2)
    put("g_nw", f("g_norm_w")[l], 1)
    put("norm_g", f("norm_g")[l], 8)
    put("final_g", f("final_g"), 8)
    pvec[:, pv(0, "flag_b")] = 1.0 if is_b else 0.0
    pvec[:, pv(0, "keep")] = 0.0 if is_b else 1.0
    bd = np.zeros((128, 20, 128), np.float32)
    for gi, key in enumerate(["lru_wa", "lru_wx"]):
        w = f(key)[l]
        for c in range(4):
            for b in range(2):
                bd[b * 64:(b + 1) * 64, gi * 4 + c, b * 64:(b + 1) * 64] = w[2 * c + b]
    for gi, key in enumerate(["m_wq", "m_wk", "m_wv"]):
        w = f(key)[l]
        for h in range(4):
            for b in range(32):
                bd[b * 4:(b + 1) * 4, 8 + gi * 4 + h, b * 4:(b + 1) * 4] = w[32 * h + b]
    wif = np.zeros((128, 2 * 12, 4), np.float32)
    gb = np.zeros((4, 2), np.float32)
    for gi, key in enumerate(["m_wi", "m_wf"]):
        w = f(key)[l]
        for ci in range(12):
            wif[:, gi * 12 + ci, :] = w[ci * 128:(ci + 1) * 128, :]
    gb[:, 0] = f("m_bi")[l]
    gb[:, 1] = f("m_bf")[l]
    lr2 = np.ascontiguousarray(f("g_w_lr2")[l][:, None, :])
    cst = np.zeros((128, 1664), np.float32)
    cst[:, 0:128] = np.eye(128, dtype=np.float32)
    cst[:, 128:256] = 1.0
    cst[0:64, 256:320] = np.triu(np.ones((64, 64), np.float32))
    rm = np.ones((T,), np.float32); rm[::L] = 0.0
    cst[:, 320:832] = rm[None, :]
    cst[0:64, 832] = 1.0
    cst[64:128, 833] = 1.0
    for j in range(4):
        cst[j, 896 + j * 128:896 + (j + 1) * 128] = 1.0
        cst[0:64, 1408 + j * 64:1408 + (j + 1) * 64] = np.triu(np.ones((64, 64), np.float32))
    return dict(w_in=np.ascontiguousarray(f("w_in")[l:l + 1]), w_branch=np.ascontiguousarray(f("w_branch")[l:l + 1]),
                w_out=np.ascontiguousarray(f("w_out")[l:l + 1]), pvec=pvec, bd=bd, wif=wif, gbias=gb, lr2=lr2, cst=cst)


_PROG_CACHE = {}


def kernel(**inputs):
    x = np.asarray(inputs["x"], dtype=np.float32)
    B, S, _ = x.shape
    packs = [_host_pack(inputs, 0, False), _host_pack(inputs, 1, True)]
    if (S, B) not in _PROG_CACHE:
        _PROG_CACHE[(S, B)] = build_program(S, n_pairs=B)[0]
    nc = _PROG_CACHE[(S, B)]
    zeros = np.zeros((D, S), np.float32)
    in_maps = []
    for b in range(B):
        ma = dict(packs[0]); ma["xT"] = np.ascontiguousarray(x[b].T)
        mb = dict(packs[1]); mb["xT"] = zeros
        in_maps += [ma, mb]
    res = run_bass_kernel_spmd(nc, in_maps, core_ids=list(range(2 * B)))
    out = np.stack([np.ascontiguousarray(res.results[2 * b + 1]["outT"].T) for b in range(B)], axis=0)
    return out.astype(np.float32)
```
